# Optimizing a Trainium2 kernel written in Bass

```python
import jax, jax.numpy as jnp
from jax import lax
import numpy as np

D_MODEL = 1024
BATCH = 16
SEQ = 2048
DEPTH = 1

LRU_WIDTH = D_MODEL
LRU_BLOCKS = 8
LRU_BLOCK_DIM = LRU_WIDTH // LRU_BLOCKS
CONV_WIDTH = 4
LRU_C = 8.0
HG_WIDTH = D_MODEL
HG_EXPAND = 128
HG_HEADS = HG_WIDTH // HG_EXPAND
HG_HEAD_V = HG_WIDTH // HG_HEADS
CHUNK = 64
HG_SCALE = HG_EXPAND ** -0.5
N_BRANCH = 2
EPS = 1e-6
IN_COLS = 2 * LRU_WIDTH + 4 * HG_WIDTH + N_BRANCH * D_MODEL
SPLITS = [LRU_WIDTH, 2 * LRU_WIDTH, 2 * LRU_WIDTH + HG_WIDTH, 2 * LRU_WIDTH + 2 * HG_WIDTH,
          2 * LRU_WIDTH + 3 * HG_WIDTH, 2 * LRU_WIDTH + 4 * HG_WIDTH]

kernel_name = "hybrid_hawk_hgrn2_gated_block"


def rms_norm(x, g):
    xf = x.astype(jnp.float32)
    y = xf * lax.rsqrt(jnp.mean(xf * xf, axis=-1, keepdims=True) + EPS)
    return y.astype(x.dtype) * g


def causal_depthwise_conv(x, w, b):
    s = x.shape[1]
    xp = jnp.pad(x, ((0, 0), (CONV_WIDTH - 1, 0), (0, 0)))
    return b + sum(xp[:, k:k + s] * w[k] for k in range(CONV_WIDTH))


def block_diag_linear(x, w, b):
    bsz, s, _ = x.shape
    xb = x.reshape(bsz, s, LRU_BLOCKS, LRU_BLOCK_DIM)
    y = jnp.einsum('bshi,hij->bshj', xb, w) + b
    return y.reshape(bsz, s, LRU_WIDTH)


def _linear_recurrence_combine(left, right):
    a_l, u_l = left
    a_r, u_r = right
    return a_l * a_r, a_r * u_l + u_r


def rg_lru(x, wx, bx, wa, ba, lam):
    xf = x.astype(jnp.float32)
    gate_i = jax.nn.sigmoid(block_diag_linear(xf, wx, bx))
    gate_r = jax.nn.sigmoid(block_diag_linear(xf, wa, ba))
    log_a = -LRU_C * gate_r * jax.nn.softplus(-lam.astype(jnp.float32))
    a = jnp.exp(log_a)
    mult = jnp.sqrt(-jnp.expm1(2.0 * log_a))
    u = mult * gate_i * xf
    _, h = lax.associative_scan(_linear_recurrence_combine, (a, u), axis=1)
    return h.astype(x.dtype)


def hgrn2_chunked(q, k, v, log_f):
    bsz, s, h, dk = q.shape
    dv = v.shape[-1]
    n = s // CHUNK

    def to_chunks(t):
        return t.reshape(bsz, n, CHUNK, h, t.shape[-1]).transpose(1, 0, 3, 2, 4)

    q, k, v, log_f = map(to_chunks, (q, k, v, log_f))
    b = jnp.cumsum(log_f, axis=3)
    b_mid = b[:, :, :, CHUNK // 2:CHUNK // 2 + 1]
    b_last = b[:, :, :, -1:]
    q_in = q * jnp.exp(b - b_mid) * HG_SCALE
    k_in = k * jnp.exp(b_mid - b)
    causal = jnp.tril(jnp.ones((CHUNK, CHUNK), dtype=bool))
    att = jnp.where(causal, jnp.einsum('nbhtd,nbhsd->nbhts', q_in, k_in), 0.0)
    o_intra = jnp.einsum('nbhts,nbhsv->nbhtv', att, v)
    q_inter = q * jnp.exp(b) * HG_SCALE
    k_state = k * jnp.exp(b_last - b)
    chunk_decay = jnp.exp(b_last[:, :, :, 0])

    def step(state, inp):
        qc, kc, vc, dc = inp
        o = jnp.einsum('bhtd,bhdv->bhtv', qc, state)
        state = state * dc[..., None] + jnp.einsum('bhsd,bhsv->bhdv', kc, vc)
        return state, o

    s0 = jnp.zeros((bsz, h, dk, dv), dtype=q.dtype)
    _, o_inter = lax.scan(step, s0, (q_inter, k_state, v, chunk_decay))
    o = o_intra + o_inter
    return o.transpose(1, 0, 3, 2, 4).reshape(bsz, s, h, dv)


def setup_inputs(seed: int = 0) -> dict:
    key = jax.random.key(seed)
    ks = jax.random.split(key, 20)
    f32 = jnp.float32
    nrm = lambda k, shape, scale: jax.random.normal(k, shape, f32) * scale
    x = jax.random.normal(ks[0], (BATCH, SEQ, D_MODEL), f32)
    w_in = nrm(ks[1], (DEPTH, D_MODEL, IN_COLS), D_MODEL ** -0.5)
    b_merge = nrm(ks[2], (DEPTH, N_BRANCH * D_MODEL), 0.01)
    conv_w = nrm(ks[3], (DEPTH, CONV_WIDTH, LRU_WIDTH), CONV_WIDTH ** -0.5)
    conv_b = nrm(ks[4], (DEPTH, LRU_WIDTH), 0.01)
    rg_wx = nrm(ks[5], (DEPTH, LRU_BLOCKS, LRU_BLOCK_DIM, LRU_BLOCK_DIM), LRU_BLOCK_DIM ** -0.5)
    rg_bx = nrm(ks[6], (DEPTH, LRU_BLOCKS, LRU_BLOCK_DIM), 0.01)
    rg_wa = nrm(ks[7], (DEPTH, LRU_BLOCKS, LRU_BLOCK_DIM, LRU_BLOCK_DIM), LRU_BLOCK_DIM ** -0.5)
    rg_ba = nrm(ks[8], (DEPTH, LRU_BLOCKS, LRU_BLOCK_DIM), 0.01)
    u = jax.random.uniform(ks[9], (DEPTH, LRU_WIDTH), f32, minval=0.9, maxval=0.999)
    a0 = u ** (1.0 / LRU_C)
    rg_lambda = jnp.log(a0) - jnp.log1p(-a0)
    hg_lb_logits = nrm(ks[10], (DEPTH + 1, HG_WIDTH), 0.1)
    hg_norm_g = 1.0 + nrm(ks[11], (DEPTH, HG_HEAD_V), 0.05)
    proj_a = nrm(ks[12], (DEPTH, LRU_WIDTH, D_MODEL), LRU_WIDTH ** -0.5)
    proj_b = nrm(ks[13], (DEPTH, HG_WIDTH, D_MODEL), HG_WIDTH ** -0.5)
    w_out = nrm(ks[14], (DEPTH, D_MODEL, D_MODEL), D_MODEL ** -0.5)
    norm_g = 1.0 + nrm(ks[15], (DEPTH, D_MODEL), 0.05)
    final_norm_g = 1.0 + nrm(ks[16], (D_MODEL,), 0.05)
    return {"x": x, "w_in": w_in, "b_merge": b_merge, "conv_w": conv_w, "conv_b": conv_b,
            "rg_wx": rg_wx, "rg_bx": rg_bx, "rg_wa": rg_wa, "rg_ba": rg_ba, "rg_lambda": rg_lambda,
            "hg_lb_logits": hg_lb_logits, "hg_norm_g": hg_norm_g, "proj_a": proj_a, "proj_b": proj_b,
            "w_out": w_out, "norm_g": norm_g, "final_norm_g": final_norm_g}


def reference(x, w_in, b_merge, conv_w, conv_b, rg_wx, rg_bx, rg_wa, rg_ba, rg_lambda,
              hg_lb_logits, hg_norm_g, proj_a, proj_b, w_out, norm_g, final_norm_g):
    bsz, s, _ = x.shape
    lb_all = jnp.cumsum(jax.nn.softmax(hg_lb_logits.astype(jnp.float32), axis=0), axis=0)
    for l in range(DEPTH):
        h = rms_norm(x, norm_g[l])
        z = h @ w_in[l]
        xa, ga, q, f_pre, i_in, gb, gm = jnp.split(z, SPLITS, axis=-1)

        xa = causal_depthwise_conv(xa, conv_w[l], conv_b[l])
        ya = rg_lru(xa, rg_wx[l], rg_bx[l], rg_wa[l], rg_ba[l], rg_lambda[l])
        ya = ya * jax.nn.silu(ga)
        out_a = ya @ proj_a[l]

        lb = lb_all[l]
        f = lb + (1.0 - lb) * jax.nn.sigmoid(f_pre.astype(jnp.float32))
        log_f = jnp.log(f)
        k = 1.0 - f
        qh = jax.nn.silu(q.astype(jnp.float32))
        heads = lambda t: t.reshape(bsz, s, HG_HEADS, t.shape[-1] // HG_HEADS)
        o = hgrn2_chunked(heads(qh), heads(k), heads(i_in.astype(jnp.float32)), heads(log_f))
        o = rms_norm(o, hg_norm_g[l]).reshape(bsz, s, HG_WIDTH).astype(x.dtype)
        yb = o * jax.nn.silu(gb)
        out_b = yb @ proj_b[l]

        gates = jax.nn.sigmoid(gm + b_merge[l])
        g_a, g_b = jnp.split(gates, N_BRANCH, axis=-1)
        mixed = g_a * out_a + g_b * out_b
        x = x + mixed @ w_out[l]
    return rms_norm(x, final_norm_g)
```

```python
import numpy as np
import concourse.bass as bass
import concourse.mybir as mybir
from concourse.bass_utils import run_bass_kernel_spmd

F32 = mybir.dt.float32
BF16 = mybir.dt.bfloat16
AF = mybir.ActivationFunctionType
ALU = mybir.AluOpType
AX = mybir.AxisListType

D = 1024
SEQ = 2048
NB = 2
TT = 512
NT = SEQ // TT
NCH = 8
IN_COLS = 8192
EPS = 1e-6
LRU_C = 8.0
HG_SCALE = 128.0 ** -0.5

P_NG = 0
P_CW = 8
P_CB = 40
P_BX = 48
P_BA = 56
P_LAM = 64
P_L0 = 72
P_L1 = 80
P_HG = 88
P_BM = 89
NPAR = 105

WORDER = ["xa", "ga", "gma", "i", "gb", "q", "f", "pa", "gmb", "pb", "wo"]
W_IN_COLS = {"xa": 0, "ga": 1024, "q": 2048, "f": 3072, "i": 4096, "gb": 5120, "gma": 6144, "gmb": 7168}

ENGINES = ("pe", "act", "dve", "pool", "sp")
LAST_PROG = None


class Buf:
    __slots__ = ("name", "last_w", "readers")

    def __init__(self, name):
        self.name = name
        self.last_w = None
        self.readers = []


class Op:
    __slots__ = ("eng", "fn", "deps", "dma_sem", "sig", "need_sig", "name")

    def __init__(self, eng, fn, dma_sem, name):
        self.eng = eng
        self.fn = fn
        self.deps = []
        self.dma_sem = dma_sem
        self.sig = None
        self.need_sig = False
        self.name = name


class Prog:
    def __init__(self):
        self.ops = {e: [] for e in ENGINES}
        self.final_waits = []
        self.lab = ""

    def emit(self, eng, fn, reads=(), writes=(), dma_sem=None, name=""):
        op = Op(eng, fn, dma_sem, name or self.lab)
        deps = {}
        for b in reads:
            w = b.last_w
            if w is not None:
                deps[id(w)] = w
        for b in writes:
            w = b.last_w
            if w is not None:
                deps[id(w)] = w
            for r in b.readers:
                if r is not op:
                    deps[id(r)] = r
        for w in deps.values():
            if w.eng == "pe" and eng == "pe":
                continue
            op.deps.append(w)
            w.need_sig = True
        for b in reads:
            b.readers.append(op)
        for b in writes:
            b.last_w = op
            b.readers = []
        if dma_sem is not None:
            op.need_sig = True
        self.ops[eng].append(op)
        return op

    def finalize(self, nc, block, engine_sems, out_ops):
        counts = {}
        for e in ENGINES:
            for op in self.ops[e]:
                if not op.need_sig:
                    continue
                if op.dma_sem is not None:
                    sem, inc = op.dma_sem, 16
                else:
                    sem, inc = engine_sems[e], 1
                counts[id(sem)] = counts.get(id(sem), 0) + inc
                op.sig = (sem, counts[id(sem)], inc)
        self.sigtable = {e: [(op.sig[1], op.name) for op in self.ops[e] if op.sig is not None and op.dma_sem is None] for e in ENGINES}
        finals = {}
        for op in out_ops:
            sem, val, _ = op.sig
            if id(sem) not in finals or finals[id(sem)][1] < val:
                finals[id(sem)] = (sem, val)

        def run(e, engobj):
            waited = {}
            for op in self.ops[e]:
                need = {}
                for d in op.deps:
                    sem, val, _ = d.sig
                    if need.get(id(sem), (None, 0))[1] < val:
                        need[id(sem)] = (sem, val)
                for k, (sem, val) in need.items():
                    if waited.get(k, 0) >= val:
                        continue
                    engobj.wait_ge(sem, val)
                    waited[k] = val
                ins = op.fn(engobj)
                if op.sig is not None:
                    ins.then_inc(op.sig[0], op.sig[2])
            if e == "sp":
                for sem, val in finals.values():
                    engobj.wait_ge(sem, val)

        @block.tensor
        def _(eng):
            run("pe", eng)

        @block.scalar
        def _(eng):
            run("act", eng)

        @block.vector
        def _(eng):
            run("dve", eng)

        @block.gpsimd
        def _(eng):
            run("pool", eng)

        @block.sync
        def _(eng):
            run("sp", eng)


class Region:
    def __init__(self, nc, es, name, nbytes, gran=1024):
        assert nbytes % 4 == 0
        self.GRAN = gran
        self.t = es.enter_context(nc.sbuf_tensor(name, [128, nbytes // 4], F32))
        self.nbytes = nbytes
        self.bufs_ = [Buf("%s.%d" % (name, i)) for i in range((nbytes + self.GRAN - 1) // self.GRAN)]
        self.f32 = self.t[:, :]
        self.b16 = self.t[:, :].bitcast(BF16)

    def bufs(self, lo, hi):
        return self.bufs_[lo // self.GRAN:(hi + self.GRAN - 1) // self.GRAN]


class Stream:
    def __init__(self, reg, dt, base, stride, nch, width):
        self.reg, self.dt, self.base, self.stride, self.nch, self.width = reg, dt, base, stride, nch, width
        self.es = 4 if dt == F32 else 2
        self.flat = reg.f32 if dt == F32 else reg.b16
        assert (base + nch * stride) * self.es <= reg.nbytes + 0, (base, nch, stride, reg.nbytes)

    def ap(self, c, lo=0, hi=None):
        hi = self.width if hi is None else hi
        o = self.base + c * self.stride
        return self.flat[:, o + lo:o + hi]

    def b(self, c, lo=0, hi=None):
        hi = self.width if hi is None else hi
        o = self.base + c * self.stride
        return self.reg.bufs((o + lo) * self.es, (o + hi) * self.es)

    def ap3(self, lo=0, hi=None):
        hi = self.width if hi is None else hi
        v = self.flat[:, self.base:self.base + self.nch * self.stride].rearrange("p (c t) -> p c t", c=self.nch)
        return v[:, :, lo:hi]

    def ball(self):
        return self.reg.bufs(self.base * self.es, (self.base + self.nch * self.stride) * self.es)


def build_nc(stage=99, debug=None):
    nc = bass.Bass("TRN2", target_bir_lowering=False)
    dram = {}
    dram["x"] = nc.dram_tensor("x", [NB * SEQ, D], F32, kind="ExternalInput").ap()
    dram["wpk"] = nc.dram_tensor("wpk", [2 * len(WORDER), 128, 8 * 512], F32, kind="ExternalInput").ap()
    dram["rg_wx"] = nc.dram_tensor("rg_wx", [128, 8, 128], F32, kind="ExternalInput").ap()
    dram["rg_wa"] = nc.dram_tensor("rg_wa", [128, 8, 128], F32, kind="ExternalInput").ap()
    dram["params"] = nc.dram_tensor("params", [128, NPAR], F32, kind="ExternalInput").ap()
    dram["fng"] = nc.dram_tensor("fng", [128, D], F32, kind="ExternalInput").ap()
    dram["consts"] = nc.dram_tensor("consts", [128, 256], F32, kind="ExternalInput").ap()
    dram["y"] = nc.dram_tensor("y", [NB * SEQ, D], F32, kind="ExternalOutput").ap()
    dram["wbf"] = nc.dram_tensor("wbf", [22, 128, 8 * 512], BF16, kind="Internal").ap()
    dbg = None
    if debug is not None:
        dbg = nc.dram_tensor("dbg", list(debug), F32, kind="ExternalOutput").ap()
    _build(nc, dram, stage, dbg)
    return nc


def _build(nc, dram, stage, dbg):
    from contextlib import ExitStack
    P = Prog()
    with ExitStack() as es:
        def sb(name, shape, dt):
            return es.enter_context(nc.sbuf_tensor(name, shape, dt))

        def sem(name):
            return es.enter_context(nc.semaphore(name))

        engine_sems = {e: sem("s_" + e) for e in ENGINES}

        XT = [sb("XT%d" % i, [128, 4, D], F32) for i in range(2)]
        HT = sb("HT", [128, NCH, TT], BF16)
        SQJ = sb("SQJ", [128, D], BF16)
        PAR = sb("PAR", [128, NPAR], F32)
        DP = sb("DP", [128, 64], F32)
        FNG = sb("FNG", [128, D], F32)
        IDB = sb("IDB", [128, 128], BF16)
        ONES = sb("ONES", [128, 1], F32)
        SMALL = sb("SMALL", [128, 128], F32)
        SM2 = sb("SM2", [128, 64], F32)
        REM = sb("REM", [128, 8], F32)
        HST = sb("HST", [128, 8], F32)
        HALO = sb("HALO", [128, 8, 3], F32)
        TST = sb("TST", [128, 8, 128], F32)
        SBF = sb("SBF", [128, 8, 128], BF16)
        ATT4 = [sb("ATT4_%d" % i, [128, 4, 128], BF16) for i in range(2)]
        MSK = sb("MSK", [128, 128], F32)
        ONB = sb("ONB", [128, D], BF16)
        WX = sb("WXb", [128, 8, 128], BF16)
        WA = sb("WAb", [128, 8, 128], BF16)
        NWS = 4
        WS = [sb("WS%d" % i, [128, 8, 512], BF16) for i in range(NWS)]
        b_XT = [[Buf("XT%d_%d" % (i, j)) for j in range(4)] for i in range(2)]
        b_HT = [Buf("HT%d" % c) for c in range(NCH)]
        b_SQJ, b_PAR, b_DP, b_FNG = Buf("SQJ"), Buf("PAR"), Buf("DP"), Buf("FNG")
        b_IDB, b_MSK, b_ONES, b_SM2 = Buf("IDB"), Buf("MSK"), Buf("ONES"), Buf("SM2")
        b_SMs0, b_SMhg, b_SMfin = Buf("SMs0"), Buf("SMhg"), Buf("SMfin")
        b_SMhg2 = [Buf("SMhg0"), Buf("SMhg1")]
        b_REM, b_HST, b_HALO = Buf("REM"), Buf("HST"), Buf("HALO")
        b_TST = [Buf("TST%d" % h) for h in range(8)]
        b_SBF = [Buf("SBF%d" % h) for h in range(8)]
        b_ATT4 = [Buf("ATT4_%d" % i) for i in range(2)]
        b_ONB = Buf("ONB")
        b_WX, b_WA = Buf("WX"), Buf("WA")
        b_WS = [Buf("WS%d" % i) for i in range(4)]

        XW = 516
        R1 = Region(nc, es, "R1", 8 * XW * 4, gran=XW * 4)
        R2 = Region(nc, es, "R2", 8 * TT * 4)
        R3 = Region(nc, es, "R3", 8 * TT * 4)
        R4 = Region(nc, es, "R4", 8 * TT * 4)
        Q1 = Region(nc, es, "Q1", 8 * TT * 2)
        Q2 = Region(nc, es, "Q2", 8 * TT * 2)
        Q3 = Region(nc, es, "Q3", 8 * TT * 2)
        Q4 = Region(nc, es, "Q4", 8 * TT * 2)
        Q5 = Region(nc, es, "Q5", 8 * TT * 2)
        Q6 = Region(nc, es, "Q6", 8 * TT * 2)
        XA = Stream(R1, F32, 0, XW, 8, 3 + TT)
        XC = Stream(R2, F32, 0, TT, 8, TT)
        AA = Stream(R3, F32, 0, TT, 8, TT)
        MM = Stream(R4, F32, 0, TT, 8, TT)
        XCb = Stream(Q1, BF16, 0, TT, 8, TT)
        GI = Stream(Q2, BF16, 0, TT, 8, TT)
        SG = Stream(Q3, BF16, 0, TT, 8, TT)
        YA = Stream(Q4, BF16, 0, TT, 8, TT)
        GA = Stream(Q6, BF16, 0, TT, 8, TT)
        OA = Stream(Q5, BF16, 0, TT, 8, TT)
        SS = Stream(R2, F32, 0, TT, 8, TT)
        BB = Stream(R3, F32, 0, TT, 8, TT)
        SMb = Stream(Q2, BF16, 0, TT, 8, TT)
        EP = Stream(R4, BF16, 8 * TT, TT, 8, TT)
        EN = Stream(Q3, BF16, 0, TT, 8, TT)
        SQ = Stream(Q1, BF16, 0, TT, 8, TT)
        KTT = Stream(R4, BF16, 0, TT, 8, TT)
        VV = Stream(R1, BF16, 8 * TT, D, 4, D)
        SGB = Stream(R1, BF16, 0, TT, 8, TT)
        YB = Stream(R4, BF16, 8 * TT, TT, 8, TT)
        ON = Stream(Q6, BF16, 0, D, 4, D)
        XN = ON
        GB = Stream(Q6, BF16, 0, TT, 8, TT)
        MIX = Stream(Q3, BF16, 0, TT, 8, TT)

        PO = [es.enter_context(nc.psum_tensor("PO%d" % i, [128, 1024], F32)) for i in range(2)]
        b_PO = [[Buf("PO%d_%d" % (i, h)) for h in range(2)] for i in range(2)]
        NROT = 4
        PS = [es.enter_context(nc.psum_tensor("PS%d" % i, [128, 512], F32)) for i in range(NROT)]
        b_PS = [Buf("PS%d" % i) for i in range(NROT)]
        ps_rr = [0]

        def next_ps():
            i = ps_rr[0]
            ps_rr[0] = (i + 1) % NROT
            return PS[i], b_PS[i]

        s_ld = [sem("ld_x0"), sem("ld_x1")]
        s_c = [sem("ld_c%d" % i) for i in range(5)]
        s_st = [sem("st_y0"), sem("st_y1")]
        s_w = [sem("ld_w%d" % i) for i in range(4)]
        out_ops = []

        P.emit("sp", lambda e: e.dma_start(out=XT[0][:, :, :], in_=dram["x"].rearrange("(n j p) d -> n p j d", j=4, p=128)[0]),
               writes=b_XT[0], dma_sem=s_ld[0])
        P.emit("sp", lambda e: e.dma_start(out=PAR[:, :], in_=dram["params"]), writes=[b_PAR], dma_sem=s_c[0])
        CST = Q6.f32[:, 0:256]
        b_CST = Q6.bufs(0, 1024)
        P.emit("sp", lambda e: e.dma_start(out=CST, in_=dram["consts"]), writes=b_CST, dma_sem=s_c[2])
        P.emit("pool", lambda e: e.dma_start(out=WX[:, :, :], in_=dram["rg_wx"]), writes=[b_WX], dma_sem=s_c[3])
        P.emit("pool", lambda e: e.dma_start(out=WA[:, :, :], in_=dram["rg_wa"]), writes=[b_WA], dma_sem=s_c[4])
        P.emit("dve", lambda e: e.tensor_copy(IDB[:, :], CST[:, 0:128]), reads=b_CST, writes=[b_IDB])
        P.emit("dve", lambda e: e.tensor_copy(MSK[:, :], CST[:, 128:256]), reads=b_CST, writes=[b_MSK])
        P.emit("dve", lambda e: e.memset(ONES[:, :], 1.0), writes=[b_ONES])
        P.emit("act", lambda e: e.activation(DP[:, 32:40], PAR[:, P_LAM:P_LAM + 8], AF.Exp, scale=-1.0), reads=[b_PAR], writes=[b_DP])
        P.emit("act", lambda e: e.activation(DP[:, 40:48], DP[:, 32:40], AF.Ln, bias=1.0), reads=[b_DP], writes=[b_DP])
        P.emit("dve", lambda e: e.tensor_scalar(DP[:, 0:8], DP[:, 40:48], -LRU_C, None, ALU.mult), reads=[b_DP], writes=[b_DP])
        P.emit("dve", lambda e: e.tensor_tensor(DP[:, 48:56], PAR[:, P_L0:P_L0 + 8], PAR[:, P_L1:P_L1 + 8], ALU.subtract), reads=[b_PAR], writes=[b_DP])
        P.emit("act", lambda e: e.activation(DP[:, 8:16], DP[:, 48:56], AF.Sigmoid), reads=[b_DP], writes=[b_DP])
        P.emit("dve", lambda e: e.tensor_scalar(DP[:, 16:24], DP[:, 8:16], -1.0, 1.0, ALU.mult, ALU.add), reads=[b_DP], writes=[b_DP])
        P.emit("act", lambda e: e.activation(DP[:, 24:32], DP[:, 16:24], AF.Ln, scale=HG_SCALE), reads=[b_DP], writes=[b_DP])

        xv = dram["x"].rearrange("(n j p) d -> n p j d", j=4, p=128)
        yv = dram["y"].rearrange("(n j p) d -> n p j d", j=4, p=128)

        NHC = 2 * len(WORDER)
        total_w = NB * NT * NHC
        wstate = {"n": 0, "issued": 0}
        b_WBF = [Buf("WBF%d" % i) for i in range(NHC)]
        s_wst = [sem("st_w%d" % i) for i in range(NWS)]
        PREF = NWS - 1
        NHC_A = 12

        def issue_w(idx):
            if idx >= total_w:
                return
            k = idx % NHC
            name, h = WORDER[k // 2], k % 2
            s = idx % NWS
            tile_i = idx // NHC
            cached = (tile_i >= 2) or (tile_i == 1 and k < NHC_A)
            if not cached:
                gate = b_XT[0] if idx <= PREF else []
                P.emit("pool", lambda e, s=s, k=k: e.dma_start(out=WS[s][:, :, :].rearrange("p k n -> p (k n)"), in_=dram["wpk"][k]),
                       reads=gate, writes=[b_WS[s]], dma_sem=s_w[s], name="ldw_" + name)
                if (tile_i == 0 and k < NHC_A) or (tile_i == 1 and k >= NHC_A):
                    P.emit("sp", lambda e, s=s, k=k: e.dma_start(out=dram["wbf"][k], in_=WS[s][:, :, :].rearrange("p k n -> p (k n)")),
                           reads=[b_WS[s]], writes=[b_WBF[k]], dma_sem=s_wst[s], name="stw_" + name)
            else:
                P.emit("pool", lambda e, s=s, k=k: e.dma_start(out=WS[s][:, :, :].rearrange("p k n -> p (k n)"), in_=dram["wbf"][k]),
                       reads=[b_WBF[k]], writes=[b_WS[s]], dma_sem=s_w[s], name="ldwb_" + name)

        def next_w(name, h):
            idx = wstate["n"]
            assert WORDER[(idx % NHC) // 2] == name and idx % 2 == h, (name, h, idx)
            while wstate["issued"] <= min(idx + PREF, total_w - 1):
                issue_w(wstate["issued"])
                wstate["issued"] += 1
            wstate["n"] = idx + 1
            return WS[idx % NWS], b_WS[idx % NWS]

        def proj_fm(wname, src, src_bufs, consume, before=None):
            for m in range(8):
                if m % 4 == 0:
                    W, bW = next_w(wname, m // 4)
                if before is not None:
                    before(m)
                P.lab = wname
                ps, bps = next_ps()
                mm = m % 4
                for k in range(8):
                    P.emit("pe", lambda e, W=W, ps=ps, k=k, mm=mm: e.matmul(ps[:, :], W[:, k, mm * 128:(mm + 1) * 128], src(k), start=(k == 0), stop=(k == 7)),
                           reads=[bW] + src_bufs(k), writes=[bps])
                consume(m, ps, bps)

        def proj_tm(wname, src, src_bufs, consume):
            for half in range(2):
                W, bW = next_w(wname, half)
                P.lab = wname
                for j in range(4):
                    ps, bps = next_ps()
                    for k in range(8):
                        P.emit("pe", lambda e, W=W, ps=ps, k=k, j=j: e.matmul(ps[:, :], src(k, j), W[:, k, :], start=(k == 0), stop=(k == 7)),
                               reads=[bW] + src_bufs(k), writes=[bps])
                    consume(j, half, ps, bps)

        par = lambda col: PAR[:, col:col + 1]
        dp = lambda col: DP[:, col:col + 1]

        hsrc = lambda k: HT[:, k, :]
        hbuf = lambda k: [b_HT[k]]
        hsrc_tm = lambda k, j: HT[:, k, j * 128:(j + 1) * 128]

        def s0_load(tile):
            XTt, b_XTt = XT[tile % 2], b_XT[tile % 2]
            P.emit("sp", lambda e, tile=tile: e.dma_start(out=XTt[:, :, :], in_=xv[tile]),
                   writes=b_XTt, dma_sem=s_ld[tile % 2])

        def s0_stats(tile):
            P.lab = "s0_stats"
            XTt, b_XTt = XT[tile % 2], b_XT[tile % 2]
            for j in range(4):
                P.emit("act", lambda e, j=j: e.activation(SQJ[:, :], XTt[:, j, :], AF.Square, accum_out=SMALL[:, j:j + 1]),
                       reads=[b_XTt[j]], writes=[b_SQJ, b_SMs0])
            P.emit("act", lambda e: e.activation(SMALL[:, 4:8], SMALL[:, 0:4], AF.Ln, bias=EPS, scale=1.0 / D),
                   reads=[b_SMs0], writes=[b_SMs0])
            P.emit("act", lambda e: e.activation(SMALL[:, 8:12], SMALL[:, 4:8], AF.Exp, scale=-0.5),
                   reads=[b_SMs0], writes=[b_SMs0])
            for j in range(4):
                P.emit("dve", lambda e, j=j: e.tensor_scalar(XN.ap(j), XTt[:, j, :], SMALL[:, 8 + j:9 + j], None, ALU.mult),
                       reads=[b_XTt[j], b_SMs0], writes=XN.b(j))

        def s0_tr(tile):
            P.lab = "s0_tr"
            for c in range(NCH):
                pt, bpt = next_ps()
                ptb = pt[:, :].bitcast(BF16)
                for j in range(4):
                    P.emit("pe", lambda e, c=c, j=j, ptb=ptb: e.transpose(ptb[:, j * 128:(j + 1) * 128], XN.ap(j, c * 128, (c + 1) * 128), IDB[:, :]),
                           reads=XN.b(j) + [b_IDB], writes=[bpt])
                P.emit("act", lambda e, c=c, ptb=ptb: e.activation(HT[:, c, :], ptb[:, 0:512], AF.Copy, scale=par(P_NG + c)),
                       reads=[bpt, b_PAR], writes=[b_HT[c]])

        def front(tile, it, after_xa=None):
            if it == 0:
                P.emit("dve", lambda e: e.memset(XA.ap3(0, 3), 0.0), writes=XA.ball())
                P.emit("dve", lambda e: e.memset(HST[:, :], 0.0), writes=[b_HST])
                P.emit("dve", lambda e: e.memset(REM[:, :], 0.0), writes=[b_REM])
                P.emit("dve", lambda e: e.memset(TST[:, :, :], 0.0), writes=b_TST)
            else:
                P.emit("dve", lambda e: e.tensor_copy(XA.ap3(0, 3), HALO[:, :, :]), reads=[b_HALO], writes=XA.ball())

            def cast_xc(m):
                P.emit("dve", lambda e: e.tensor_copy(XCb.ap(m), XC.ap(m)), reads=XC.b(m), writes=XCb.b(m))

            def cons_xa(m, ps, bps):
                P.emit("act", lambda e: e.activation(XA.ap(m, 3, 3 + TT), ps[:, :], AF.Copy), reads=[bps], writes=XA.b(m))
                P.emit("dve", lambda e: e.tensor_scalar(XC.ap(m), XA.ap(m, 0, TT), par(P_CW + m), par(P_CB + m), ALU.mult, ALU.add),
                       reads=XA.b(m) + [b_PAR], writes=XC.b(m))
                for k in range(1, 4):
                    P.emit("dve", lambda e, k=k: e.scalar_tensor_tensor(XC.ap(m), XA.ap(m, k, k + TT), par(P_CW + 8 * k + m), XC.ap(m), ALU.mult, ALU.add),
                           reads=XA.b(m) + XC.b(m) + [b_PAR], writes=XC.b(m))
            proj_fm("xa", hsrc, hbuf, cons_xa)
            P.emit("dve", lambda e: e.tensor_copy(HALO[:, :, :], XA.ap3(TT, TT + 3)), reads=XA.ball(), writes=[b_HALO])
            if after_xa is not None:
                after_xa()

            def cons_ga(m, ps, bps):
                P.emit("act", lambda e: e.activation(SG.ap(m), ps[:, :], AF.Silu), reads=[bps], writes=SG.b(m))
            proj_fm("ga", hsrc, hbuf, cons_ga)
            P.lab = "cast"
            for m in range(8):
                cast_xc(m)

            def cons_gma(m, ps, bps):
                P.emit("act", lambda e: e.activation(GA.ap(m), ps[:, :], AF.Sigmoid, bias=par(P_BM + m)),
                       reads=[bps, b_PAR], writes=GA.b(m))
            proj_fm("gma", hsrc, hbuf, cons_gma)

            P.lab = "bd"
            for m in range(8):
                ps, bps = next_ps()
                P.emit("pe", lambda e, ps=ps, m=m: e.matmul(ps[:, :], WX[:, m, :], XCb.ap(m), start=True, stop=True),
                       reads=[b_WX] + XCb.b(m), writes=[bps])
                P.emit("act", lambda e, ps=ps, m=m: e.activation(GI.ap(m), ps[:, :], AF.Sigmoid, bias=par(P_BX + m)),
                       reads=[bps, b_PAR], writes=GI.b(m))
                ps, bps = next_ps()
                P.emit("pe", lambda e, ps=ps, m=m: e.matmul(ps[:, :], WA[:, m, :], XCb.ap(m), start=True, stop=True),
                       reads=[b_WA] + XCb.b(m), writes=[bps])
                P.emit("act", lambda e, ps=ps, m=m: e.activation(AA.ap(m), ps[:, :], AF.Sigmoid, bias=par(P_BA + m)),
                       reads=[bps, b_PAR], writes=AA.b(m))

            P.lab = "A1"
            for m in range(8):
                P.emit("act", lambda e, m=m: e.activation(AA.ap(m), AA.ap(m), AF.Exp, scale=dp(m)),
                       reads=AA.b(m) + [b_DP], writes=AA.b(m))
            for m in range(8):
                P.emit("dve", lambda e, m=m: e.scalar_tensor_tensor(MM.ap(m), AA.ap(m), -1.0, AA.ap(m), ALU.mult, ALU.mult),
                       reads=AA.b(m), writes=MM.b(m))

            def cons_v(j, half, ps, bps):
                P.emit("dve", lambda e: e.tensor_copy(VV.ap(j, half * 512, half * 512 + 512), ps[:, :]),
                       reads=[bps], writes=VV.b(j, half * 512, half * 512 + 512))
            proj_tm("i", hsrc_tm, hbuf, cons_v)


            P.lab = "A2"
            for m in range(8):
                P.emit("act", lambda e, m=m: e.activation(MM.ap(m), MM.ap(m), AF.Ln, bias=1.0),
                       reads=MM.b(m), writes=MM.b(m))


            P.lab = "A3"
            for m in range(8):
                P.emit("act", lambda e, m=m: e.activation(MM.ap(m), MM.ap(m), AF.Exp, scale=0.5),
                       reads=MM.b(m), writes=MM.b(m))
            for m in range(8):
                P.emit("dve", lambda e, m=m: e.tensor_tensor(MM.ap(m), MM.ap(m), GI.ap(m), ALU.mult),
                       reads=MM.b(m) + GI.b(m), writes=MM.b(m))
                P.emit("dve", lambda e, m=m: e.tensor_tensor(XC.ap(m), MM.ap(m), XC.ap(m), ALU.mult),
                       reads=MM.b(m) + XC.b(m), writes=XC.b(m))
                P.emit("dve", lambda e, m=m: e.tensor_tensor_scan(MM.ap(m), AA.ap(m), XC.ap(m), HST[:, m:m + 1], ALU.mult, ALU.add),
                       reads=AA.b(m) + XC.b(m) + [b_HST], writes=MM.b(m))
                P.emit("dve", lambda e, m=m: e.tensor_tensor(YA.ap(m), MM.ap(m), SG.ap(m), ALU.mult),
                       reads=MM.b(m) + SG.b(m), writes=YA.b(m))
            P.emit("dve", lambda e: e.tensor_copy(HST[:, :], MM.ap3()[:, :, TT - 1]), reads=MM.ball(), writes=[b_HST])

            def cons_gb(m, ps, bps):
                P.emit("act", lambda e: e.activation(SGB.ap(m), ps[:, :], AF.Silu), reads=[bps], writes=SGB.b(m))
            proj_fm("gb", hsrc, hbuf, cons_gb)

            def cons_q(m, ps, bps):
                P.emit("act", lambda e: e.activation(SQ.ap(m), ps[:, :], AF.Silu), reads=[bps], writes=SQ.b(m))
            proj_fm("q", hsrc, hbuf, cons_q)

            def cons_f(m, ps, bps):
                P.emit("act", lambda e: e.activation(SS.ap(m), ps[:, :], AF.Sigmoid), reads=[bps], writes=SS.b(m))
                P.emit("act", lambda e: e.activation(SMb.ap(m), ps[:, :], AF.Sigmoid, scale=-1.0), reads=[bps], writes=SMb.b(m))
            proj_fm("f", hsrc, hbuf, cons_f)

            P.lab = "B1"
            for hd in range(8):
                P.emit("act", lambda e, hd=hd: e.activation(SS.ap(hd), SS.ap(hd), AF.Ln, bias=dp(8 + hd), scale=dp(16 + hd)),
                       reads=SS.b(hd) + [b_DP], writes=SS.b(hd))
            def bscan(hd):
                P.emit("dve", lambda e: e.tensor_tensor_scan(BB.ap(hd), ONES[:, 0:1].to_broadcast([128, TT]), SS.ap(hd), 0.0, ALU.mult, ALU.add),
                       reads=SS.b(hd) + [b_ONES], writes=BB.b(hd), name="B1")
            for hd in range(4):
                bscan(hd)
            DG3 = SM2[:, 0:32].rearrange("p (h n) -> p h n", h=8)
            bref = BB.ap3()[:, :, 63::128]

            def decay_book():
                P.emit("dve", lambda e: e.tensor_tensor(DG3[:, :, 1:4], bref[:, :, 1:4], bref[:, :, 0:3], ALU.subtract),
                       reads=BB.ball(), writes=[b_SM2], name="B2")
                P.emit("dve", lambda e: e.tensor_tensor(DG3[:, :, 0], bref[:, :, 0], REM[:, :], ALU.add),
                       reads=BB.ball() + [b_REM], writes=[b_SM2], name="B2")
                P.emit("dve", lambda e: e.tensor_tensor(REM[:, :], BB.ap3()[:, :, TT - 1], bref[:, :, 3], ALU.subtract),
                       reads=BB.ball(), writes=[b_REM], name="B2")
                P.emit("act", lambda e: e.activation(SM2[:, 32:64], SM2[:, 0:32], AF.Exp), reads=[b_SM2], writes=[b_SM2], name="B2")

            def bc(hd):
                bc_out = SS.ap(hd).rearrange("p (n t) -> p n t", n=4)
                b_in = BB.ap(hd).rearrange("p (n t) -> p n t", n=4)
                b_ref = BB.ap(hd)[:, 63::128].unsqueeze(2).to_broadcast([128, 4, 128])
                P.emit("dve", lambda e, o=bc_out, i0=b_in, i1=b_ref: e.tensor_tensor(o, i0, i1, ALU.subtract),
                       reads=BB.b(hd), writes=SS.b(hd), name="B2")

            def cons_pa(m, ps, bps):
                P.emit("dve", lambda e: e.tensor_tensor(OA.ap(m), ps[:, :], GA.ap(m), ALU.mult),
                       reads=[bps] + GA.b(m), writes=OA.b(m))
                if m < 4:
                    bscan(4 + m)
                    if m == 3:
                        decay_book()
                else:
                    bc(2 * (m - 4))
                    bc(2 * (m - 4) + 1)
            proj_fm("pa", lambda k: YA.ap(k), lambda k: YA.b(k), cons_pa)

            def epen(m):
                if m % 4 != 0:
                    return
                P.lab = "B2"
                for hd in range(m, m + 4):
                    P.emit("act", lambda e, hd=hd: e.activation(EP.ap(hd), SS.ap(hd), AF.Exp, bias=dp(24 + hd)),
                           reads=SS.b(hd) + [b_DP], writes=EP.b(hd))
                    P.emit("act", lambda e, hd=hd: e.activation(EN.ap(hd), SS.ap(hd), AF.Exp, scale=-1.0),
                           reads=SS.b(hd), writes=EN.b(hd))
            def cons_gmb(m, ps, bps):
                P.emit("act", lambda e: e.activation(GB.ap(m), ps[:, :], AF.Sigmoid, bias=par(P_BM + 8 + m)),
                       reads=[bps, b_PAR], writes=GB.b(m))
            proj_fm("gmb", hsrc, hbuf, cons_gmb, before=epen)

            P.lab = "B3"
            for hd in range(8):
                P.emit("dve", lambda e, hd=hd: e.tensor_tensor(SQ.ap(hd), SQ.ap(hd), EP.ap(hd), ALU.mult),
                       reads=SQ.b(hd) + EP.b(hd), writes=SQ.b(hd))
                P.emit("dve", lambda e, hd=hd: e.tensor_tensor(SMb.ap(hd), SMb.ap(hd), EN.ap(hd), ALU.mult),
                       reads=SMb.b(hd) + EN.b(hd), writes=SMb.b(hd))
            QT, KT = SQ, SMb

            P.lab = "KTtr"
            for hd in range(8):
                pt, bpt = next_ps()
                ptb = pt[:, :].bitcast(BF16)
                for n in range(4):
                    P.emit("pe", lambda e, hd=hd, n=n, ptb=ptb: e.transpose(ptb[:, n * 128:(n + 1) * 128], KT.ap(hd, n * 128, (n + 1) * 128), IDB[:, :]),
                           reads=KT.b(hd) + [b_IDB], writes=[bpt])
                P.emit("act", lambda e, hd=hd, ptb=ptb: e.activation(KTT.ap(hd), ptb[:, 0:512], AF.Copy),
                       reads=[bpt], writes=KTT.b(hd))
            G3 = SM2[:, 32:64].rearrange("p (h n) -> p h n", h=8)

            def emit_sbf(n, half):
                hsl = slice(half * 4, half * 4 + 4)
                P.emit("dve", lambda e: e.tensor_tensor(SBF[:, hsl, :], TST[:, hsl, :], G3[:, hsl, n].unsqueeze(2).to_broadcast([128, 4, 128]), ALU.mult),
                       reads=b_TST[hsl] + [b_SM2], writes=b_SBF[hsl])
            att_rr = [0]
            pend_att = {}

            def att_group(n, half):
                P.lab = "heads%d" % n
                ps_a, bps_a = next_ps()
                for hh in range(4):
                    hd = half * 4 + hh
                    P.emit("pe", lambda e, hh=hh, hd=hd: e.matmul(ps_a[:, hh * 128:(hh + 1) * 128], KT.ap(hd, n * 128, (n + 1) * 128), QT.ap(hd, n * 128, (n + 1) * 128), start=True, stop=True),
                           reads=KT.b(hd) + QT.b(hd), writes=[bps_a])
                gi = att_rr[0] % 2
                att_rr[0] += 1
                P.emit("dve", lambda e: e.tensor_tensor(ATT4[gi][:, :, :], ps_a[:, :].rearrange("p (h t) -> p h t", h=4),
                                                        MSK[:, :].unsqueeze(1).to_broadcast([128, 4, 128]), ALU.mult),
                       reads=[bps_a, b_MSK], writes=[b_ATT4[gi]])
                pend_att[(n, half)] = gi

            def heads(n, half):
                P.lab = "heads%d" % n
                POn, b_POn = PO[n % 2], b_PO[n % 2]
                gcols = [SM2[:, 32 + hd * 4 + n:33 + hd * 4 + n] for hd in range(8)]
                if n == 0 and half == 0:
                    emit_sbf(0, 0)
                    emit_sbf(0, 1)
                    att_group(0, 0)
                nxt = (n, 1) if half == 0 else (n + 1, 0)
                if nxt[0] < 4:
                    att_group(*nxt)
                P.lab = "heads%d" % n
                gi = pend_att.pop((n, half))
                for hd in range(half * 4, half * 4 + 4):
                    hs = slice(hd * 128, (hd + 1) * 128)
                    P.emit("pe", lambda e, hd=hd, hs=hs: e.matmul(POn[:, hs], ATT4[gi][:, hd % 4, :], VV.ap(n, hs.start, hs.stop), start=True, stop=False),
                           reads=[b_ATT4[gi]] + VV.b(n), writes=[b_POn[hd // 4]])
                    P.emit("pe", lambda e, hd=hd, hs=hs: e.matmul(POn[:, hs], QT.ap(hd, n * 128, (n + 1) * 128), SBF[:, hd, :], start=False, stop=True),
                           reads=QT.b(hd) + [b_SBF[hd]], writes=[b_POn[hd // 4]])
                    ps_k, bps_k = next_ps()
                    P.emit("pe", lambda e, ps_k=ps_k, hd=hd, hs=hs: e.matmul(ps_k[:, 0:128], KTT.ap(hd, n * 128, (n + 1) * 128), VV.ap(n, hs.start, hs.stop), start=True, stop=True),
                           reads=KTT.b(hd) + VV.b(n), writes=[bps_k])
                    P.emit("dve", lambda e, ps_k=ps_k, hd=hd, gcol=gcols[hd]: e.scalar_tensor_tensor(TST[:, hd, :], TST[:, hd, :], gcol, ps_k[:, 0:128], ALU.mult, ALU.add),
                           reads=[b_TST[hd], b_SM2, bps_k], writes=[b_TST[hd]])
                smc = 64 + 24 * (n % 2)
                for h2 in range(half * 4, half * 4 + 4):
                    P.emit("act", lambda e, h2=h2: e.activation(SQJ[:, h2 * 128:(h2 + 1) * 128], POn[:, h2 * 128:(h2 + 1) * 128], AF.Square,
                                                                accum_out=SMALL[:, smc + h2:smc + h2 + 1]),
                           reads=[b_POn[half]], writes=[b_SQJ, b_SMhg2[n % 2]])
                if n + 1 < 4:
                    emit_sbf(n + 1, half)

            def stats(n):
                P.lab = "stats%d" % n
                POn, b_POn = PO[n % 2], b_PO[n % 2]
                smc = 64 + 24 * (n % 2)
                b_sm = b_SMhg2[n % 2]
                P.emit("act", lambda e: e.activation(SMALL[:, smc + 8:smc + 16], SMALL[:, smc:smc + 8], AF.Ln, bias=EPS, scale=1.0 / 128),
                       reads=[b_sm], writes=[b_sm])
                P.emit("act", lambda e: e.activation(SMALL[:, smc + 16:smc + 24], SMALL[:, smc + 8:smc + 16], AF.Exp, scale=-0.5),
                       reads=[b_sm], writes=[b_sm])
                P.emit("dve", lambda e: e.tensor_tensor(ONB[:, :].rearrange("p (h v) -> p h v", h=8), POn[:, :].rearrange("p (h v) -> p h v", h=8),
                                                        SMALL[:, smc + 16:smc + 24].unsqueeze(2).to_broadcast([128, 8, 128]), ALU.mult),
                       reads=b_POn + [b_sm], writes=[b_ONB])

            def trans(n):
                P.lab = "trans%d" % n
                cs = slice(n * 128, (n + 1) * 128)
                pt, bpt = next_ps()
                ptb = pt[:, :].bitcast(BF16)
                for hd in range(8):
                    P.emit("pe", lambda e, hd=hd, ptb=ptb: e.transpose(ptb[:, hd * 128:(hd + 1) * 128], ONB[:, hd * 128:(hd + 1) * 128], IDB[:, :]),
                           reads=[b_ONB, b_IDB], writes=[bpt])
                if n == 3:
                    P.emit("dve", lambda e, cs=cs, ptb=ptb: e.scalar_tensor_tensor(YB.ap3(cs.start, cs.stop), ptb[:, :].rearrange("p (h t) -> p h t", h=8), par(P_HG),
                                                                                  SGB.ap3(cs.start, cs.stop), ALU.mult, ALU.mult),
                           reads=[bpt, b_PAR] + SGB.ball(), writes=YB.ball())
                    return
                P.emit("act", lambda e, cs=cs, ptb=ptb: e.activation(YB.ap3(cs.start, cs.stop), ptb[:, :].rearrange("p (h t) -> p h t", h=8), AF.Copy, scale=par(P_HG)),
                       reads=[bpt, b_PAR], writes=YB.ball())
                P.emit("pool", lambda e, cs=cs: e.tensor_tensor(YB.ap3(cs.start, cs.stop), YB.ap3(cs.start, cs.stop), SGB.ap3(cs.start, cs.stop), ALU.mult),
                       reads=YB.ball() + SGB.ball(), writes=YB.ball())

            heads(0, 0)
            heads(0, 1)
            for n in range(1, 4):
                heads(n, 0)
                stats(n - 1)
                heads(n, 1)
                trans(n - 1)
            stats(3)
            trans(3)

        def merge_mm(tile):
            XTt, b_XTt = XT[tile % 2], b_XT[tile % 2]
            def cons_pb(m, ps, bps):
                P.emit("dve", lambda e: e.tensor_tensor(MIX.ap(m), ps[:, :], GB.ap(m), ALU.mult),
                       reads=[bps] + GB.b(m), writes=MIX.b(m))
                P.emit("dve", lambda e: e.tensor_tensor(MIX.ap(m), MIX.ap(m), OA.ap(m), ALU.add),
                       reads=MIX.b(m) + OA.b(m), writes=MIX.b(m))
            proj_fm("pb", lambda k: YB.ap(k), lambda k: YB.b(k), cons_pb)

        def merge_wo(tile):
            XTt, b_XTt = XT[tile % 2], b_XT[tile % 2]

            def cons_wo(j, half, ps, bps):
                sl = slice(half * 512, half * 512 + 512)
                P.emit("dve", lambda e: e.tensor_tensor(XTt[:, j, sl], ps[:, :], XTt[:, j, sl], ALU.add),
                       reads=[bps, b_XTt[j]], writes=[b_XTt[j]])
            proj_tm("wo", lambda k, j: MIX.ap(k, j * 128, (j + 1) * 128), lambda k: MIX.b(k), cons_wo)

        s_fin = [sem("st_f%d" % j) for j in range(4)]
        b_SMfj = [Buf("SMfin%d" % j) for j in range(4)]

        def final_split(tile):
            P.lab = "final"
            XTt, b_XTt = XT[tile % 2], b_XT[tile % 2]
            for j in range(4):
                P.emit("act", lambda e, j=j: e.activation(SQJ[:, :], XTt[:, j, :], AF.Square, accum_out=SMALL[:, 40 + j:41 + j]),
                       reads=[b_XTt[j]], writes=[b_SQJ, b_SMfj[j]])
                P.emit("act", lambda e, j=j: e.activation(SMALL[:, 44 + j:45 + j], SMALL[:, 40 + j:41 + j], AF.Ln, bias=EPS, scale=1.0 / D),
                       reads=[b_SMfj[j]], writes=[b_SMfj[j]])
                P.emit("act", lambda e, j=j: e.activation(SMALL[:, 48 + j:49 + j], SMALL[:, 44 + j:45 + j], AF.Exp, scale=-0.5),
                       reads=[b_SMfj[j]], writes=[b_SMfj[j]])
                P.emit("dve", lambda e, j=j: e.scalar_tensor_tensor(XTt[:, j, :], XTt[:, j, :], SMALL[:, 48 + j:49 + j], FNG[:, :], ALU.mult, ALU.mult),
                       reads=[b_XTt[j], b_SMfj[j], b_FNG], writes=[b_XTt[j]])
                op = P.emit("sp", lambda e, tile=tile, j=j: e.dma_start(out=yv[tile][:, j, :], in_=XTt[:, j, :]), reads=[b_XTt[j]], dma_sem=s_fin[j])
                out_ops.append(op)

        def final(tile):
            P.lab = "final"
            XTt, b_XTt = XT[tile % 2], b_XT[tile % 2]
            for j in range(4):
                P.emit("act", lambda e, j=j: e.activation(SQJ[:, :], XTt[:, j, :], AF.Square, accum_out=SMALL[:, 40 + j:41 + j]),
                       reads=[b_XTt[j]], writes=[b_SQJ, b_SMfin])
            P.emit("act", lambda e: e.activation(SMALL[:, 44:48], SMALL[:, 40:44], AF.Ln, bias=EPS, scale=1.0 / D),
                   reads=[b_SMfin], writes=[b_SMfin])
            P.emit("act", lambda e: e.activation(SMALL[:, 48:52], SMALL[:, 44:48], AF.Exp, scale=-0.5),
                   reads=[b_SMfin], writes=[b_SMfin])
            for j in range(4):
                P.emit("dve", lambda e, j=j: e.scalar_tensor_tensor(XTt[:, j, :], XTt[:, j, :], SMALL[:, 48 + j:49 + j], FNG[:, :], ALU.mult, ALU.mult),
                       reads=[b_XTt[j], b_SMfin, b_FNG], writes=[b_XTt[j]])
            op = P.emit("sp", lambda e, tile=tile: e.dma_start(out=yv[tile], in_=XTt[:, :, :]), reads=b_XTt, dma_sem=s_st[tile % 2])
            out_ops.append(op)


        ntiles = NB * NT
        s0_stats(0)
        s0_tr(0)
        for tile in range(ntiles):
            it = tile % NT
            if tile == 0:
                s0_load(1)
                P.emit("sp", lambda e: e.dma_start(out=FNG[:, :], in_=dram["fng"]), writes=[b_FNG], dma_sem=s_c[1])
                front(tile, it)
            else:
                def after_xa(tile=tile):
                    final(tile - 1)
                    if tile + 1 < ntiles:
                        s0_load(tile + 1)
                front(tile, it, after_xa)
            merge_mm(tile)
            if tile + 1 < ntiles:
                s0_stats(tile + 1)
            merge_wo(tile)
            if tile + 1 < ntiles:
                s0_tr(tile + 1)
        final_split(ntiles - 1)

        block = es.enter_context(nc.Block())
        P.finalize(nc, block, engine_sems, out_ops)
        global LAST_PROG
        LAST_PROG = P


def _consts():
    c = np.zeros((128, 256), np.float32)
    c[:, 0:128] = np.eye(128, dtype=np.float32)
    s = np.arange(128)[:, None]
    t = np.arange(128)[None, :]
    c[:, 128:256] = (s <= t).astype(np.float32)
    return c


def _pack_params(inp):
    def fm(v, n):
        return np.ascontiguousarray(np.asarray(v, np.float32).reshape(n, 128).T)
    p = np.zeros((128, NPAR), np.float32)
    p[:, P_NG:P_NG + 8] = fm(inp["norm_g"][0], 8)
    for k in range(4):
        p[:, P_CW + 8 * k:P_CW + 8 * k + 8] = fm(inp["conv_w"][0, k], 8)
    p[:, P_CB:P_CB + 8] = fm(inp["conv_b"][0], 8)
    p[:, P_BX:P_BX + 8] = fm(inp["rg_bx"][0].reshape(-1), 8)
    p[:, P_BA:P_BA + 8] = fm(inp["rg_ba"][0].reshape(-1), 8)
    p[:, P_LAM:P_LAM + 8] = fm(inp["rg_lambda"][0], 8)
    p[:, P_L0:P_L0 + 8] = fm(inp["hg_lb_logits"][0], 8)
    p[:, P_L1:P_L1 + 8] = fm(inp["hg_lb_logits"][1], 8)
    p[:, P_HG:P_HG + 1] = fm(inp["hg_norm_g"][0], 1)
    p[:, P_BM:P_BM + 16] = fm(inp["b_merge"][0], 16)
    return p


def _pack_weights(inp):
    w_in = np.asarray(inp["w_in"], np.float32)[0]
    extra = {"pa": np.asarray(inp["proj_a"], np.float32)[0], "pb": np.asarray(inp["proj_b"], np.float32)[0],
             "wo": np.asarray(inp["w_out"], np.float32)[0]}
    out = np.empty((2 * len(WORDER), 128, 8 * 512), np.float32)
    for i, name in enumerate(WORDER):
        W = extra[name] if name in extra else w_in[:, W_IN_COLS[name]:W_IN_COLS[name] + 1024]
        Wr = W.reshape(8, 128, 2, 512)
        out[2 * i:2 * i + 2] = Wr.transpose(2, 1, 0, 3).reshape(2, 128, 8 * 512)
    return out


def make_in_maps(inp, n_cores=8):
    x = np.asarray(inp["x"], np.float32)
    shared = {
        "wpk": _pack_weights(inp),
        "rg_wx": np.ascontiguousarray(np.asarray(inp["rg_wx"], np.float32)[0].transpose(1, 0, 2)),
        "rg_wa": np.ascontiguousarray(np.asarray(inp["rg_wa"], np.float32)[0].transpose(1, 0, 2)),
        "params": _pack_params(inp),
        "fng": np.ascontiguousarray(np.broadcast_to(np.asarray(inp["final_norm_g"], np.float32)[None, :], (128, D))),
        "consts": _consts(),
    }
    maps = []
    for c in range(n_cores):
        m = dict(shared)
        m["x"] = np.ascontiguousarray(x[NB * c:NB * (c + 1)].reshape(NB * SEQ, D))
        maps.append(m)
    return maps


def kernel(**inputs):
    nc = build_nc()
    in_maps = make_in_maps(inputs)
    res = run_bass_kernel_spmd(nc, in_maps, core_ids=list(range(8)))
    out = np.concatenate([np.asarray(r["y"]).reshape(NB, SEQ, D) for r in res.results], axis=0)
    return out.astype(np.float32)
```

```python
import numpy as np
import concourse.bass as bass
import concourse.mybir as mybir
from concourse.bass_utils import run_bass_kernel_spmd

F32 = mybir.dt.float32
BF16 = mybir.dt.bfloat16
AF = mybir.ActivationFunctionType
ALU = mybir.AluOpType
AX = mybir.AxisListType

D = 1024
SEQ = 2048
NB = 2
TT = 512
NT = SEQ // TT
NCH = 8
IN_COLS = 8192
EPS = 1e-6
LRU_C = 8.0
HG_SCALE = 128.0 ** -0.5

P_NG = 0
P_CW = 8
P_CB = 40
P_BX = 48
P_BA = 56
P_LAM = 64
P_L0 = 72
P_L1 = 80
P_HG = 88
P_BM = 89
NPAR = 105

WORDER = ["xa", "ga", "gma", "i", "gb", "q", "f", "pa", "gmb", "pb", "wo"]
W_IN_COLS = {"xa": 0, "ga": 1024, "q": 2048, "f": 3072, "i": 4096, "gb": 5120, "gma": 6144, "gmb": 7168}

ENGINES = ("pe", "act", "dve", "pool", "sp")
LAST_PROG = None


class Buf:
    __slots__ = ("name", "last_w", "readers")

    def __init__(self, name):
        self.name = name
        self.last_w = None
        self.readers = []


class Op:
    __slots__ = ("eng", "fn", "deps", "dma_sem", "sig", "need_sig", "name")

    def __init__(self, eng, fn, dma_sem, name):
        self.eng = eng
        self.fn = fn
        self.deps = []
        self.dma_sem = dma_sem
        self.sig = None
        self.need_sig = False
        self.name = name


class Prog:
    def __init__(self):
        self.ops = {e: [] for e in ENGINES}
        self.final_waits = []
        self.lab = ""

    def emit(self, eng, fn, reads=(), writes=(), dma_sem=None, name=""):
        op = Op(eng, fn, dma_sem, name or self.lab)
        deps = {}
        for b in reads:
            w = b.last_w
            if w is not None:
                deps[id(w)] = w
        for b in writes:
            w = b.last_w
            if w is not None:
                deps[id(w)] = w
            for r in b.readers:
                if r is not op:
                    deps[id(r)] = r
        for w in deps.values():
            if w.eng == "pe" and eng == "pe":
                continue
            op.deps.append(w)
            w.need_sig = True
        for b in reads:
            b.readers.append(op)
        for b in writes:
            b.last_w = op
            b.readers = []
        if dma_sem is not None:
            op.need_sig = True
        self.ops[eng].append(op)
        return op

    def finalize(self, nc, block, engine_sems, out_ops):
        counts = {}
        for e in ENGINES:
            for op in self.ops[e]:
                if not op.need_sig:
                    continue
                if op.dma_sem is not None:
                    sem, inc = op.dma_sem, 16
                else:
                    sem, inc = engine_sems[e], 1
                counts[id(sem)] = counts.get(id(sem), 0) + inc
                op.sig = (sem, counts[id(sem)], inc)
        self.sigtable = {e: [(op.sig[1], op.name) for op in self.ops[e] if op.sig is not None and op.dma_sem is None] for e in ENGINES}
        finals = {}
        for op in out_ops:
            sem, val, _ = op.sig
            if id(sem) not in finals or finals[id(sem)][1] < val:
                finals[id(sem)] = (sem, val)

        def run(e, engobj):
            waited = {}
            for op in self.ops[e]:
                need = {}
                for d in op.deps:
                    sem, val, _ = d.sig
                    if need.get(id(sem), (None, 0))[1] < val:
                        need[id(sem)] = (sem, val)
                for k, (sem, val) in need.items():
                    if waited.get(k, 0) >= val:
                        continue
                    engobj.wait_ge(sem, val)
                    waited[k] = val
                ins = op.fn(engobj)
                if op.sig is not None:
                    ins.then_inc(op.sig[0], op.sig[2])
            if e == "sp":
                for sem, val in finals.values():
                    engobj.wait_ge(sem, val)

        @block.tensor
        def _(eng):
            run("pe", eng)

        @block.scalar
        def _(eng):
            run("act", eng)

        @block.vector
        def _(eng):
            run("dve", eng)

        @block.gpsimd
        def _(eng):
            run("pool", eng)

        @block.sync
        def _(eng):
            run("sp", eng)


class Region:
    def __init__(self, nc, es, name, nbytes, gran=1024):
        assert nbytes % 4 == 0
        self.GRAN = gran
        self.t = es.enter_context(nc.sbuf_tensor(name, [128, nbytes // 4], F32))
        self.nbytes = nbytes
        self.bufs_ = [Buf("%s.%d" % (name, i)) for i in range((nbytes + self.GRAN - 1) // self.GRAN)]
        self.f32 = self.t[:, :]
        self.b16 = self.t[:, :].bitcast(BF16)

    def bufs(self, lo, hi):
        return self.bufs_[lo // self.GRAN:(hi + self.GRAN - 1) // self.GRAN]


class Stream:
    def __init__(self, reg, dt, base, stride, nch, width):
        self.reg, self.dt, self.base, self.stride, self.nch, self.width = reg, dt, base, stride, nch, width
        self.es = 4 if dt == F32 else 2
        self.flat = reg.f32 if dt == F32 else reg.b16
        assert (base + nch * stride) * self.es <= reg.nbytes + 0, (base, nch, stride, reg.nbytes)

    def ap(self, c, lo=0, hi=None):
        hi = self.width if hi is None else hi
        o = self.base + c * self.stride
        return self.flat[:, o + lo:o + hi]

    def b(self, c, lo=0, hi=None):
        hi = self.width if hi is None else hi
        o = self.base + c * self.stride
        return self.reg.bufs((o + lo) * self.es, (o + hi) * self.es)

    def ap3(self, lo=0, hi=None):
        hi = self.width if hi is None else hi
        v = self.flat[:, self.base:self.base + self.nch * self.stride].rearrange("p (c t) -> p c t", c=self.nch)
        return v[:, :, lo:hi]

    def ball(self):
        return self.reg.bufs(self.base * self.es, (self.base + self.nch * self.stride) * self.es)


def build_nc(stage=99, debug=None):
    nc = bass.Bass("TRN2", target_bir_lowering=False)
    dram = {}
    dram["x"] = nc.dram_tensor("x", [NB * SEQ, D], F32, kind="ExternalInput").ap()
    dram["wpk"] = nc.dram_tensor("wpk", [2 * len(WORDER), 128, 8 * 512], F32, kind="ExternalInput").ap()
    dram["rg_wx"] = nc.dram_tensor("rg_wx", [128, 8, 128], F32, kind="ExternalInput").ap()
    dram["rg_wa"] = nc.dram_tensor("rg_wa", [128, 8, 128], F32, kind="ExternalInput").ap()
    dram["params"] = nc.dram_tensor("params", [128, NPAR], F32, kind="ExternalInput").ap()
    dram["fng"] = nc.dram_tensor("fng", [128, D], F32, kind="ExternalInput").ap()
    dram["consts"] = nc.dram_tensor("consts", [128, 256], F32, kind="ExternalInput").ap()
    dram["y"] = nc.dram_tensor("y", [NB * SEQ, D], F32, kind="ExternalOutput").ap()
    dram["wbf"] = nc.dram_tensor("wbf", [22, 128, 8 * 512], BF16, kind="Internal").ap()
    dbg = None
    if debug is not None:
        dbg = nc.dram_tensor("dbg", list(debug), F32, kind="ExternalOutput").ap()
    _build(nc, dram, stage, dbg)
    return nc


def _build(nc, dram, stage, dbg):
    from contextlib import ExitStack
    P = Prog()
    with ExitStack() as es:
        def sb(name, shape, dt):
            return es.enter_context(nc.sbuf_tensor(name, shape, dt))

        def sem(name):
            return es.enter_context(nc.semaphore(name))

        engine_sems = {e: sem("s_" + e) for e in ENGINES}

        XT = [sb("XT%d" % i, [128, 4, D], F32) for i in range(2)]
        HT = sb("HT", [128, NCH, TT], BF16)
        SQJ = sb("SQJ", [128, D], BF16)
        PAR = sb("PAR", [128, NPAR], F32)
        DP = sb("DP", [128, 64], F32)
        FNG = sb("FNG", [128, D], F32)
        IDB = sb("IDB", [128, 128], BF16)
        ONES = sb("ONES", [128, 1], F32)
        SMALL = sb("SMALL", [128, 128], F32)
        SM2 = sb("SM2", [128, 64], F32)
        REM = sb("REM", [128, 8], F32)
        HST = sb("HST", [128, 8], F32)
        HALO = sb("HALO", [128, 8, 3], F32)
        TST = sb("TST", [128, 8, 128], F32)
        SBF = sb("SBF", [128, 8, 128], BF16)
        ATT4 = [sb("ATT4_%d" % i, [128, 4, 128], BF16) for i in range(2)]
        MSK = sb("MSK", [128, 128], F32)
        ONB = sb("ONB", [128, D], BF16)
        WX = sb("WXb", [128, 8, 128], BF16)
        WA = sb("WAb", [128, 8, 128], BF16)
        NWS = 4
        WS = [sb("WS%d" % i, [128, 8, 512], BF16) for i in range(NWS)]
        b_XT = [[Buf("XT%d_%d" % (i, j)) for j in range(4)] for i in range(2)]
        b_HT = [Buf("HT%d" % c) for c in range(NCH)]
        b_SQJ, b_PAR, b_DP, b_FNG = Buf("SQJ"), Buf("PAR"), Buf("DP"), Buf("FNG")
        b_IDB, b_MSK, b_ONES, b_SM2 = Buf("IDB"), Buf("MSK"), Buf("ONES"), Buf("SM2")
        b_SMs0, b_SMhg, b_SMfin = Buf("SMs0"), Buf("SMhg"), Buf("SMfin")
        b_SMhg2 = [Buf("SMhg0"), Buf("SMhg1")]
        b_REM, b_HST, b_HALO = Buf("REM"), Buf("HST"), Buf("HALO")
        b_TST = [Buf("TST%d" % h) for h in range(8)]
        b_SBF = [Buf("SBF%d" % h) for h in range(8)]
        b_ATT4 = [Buf("ATT4_%d" % i) for i in range(2)]
        b_ONB = Buf("ONB")
        b_WX, b_WA = Buf("WX"), Buf("WA")
        b_WS = [Buf("WS%d" % i) for i in range(4)]

        XW = 516
        R1 = Region(nc, es, "R1", 8 * XW * 4, gran=XW * 4)
        R2 = Region(nc, es, "R2", 8 * TT * 4)
        R3 = Region(nc, es, "R3", 8 * TT * 4)
        R4 = Region(nc, es, "R4", 8 * TT * 4)
        Q1 = Region(nc, es, "Q1", 8 * TT * 2)
        Q2 = Region(nc, es, "Q2", 8 * TT * 2)
        Q3 = Region(nc, es, "Q3", 8 * TT * 2)
        Q4 = Region(nc, es, "Q4", 8 * TT * 2)
        Q5 = Region(nc, es, "Q5", 8 * TT * 2)
        Q6 = Region(nc, es, "Q6", 8 * TT * 2)
        XA = Stream(R1, F32, 0, XW, 8, 3 + TT)
        XC = Stream(R2, F32, 0, TT, 8, TT)
        AA = Stream(R3, F32, 0, TT, 8, TT)
        MM = Stream(R4, F32, 0, TT, 8, TT)
        XCb = Stream(Q1, BF16, 0, TT, 8, TT)
        GI = Stream(Q2, BF16, 0, TT, 8, TT)
        SG = Stream(Q3, BF16, 0, TT, 8, TT)
        YA = Stream(Q4, BF16, 0, TT, 8, TT)
        GA = Stream(Q6, BF16, 0, TT, 8, TT)
        OA = Stream(Q5, BF16, 0, TT, 8, TT)
        SS = Stream(R2, F32, 0, TT, 8, TT)
        BB = Stream(R3, F32, 0, TT, 8, TT)
        SMb = Stream(Q2, BF16, 0, TT, 8, TT)
        EP = Stream(R4, BF16, 8 * TT, TT, 8, TT)
        EN = Stream(Q3, BF16, 0, TT, 8, TT)
        SQ = Stream(Q1, BF16, 0, TT, 8, TT)
        KTT = Stream(R4, BF16, 0, TT, 8, TT)
        VV = Stream(R1, BF16, 8 * TT, D, 4, D)
        SGB = Stream(R1, BF16, 0, TT, 8, TT)
        YB = Stream(R4, BF16, 8 * TT, TT, 8, TT)
        ON = Stream(Q6, BF16, 0, D, 4, D)
        XN = ON
        GB = Stream(Q6, BF16, 0, TT, 8, TT)
        MIX = Stream(Q3, BF16, 0, TT, 8, TT)

        PO = [es.enter_context(nc.psum_tensor("PO%d" % i, [128, 1024], F32)) for i in range(2)]
        b_PO = [[Buf("PO%d_%d" % (i, h)) for h in range(2)] for i in range(2)]
        NROT = 4
        PS = [es.enter_context(nc.psum_tensor("PS%d" % i, [128, 512], F32)) for i in range(NROT)]
        b_PS = [Buf("PS%d" % i) for i in range(NROT)]
        ps_rr = [0]

        def next_ps():
            i = ps_rr[0]
            ps_rr[0] = (i + 1) % NROT
            return PS[i], b_PS[i]

        s_ld = [sem("ld_x0"), sem("ld_x1")]
        s_c = [sem("ld_c%d" % i) for i in range(5)]
        s_st = [sem("st_y0"), sem("st_y1")]
        s_w = [sem("ld_w%d" % i) for i in range(4)]
        out_ops = []

        P.emit("sp", lambda e: e.dma_start(out=XT[0][:, :, :], in_=dram["x"].rearrange("(n j p) d -> n p j d", j=4, p=128)[0]),
               writes=b_XT[0], dma_sem=s_ld[0])
        P.emit("sp", lambda e: e.dma_start(out=PAR[:, :], in_=dram["params"]), writes=[b_PAR], dma_sem=s_c[0])
        CST = Q6.f32[:, 0:256]
        b_CST = Q6.bufs(0, 1024)
        P.emit("sp", lambda e: e.dma_start(out=CST, in_=dram["consts"]), writes=b_CST, dma_sem=s_c[2])
        P.emit("pool", lambda e: e.dma_start(out=WX[:, :, :], in_=dram["rg_wx"]), writes=[b_WX], dma_sem=s_c[3])
        P.emit("pool", lambda e: e.dma_start(out=WA[:, :, :], in_=dram["rg_wa"]), writes=[b_WA], dma_sem=s_c[4])
        P.emit("dve", lambda e: e.tensor_copy(IDB[:, :], CST[:, 0:128]), reads=b_CST, writes=[b_IDB])
        P.emit("dve", lambda e: e.tensor_copy(MSK[:, :], CST[:, 128:256]), reads=b_CST, writes=[b_MSK])
        P.emit("dve", lambda e: e.memset(ONES[:, :], 1.0), writes=[b_ONES])
        P.emit("act", lambda e: e.activation(DP[:, 32:40], PAR[:, P_LAM:P_LAM + 8], AF.Exp, scale=-1.0), reads=[b_PAR], writes=[b_DP])
        P.emit("act", lambda e: e.activation(DP[:, 40:48], DP[:, 32:40], AF.Ln, bias=1.0), reads=[b_DP], writes=[b_DP])
        P.emit("dve", lambda e: e.tensor_scalar(DP[:, 0:8], DP[:, 40:48], -LRU_C, None, ALU.mult), reads=[b_DP], writes=[b_DP])
        P.emit("dve", lambda e: e.tensor_tensor(DP[:, 48:56], PAR[:, P_L0:P_L0 + 8], PAR[:, P_L1:P_L1 + 8], ALU.subtract), reads=[b_PAR], writes=[b_DP])
        P.emit("act", lambda e: e.activation(DP[:, 8:16], DP[:, 48:56], AF.Sigmoid), reads=[b_DP], writes=[b_DP])
        P.emit("dve", lambda e: e.tensor_scalar(DP[:, 16:24], DP[:, 8:16], -1.0, 1.0, ALU.mult, ALU.add), reads=[b_DP], writes=[b_DP])
        P.emit("act", lambda e: e.activation(DP[:, 24:32], DP[:, 16:24], AF.Ln, scale=HG_SCALE), reads=[b_DP], writes=[b_DP])

        xv = dram["x"].rearrange("(n j p) d -> n p j d", j=4, p=128)
        yv = dram["y"].rearrange("(n j p) d -> n p j d", j=4, p=128)

        NHC = 2 * len(WORDER)
        total_w = NB * NT * NHC
        wstate = {"n": 0, "issued": 0}
        b_WBF = [Buf("WBF%d" % i) for i in range(NHC)]
        s_wst = [sem("st_w%d" % i) for i in range(NWS)]
        PREF = NWS - 1
        NHC_A = 12

        def issue_w(idx):
            if idx >= total_w:
                return
            k = idx % NHC
            name, h = WORDER[k // 2], k % 2
            s = idx % NWS
            tile_i = idx // NHC
            cached = (tile_i >= 2) or (tile_i == 1 and k < NHC_A)
            if not cached:
                gate = b_XT[0] if idx <= PREF else []
                P.emit("pool", lambda e, s=s, k=k: e.dma_start(out=WS[s][:, :, :].rearrange("p k n -> p (k n)"), in_=dram["wpk"][k]),
                       reads=gate, writes=[b_WS[s]], dma_sem=s_w[s], name="ldw_" + name)
                if (tile_i == 0 and k < NHC_A) or (tile_i == 1 and k >= NHC_A):
                    P.emit("sp", lambda e, s=s, k=k: e.dma_start(out=dram["wbf"][k], in_=WS[s][:, :, :].rearrange("p k n -> p (k n)")),
                           reads=[b_WS[s]], writes=[b_WBF[k]], dma_sem=s_wst[s], name="stw_" + name)
            else:
                P.emit("pool", lambda e, s=s, k=k: e.dma_start(out=WS[s][:, :, :].rearrange("p k n -> p (k n)"), in_=dram["wbf"][k]),
                       reads=[b_WBF[k]], writes=[b_WS[s]], dma_sem=s_w[s], name="ldwb_" + name)

        def next_w(name, h):
            idx = wstate["n"]
            assert WORDER[(idx % NHC) // 2] == name and idx % 2 == h, (name, h, idx)
            while wstate["issued"] <= min(idx + PREF, total_w - 1):
                issue_w(wstate["issued"])
                wstate["issued"] += 1
            wstate["n"] = idx + 1
            return WS[idx % NWS], b_WS[idx % NWS]

        def proj_fm(wname, src, src_bufs, consume, before=None):
            for m in range(8):
                if m % 4 == 0:
                    W, bW = next_w(wname, m // 4)
                if before is not None:
                    before(m)
                P.lab = wname
                ps, bps = next_ps()
                mm = m % 4
                for k in range(8):
                    P.emit("pe", lambda e, W=W, ps=ps, k=k, mm=mm: e.matmul(ps[:, :], W[:, k, mm * 128:(mm + 1) * 128], src(k), start=(k == 0), stop=(k == 7)),
                           reads=[bW] + src_bufs(k), writes=[bps])
                consume(m, ps, bps)

        def proj_tm(wname, src, src_bufs, consume, j_outer=False):
            if j_outer:
                Ws = [next_w(wname, 0), next_w(wname, 1)]
                P.lab = wname
                for j in range(4):
                    for half in range(2):
                        W, bW = Ws[half]
                        ps, bps = next_ps()
                        for k in range(8):
                            P.emit("pe", lambda e, W=W, ps=ps, k=k, j=j: e.matmul(ps[:, :], src(k, j), W[:, k, :], start=(k == 0), stop=(k == 7)),
                                   reads=[bW] + src_bufs(k), writes=[bps])
                        consume(j, half, ps, bps)
                return
            for half in range(2):
                W, bW = next_w(wname, half)
                P.lab = wname
                for j in range(4):
                    ps, bps = next_ps()
                    for k in range(8):
                        P.emit("pe", lambda e, W=W, ps=ps, k=k, j=j: e.matmul(ps[:, :], src(k, j), W[:, k, :], start=(k == 0), stop=(k == 7)),
                               reads=[bW] + src_bufs(k), writes=[bps])
                    consume(j, half, ps, bps)

        par = lambda col: PAR[:, col:col + 1]
        dp = lambda col: DP[:, col:col + 1]

        hsrc = lambda k: HT[:, k, :]
        hbuf = lambda k: [b_HT[k]]
        hsrc_tm = lambda k, j: HT[:, k, j * 128:(j + 1) * 128]

        def s0_load(tile):
            XTt, b_XTt = XT[tile % 2], b_XT[tile % 2]
            P.emit("sp", lambda e, tile=tile: e.dma_start(out=XTt[:, :, :], in_=xv[tile]),
                   writes=b_XTt, dma_sem=s_ld[tile % 2])

        def s0_stats(tile):
            P.lab = "s0_stats"
            XTt, b_XTt = XT[tile % 2], b_XT[tile % 2]
            for j in range(4):
                P.emit("act", lambda e, j=j: e.activation(SQJ[:, :], XTt[:, j, :], AF.Square, accum_out=SMALL[:, j:j + 1]),
                       reads=[b_XTt[j]], writes=[b_SQJ, b_SMs0])
            P.emit("act", lambda e: e.activation(SMALL[:, 4:8], SMALL[:, 0:4], AF.Ln, bias=EPS, scale=1.0 / D),
                   reads=[b_SMs0], writes=[b_SMs0])
            P.emit("act", lambda e: e.activation(SMALL[:, 8:12], SMALL[:, 4:8], AF.Exp, scale=-0.5),
                   reads=[b_SMs0], writes=[b_SMs0])
            for j in range(4):
                P.emit("dve", lambda e, j=j: e.tensor_scalar(XN.ap(j), XTt[:, j, :], SMALL[:, 8 + j:9 + j], None, ALU.mult),
                       reads=[b_XTt[j], b_SMs0], writes=XN.b(j))

        def s0_tr(tile):
            P.lab = "s0_tr"
            for c in range(NCH):
                pt, bpt = next_ps()
                ptb = pt[:, :].bitcast(BF16)
                for j in range(4):
                    P.emit("pe", lambda e, c=c, j=j, ptb=ptb: e.transpose(ptb[:, j * 128:(j + 1) * 128], XN.ap(j, c * 128, (c + 1) * 128), IDB[:, :]),
                           reads=XN.b(j) + [b_IDB], writes=[bpt])
                P.emit("act", lambda e, c=c, ptb=ptb: e.activation(HT[:, c, :], ptb[:, 0:512], AF.Copy, scale=par(P_NG + c)),
                       reads=[bpt, b_PAR], writes=[b_HT[c]])

        def front(tile, it, after_xa=None):
            if it == 0:
                P.emit("dve", lambda e: e.memset(XA.ap3(0, 3), 0.0), writes=XA.ball())
                P.emit("dve", lambda e: e.memset(HST[:, :], 0.0), writes=[b_HST])
                P.emit("dve", lambda e: e.memset(REM[:, :], 0.0), writes=[b_REM])
                P.emit("dve", lambda e: e.memset(TST[:, :, :], 0.0), writes=b_TST)
            else:
                P.emit("dve", lambda e: e.tensor_copy(XA.ap3(0, 3), HALO[:, :, :]), reads=[b_HALO], writes=XA.ball())

            def cast_xc(m):
                P.emit("dve", lambda e: e.tensor_copy(XCb.ap(m), XC.ap(m)), reads=XC.b(m), writes=XCb.b(m))

            def cons_xa(m, ps, bps):
                P.emit("act", lambda e: e.activation(XA.ap(m, 3, 3 + TT), ps[:, :], AF.Copy), reads=[bps], writes=XA.b(m))
                P.emit("dve", lambda e: e.tensor_scalar(XC.ap(m), XA.ap(m, 0, TT), par(P_CW + m), par(P_CB + m), ALU.mult, ALU.add),
                       reads=XA.b(m) + [b_PAR], writes=XC.b(m))
                for k in range(1, 4):
                    P.emit("dve", lambda e, k=k: e.scalar_tensor_tensor(XC.ap(m), XA.ap(m, k, k + TT), par(P_CW + 8 * k + m), XC.ap(m), ALU.mult, ALU.add),
                           reads=XA.b(m) + XC.b(m) + [b_PAR], writes=XC.b(m))
            proj_fm("xa", hsrc, hbuf, cons_xa)
            P.emit("dve", lambda e: e.tensor_copy(HALO[:, :, :], XA.ap3(TT, TT + 3)), reads=XA.ball(), writes=[b_HALO])
            if after_xa is not None:
                after_xa()

            def cons_ga(m, ps, bps):
                P.emit("act", lambda e: e.activation(SG.ap(m), ps[:, :], AF.Silu), reads=[bps], writes=SG.b(m))
            proj_fm("ga", hsrc, hbuf, cons_ga)
            P.lab = "cast"
            for m in range(8):
                cast_xc(m)

            def cons_gma(m, ps, bps):
                P.emit("act", lambda e: e.activation(GA.ap(m), ps[:, :], AF.Sigmoid, bias=par(P_BM + m)),
                       reads=[bps, b_PAR], writes=GA.b(m))
            proj_fm("gma", hsrc, hbuf, cons_gma)

            P.lab = "bd"
            for m in range(8):
                ps, bps = next_ps()
                P.emit("pe", lambda e, ps=ps, m=m: e.matmul(ps[:, :], WX[:, m, :], XCb.ap(m), start=True, stop=True),
                       reads=[b_WX] + XCb.b(m), writes=[bps])
                P.emit("act", lambda e, ps=ps, m=m: e.activation(GI.ap(m), ps[:, :], AF.Sigmoid, bias=par(P_BX + m)),
                       reads=[bps, b_PAR], writes=GI.b(m))
                ps, bps = next_ps()
                P.emit("pe", lambda e, ps=ps, m=m: e.matmul(ps[:, :], WA[:, m, :], XCb.ap(m), start=True, stop=True),
                       reads=[b_WA] + XCb.b(m), writes=[bps])
                P.emit("act", lambda e, ps=ps, m=m: e.activation(AA.ap(m), ps[:, :], AF.Sigmoid, bias=par(P_BA + m)),
                       reads=[bps, b_PAR], writes=AA.b(m))

            P.lab = "A1"
            for m in range(8):
                P.emit("act", lambda e, m=m: e.activation(AA.ap(m), AA.ap(m), AF.Exp, scale=dp(m)),
                       reads=AA.b(m) + [b_DP], writes=AA.b(m))
            for m in range(8):
                P.emit("dve", lambda e, m=m: e.scalar_tensor_tensor(MM.ap(m), AA.ap(m), -1.0, AA.ap(m), ALU.mult, ALU.mult),
                       reads=AA.b(m), writes=MM.b(m))

            def cons_v(j, half, ps, bps):
                P.emit("dve", lambda e: e.tensor_copy(VV.ap(j, half * 512, half * 512 + 512), ps[:, :]),
                       reads=[bps], writes=VV.b(j, half * 512, half * 512 + 512))
            proj_tm("i", hsrc_tm, hbuf, cons_v)


            P.lab = "A2"
            for m in range(8):
                P.emit("act", lambda e, m=m: e.activation(MM.ap(m), MM.ap(m), AF.Ln, bias=1.0),
                       reads=MM.b(m), writes=MM.b(m))


            P.lab = "A3"
            for m in range(8):
                P.emit("act", lambda e, m=m: e.activation(MM.ap(m), MM.ap(m), AF.Exp, scale=0.5),
                       reads=MM.b(m), writes=MM.b(m))
            for m in range(8):
                P.emit("dve", lambda e, m=m: e.tensor_tensor(MM.ap(m), MM.ap(m), GI.ap(m), ALU.mult),
                       reads=MM.b(m) + GI.b(m), writes=MM.b(m))
                P.emit("dve", lambda e, m=m: e.tensor_tensor(XC.ap(m), MM.ap(m), XC.ap(m), ALU.mult),
                       reads=MM.b(m) + XC.b(m), writes=XC.b(m))
                P.emit("dve", lambda e, m=m: e.tensor_tensor_scan(MM.ap(m), AA.ap(m), XC.ap(m), HST[:, m:m + 1], ALU.mult, ALU.add),
                       reads=AA.b(m) + XC.b(m) + [b_HST], writes=MM.b(m))
                P.emit("dve", lambda e, m=m: e.tensor_tensor(YA.ap(m), MM.ap(m), SG.ap(m), ALU.mult),
                       reads=MM.b(m) + SG.b(m), writes=YA.b(m))
            P.emit("dve", lambda e: e.tensor_copy(HST[:, :], MM.ap3()[:, :, TT - 1]), reads=MM.ball(), writes=[b_HST])

            def cons_gb(m, ps, bps):
                P.emit("act", lambda e: e.activation(SGB.ap(m), ps[:, :], AF.Silu), reads=[bps], writes=SGB.b(m))
            proj_fm("gb", hsrc, hbuf, cons_gb)

            def cons_q(m, ps, bps):
                P.emit("act", lambda e: e.activation(SQ.ap(m), ps[:, :], AF.Silu), reads=[bps], writes=SQ.b(m))
            proj_fm("q", hsrc, hbuf, cons_q)

            def cons_f(m, ps, bps):
                P.emit("act", lambda e: e.activation(SS.ap(m), ps[:, :], AF.Sigmoid), reads=[bps], writes=SS.b(m))
                P.emit("act", lambda e: e.activation(SMb.ap(m), ps[:, :], AF.Sigmoid, scale=-1.0), reads=[bps], writes=SMb.b(m))
            proj_fm("f", hsrc, hbuf, cons_f)

            P.lab = "B1"
            for hd in range(8):
                P.emit("act", lambda e, hd=hd: e.activation(SS.ap(hd), SS.ap(hd), AF.Ln, bias=dp(8 + hd), scale=dp(16 + hd)),
                       reads=SS.b(hd) + [b_DP], writes=SS.b(hd))
            def bscan(hd):
                P.emit("dve", lambda e: e.tensor_tensor_scan(BB.ap(hd), ONES[:, 0:1].to_broadcast([128, TT]), SS.ap(hd), 0.0, ALU.mult, ALU.add),
                       reads=SS.b(hd) + [b_ONES], writes=BB.b(hd), name="B1")
            for hd in range(4):
                bscan(hd)
            DG3 = SM2[:, 0:32].rearrange("p (h n) -> p h n", h=8)
            bref = BB.ap3()[:, :, 63::128]

            def decay_book():
                P.emit("dve", lambda e: e.tensor_tensor(DG3[:, :, 1:4], bref[:, :, 1:4], bref[:, :, 0:3], ALU.subtract),
                       reads=BB.ball(), writes=[b_SM2], name="B2")
                P.emit("dve", lambda e: e.tensor_tensor(DG3[:, :, 0], bref[:, :, 0], REM[:, :], ALU.add),
                       reads=BB.ball() + [b_REM], writes=[b_SM2], name="B2")
                P.emit("dve", lambda e: e.tensor_tensor(REM[:, :], BB.ap3()[:, :, TT - 1], bref[:, :, 3], ALU.subtract),
                       reads=BB.ball(), writes=[b_REM], name="B2")
                P.emit("act", lambda e: e.activation(SM2[:, 32:64], SM2[:, 0:32], AF.Exp), reads=[b_SM2], writes=[b_SM2], name="B2")

            def bc(hd):
                bc_out = SS.ap(hd).rearrange("p (n t) -> p n t", n=4)
                b_in = BB.ap(hd).rearrange("p (n t) -> p n t", n=4)
                b_ref = BB.ap(hd)[:, 63::128].unsqueeze(2).to_broadcast([128, 4, 128])
                P.emit("dve", lambda e, o=bc_out, i0=b_in, i1=b_ref: e.tensor_tensor(o, i0, i1, ALU.subtract),
                       reads=BB.b(hd), writes=SS.b(hd), name="B2")

            def cons_pa(m, ps, bps):
                P.emit("dve", lambda e: e.tensor_tensor(OA.ap(m), ps[:, :], GA.ap(m), ALU.mult),
                       reads=[bps] + GA.b(m), writes=OA.b(m))
                if m < 4:
                    bscan(4 + m)
                    if m == 3:
                        decay_book()
                else:
                    bc(2 * (m - 4))
                    bc(2 * (m - 4) + 1)
            proj_fm("pa", lambda k: YA.ap(k), lambda k: YA.b(k), cons_pa)

            def epen(m):
                if m % 4 != 0:
                    return
                P.lab = "B2"
                for hd in range(m, m + 4):
                    P.emit("act", lambda e, hd=hd: e.activation(EP.ap(hd), SS.ap(hd), AF.Exp, bias=dp(24 + hd)),
                           reads=SS.b(hd) + [b_DP], writes=EP.b(hd))
                    P.emit("act", lambda e, hd=hd: e.activation(EN.ap(hd), SS.ap(hd), AF.Exp, scale=-1.0),
                           reads=SS.b(hd), writes=EN.b(hd))
            def cons_gmb(m, ps, bps):
                P.emit("act", lambda e: e.activation(GB.ap(m), ps[:, :], AF.Sigmoid, bias=par(P_BM + 8 + m)),
                       reads=[bps, b_PAR], writes=GB.b(m))
            proj_fm("gmb", hsrc, hbuf, cons_gmb, before=epen)

            P.lab = "B3"
            for hd in range(8):
                P.emit("dve", lambda e, hd=hd: e.tensor_tensor(SQ.ap(hd), SQ.ap(hd), EP.ap(hd), ALU.mult),
                       reads=SQ.b(hd) + EP.b(hd), writes=SQ.b(hd))
                P.emit("dve", lambda e, hd=hd: e.tensor_tensor(SMb.ap(hd), SMb.ap(hd), EN.ap(hd), ALU.mult),
                       reads=SMb.b(hd) + EN.b(hd), writes=SMb.b(hd))
            QT, KT = SQ, SMb

            P.lab = "KTtr"
            for hd in range(8):
                pt, bpt = next_ps()
                ptb = pt[:, :].bitcast(BF16)
                for n in range(4):
                    P.emit("pe", lambda e, hd=hd, n=n, ptb=ptb: e.transpose(ptb[:, n * 128:(n + 1) * 128], KT.ap(hd, n * 128, (n + 1) * 128), IDB[:, :]),
                           reads=KT.b(hd) + [b_IDB], writes=[bpt])
                P.emit("act", lambda e, hd=hd, ptb=ptb: e.activation(KTT.ap(hd), ptb[:, 0:512], AF.Copy),
                       reads=[bpt], writes=KTT.b(hd))
            G3 = SM2[:, 32:64].rearrange("p (h n) -> p h n", h=8)

            def emit_sbf(n, half):
                hsl = slice(half * 4, half * 4 + 4)
                P.emit("dve", lambda e: e.tensor_tensor(SBF[:, hsl, :], TST[:, hsl, :], G3[:, hsl, n].unsqueeze(2).to_broadcast([128, 4, 128]), ALU.mult),
                       reads=b_TST[hsl] + [b_SM2], writes=b_SBF[hsl])
            att_rr = [0]
            pend_att = {}

            def att_group(n, half):
                P.lab = "heads%d" % n
                ps_a, bps_a = next_ps()
                for hh in range(4):
                    hd = half * 4 + hh
                    P.emit("pe", lambda e, hh=hh, hd=hd: e.matmul(ps_a[:, hh * 128:(hh + 1) * 128], KT.ap(hd, n * 128, (n + 1) * 128), QT.ap(hd, n * 128, (n + 1) * 128), start=True, stop=True),
                           reads=KT.b(hd) + QT.b(hd), writes=[bps_a])
                gi = att_rr[0] % 2
                att_rr[0] += 1
                P.emit("dve", lambda e: e.tensor_tensor(ATT4[gi][:, :, :], ps_a[:, :].rearrange("p (h t) -> p h t", h=4),
                                                        MSK[:, :].unsqueeze(1).to_broadcast([128, 4, 128]), ALU.mult),
                       reads=[bps_a, b_MSK], writes=[b_ATT4[gi]])
                pend_att[(n, half)] = gi

            def heads(n, half):
                P.lab = "heads%d" % n
                POn, b_POn = PO[n % 2], b_PO[n % 2]
                gcols = [SM2[:, 32 + hd * 4 + n:33 + hd * 4 + n] for hd in range(8)]
                if n == 0 and half == 0:
                    emit_sbf(0, 0)
                    emit_sbf(0, 1)
                    att_group(0, 0)
                nxt = (n, 1) if half == 0 else (n + 1, 0)
                if nxt[0] < 4:
                    att_group(*nxt)
                P.lab = "heads%d" % n
                gi = pend_att.pop((n, half))
                for hd in range(half * 4, half * 4 + 4):
                    hs = slice(hd * 128, (hd + 1) * 128)
                    P.emit("pe", lambda e, hd=hd, hs=hs: e.matmul(POn[:, hs], ATT4[gi][:, hd % 4, :], VV.ap(n, hs.start, hs.stop), start=True, stop=False),
                           reads=[b_ATT4[gi]] + VV.b(n), writes=[b_POn[hd // 4]])
                    P.emit("pe", lambda e, hd=hd, hs=hs: e.matmul(POn[:, hs], QT.ap(hd, n * 128, (n + 1) * 128), SBF[:, hd, :], start=False, stop=True),
                           reads=QT.b(hd) + [b_SBF[hd]], writes=[b_POn[hd // 4]])
                    ps_k, bps_k = next_ps()
                    P.emit("pe", lambda e, ps_k=ps_k, hd=hd, hs=hs: e.matmul(ps_k[:, 0:128], KTT.ap(hd, n * 128, (n + 1) * 128), VV.ap(n, hs.start, hs.stop), start=True, stop=True),
                           reads=KTT.b(hd) + VV.b(n), writes=[bps_k])
                    P.emit("dve", lambda e, ps_k=ps_k, hd=hd, gcol=gcols[hd]: e.scalar_tensor_tensor(TST[:, hd, :], TST[:, hd, :], gcol, ps_k[:, 0:128], ALU.mult, ALU.add),
                           reads=[b_TST[hd], b_SM2, bps_k], writes=[b_TST[hd]])
                smc = 64 + 24 * (n % 2)
                for h2 in range(half * 4, half * 4 + 4):
                    P.emit("act", lambda e, h2=h2: e.activation(SQJ[:, h2 * 128:(h2 + 1) * 128], POn[:, h2 * 128:(h2 + 1) * 128], AF.Square,
                                                                accum_out=SMALL[:, smc + h2:smc + h2 + 1]),
                           reads=[b_POn[half]], writes=[b_SQJ, b_SMhg2[n % 2]])
                if n + 1 < 4:
                    emit_sbf(n + 1, half)

            def stats(n):
                P.lab = "stats%d" % n
                POn, b_POn = PO[n % 2], b_PO[n % 2]
                smc = 64 + 24 * (n % 2)
                b_sm = b_SMhg2[n % 2]
                P.emit("act", lambda e: e.activation(SMALL[:, smc + 8:smc + 16], SMALL[:, smc:smc + 8], AF.Ln, bias=EPS, scale=1.0 / 128),
                       reads=[b_sm], writes=[b_sm])
                P.emit("act", lambda e: e.activation(SMALL[:, smc + 16:smc + 24], SMALL[:, smc + 8:smc + 16], AF.Exp, scale=-0.5),
                       reads=[b_sm], writes=[b_sm])
                P.emit("dve", lambda e: e.tensor_tensor(ONB[:, :].rearrange("p (h v) -> p h v", h=8), POn[:, :].rearrange("p (h v) -> p h v", h=8),
                                                        SMALL[:, smc + 16:smc + 24].unsqueeze(2).to_broadcast([128, 8, 128]), ALU.mult),
                       reads=b_POn + [b_sm], writes=[b_ONB])

            def trans(n):
                P.lab = "trans%d" % n
                cs = slice(n * 128, (n + 1) * 128)
                pt, bpt = next_ps()
                ptb = pt[:, :].bitcast(BF16)
                for hd in range(8):
                    P.emit("pe", lambda e, hd=hd, ptb=ptb: e.transpose(ptb[:, hd * 128:(hd + 1) * 128], ONB[:, hd * 128:(hd + 1) * 128], IDB[:, :]),
                           reads=[b_ONB, b_IDB], writes=[bpt])
                if n == 3:
                    P.emit("dve", lambda e, cs=cs, ptb=ptb: e.scalar_tensor_tensor(YB.ap3(cs.start, cs.stop), ptb[:, :].rearrange("p (h t) -> p h t", h=8), par(P_HG),
                                                                                  SGB.ap3(cs.start, cs.stop), ALU.mult, ALU.mult),
                           reads=[bpt, b_PAR] + SGB.ball(), writes=YB.ball())
                    return
                P.emit("act", lambda e, cs=cs, ptb=ptb: e.activation(YB.ap3(cs.start, cs.stop), ptb[:, :].rearrange("p (h t) -> p h t", h=8), AF.Copy, scale=par(P_HG)),
                       reads=[bpt, b_PAR], writes=YB.ball())
                P.emit("pool", lambda e, cs=cs: e.tensor_tensor(YB.ap3(cs.start, cs.stop), YB.ap3(cs.start, cs.stop), SGB.ap3(cs.start, cs.stop), ALU.mult),
                       reads=YB.ball() + SGB.ball(), writes=YB.ball())

            heads(0, 0)
            heads(0, 1)
            for n in range(1, 4):
                heads(n, 0)
                stats(n - 1)
                heads(n, 1)
                trans(n - 1)
            stats(3)
            trans(3)

        def merge_mm(tile):
            XTt, b_XTt = XT[tile % 2], b_XT[tile % 2]
            def cons_pb(m, ps, bps):
                P.emit("dve", lambda e: e.tensor_tensor(MIX.ap(m), ps[:, :], GB.ap(m), ALU.mult),
                       reads=[bps] + GB.b(m), writes=MIX.b(m))
                P.emit("dve", lambda e: e.tensor_tensor(MIX.ap(m), MIX.ap(m), OA.ap(m), ALU.add),
                       reads=MIX.b(m) + OA.b(m), writes=MIX.b(m))
            proj_fm("pb", lambda k: YB.ap(k), lambda k: YB.b(k), cons_pb)

        def merge_wo(tile):
            XTt, b_XTt = XT[tile % 2], b_XT[tile % 2]

            def cons_wo(j, half, ps, bps):
                sl = slice(half * 512, half * 512 + 512)
                P.emit("dve", lambda e: e.tensor_tensor(XTt[:, j, sl], ps[:, :], XTt[:, j, sl], ALU.add),
                       reads=[bps, b_XTt[j]], writes=[b_XTt[j]])
            proj_tm("wo", lambda k, j: MIX.ap(k, j * 128, (j + 1) * 128), lambda k: MIX.b(k), cons_wo, j_outer=(tile == NB * NT - 1))

        s_fin = [sem("st_f%d" % j) for j in range(4)]
        b_SMfj = [Buf("SMfin%d" % j) for j in range(4)]

        def final_split(tile):
            P.lab = "final"
            XTt, b_XTt = XT[tile % 2], b_XT[tile % 2]
            for j in range(4):
                P.emit("act", lambda e, j=j: e.activation(SQJ[:, :], XTt[:, j, :], AF.Square, accum_out=SMALL[:, 40 + j:41 + j]),
                       reads=[b_XTt[j]], writes=[b_SQJ, b_SMfj[j]])
                P.emit("act", lambda e, j=j: e.activation(SMALL[:, 44 + j:45 + j], SMALL[:, 40 + j:41 + j], AF.Ln, bias=EPS, scale=1.0 / D),
                       reads=[b_SMfj[j]], writes=[b_SMfj[j]])
                P.emit("act", lambda e, j=j: e.activation(SMALL[:, 48 + j:49 + j], SMALL[:, 44 + j:45 + j], AF.Exp, scale=-0.5),
                       reads=[b_SMfj[j]], writes=[b_SMfj[j]])
                P.emit("dve", lambda e, j=j: e.scalar_tensor_tensor(XTt[:, j, :], XTt[:, j, :], SMALL[:, 48 + j:49 + j], FNG[:, :], ALU.mult, ALU.mult),
                       reads=[b_XTt[j], b_SMfj[j], b_FNG], writes=[b_XTt[j]])
                op = P.emit("sp", lambda e, tile=tile, j=j: e.dma_start(out=yv[tile][:, j, :], in_=XTt[:, j, :]), reads=[b_XTt[j]], dma_sem=s_fin[j])
                out_ops.append(op)

        def final(tile):
            P.lab = "final"
            XTt, b_XTt = XT[tile % 2], b_XT[tile % 2]
            for j in range(4):
                P.emit("act", lambda e, j=j: e.activation(SQJ[:, :], XTt[:, j, :], AF.Square, accum_out=SMALL[:, 40 + j:41 + j]),
                       reads=[b_XTt[j]], writes=[b_SQJ, b_SMfin])
            P.emit("act", lambda e: e.activation(SMALL[:, 44:48], SMALL[:, 40:44], AF.Ln, bias=EPS, scale=1.0 / D),
                   reads=[b_SMfin], writes=[b_SMfin])
            P.emit("act", lambda e: e.activation(SMALL[:, 48:52], SMALL[:, 44:48], AF.Exp, scale=-0.5),
                   reads=[b_SMfin], writes=[b_SMfin])
            for j in range(4):
                P.emit("dve", lambda e, j=j: e.scalar_tensor_tensor(XTt[:, j, :], XTt[:, j, :], SMALL[:, 48 + j:49 + j], FNG[:, :], ALU.mult, ALU.mult),
                       reads=[b_XTt[j], b_SMfin, b_FNG], writes=[b_XTt[j]])
            op = P.emit("sp", lambda e, tile=tile: e.dma_start(out=yv[tile], in_=XTt[:, :, :]), reads=b_XTt, dma_sem=s_st[tile % 2])
            out_ops.append(op)


        ntiles = NB * NT
        s0_stats(0)
        s0_tr(0)
        for tile in range(ntiles):
            it = tile % NT
            if tile == 0:
                s0_load(1)
                P.emit("sp", lambda e: e.dma_start(out=FNG[:, :], in_=dram["fng"]), writes=[b_FNG], dma_sem=s_c[1])
                front(tile, it)
            else:
                def after_xa(tile=tile):
                    final(tile - 1)
                    if tile + 1 < ntiles:
                        s0_load(tile + 1)
                front(tile, it, after_xa)
            merge_mm(tile)
            if tile + 1 < ntiles:
                s0_stats(tile + 1)
            merge_wo(tile)
            if tile + 1 < ntiles:
                s0_tr(tile + 1)
        final_split(ntiles - 1)

        block = es.enter_context(nc.Block())
        P.finalize(nc, block, engine_sems, out_ops)
        global LAST_PROG
        LAST_PROG = P


def _consts():
    c = np.zeros((128, 256), np.float32)
    c[:, 0:128] = np.eye(128, dtype=np.float32)
    s = np.arange(128)[:, None]
    t = np.arange(128)[None, :]
    c[:, 128:256] = (s <= t).astype(np.float32)
    return c


def _pack_params(inp):
    def fm(v, n):
        return np.ascontiguousarray(np.asarray(v, np.float32).reshape(n, 128).T)
    p = np.zeros((128, NPAR), np.float32)
    p[:, P_NG:P_NG + 8] = fm(inp["norm_g"][0], 8)
    for k in range(4):
        p[:, P_CW + 8 * k:P_CW + 8 * k + 8] = fm(inp["conv_w"][0, k], 8)
    p[:, P_CB:P_CB + 8] = fm(inp["conv_b"][0], 8)
    p[:, P_BX:P_BX + 8] = fm(inp["rg_bx"][0].reshape(-1), 8)
    p[:, P_BA:P_BA + 8] = fm(inp["rg_ba"][0].reshape(-1), 8)
    p[:, P_LAM:P_LAM + 8] = fm(inp["rg_lambda"][0], 8)
    p[:, P_L0:P_L0 + 8] = fm(inp["hg_lb_logits"][0], 8)
    p[:, P_L1:P_L1 + 8] = fm(inp["hg_lb_logits"][1], 8)
    p[:, P_HG:P_HG + 1] = fm(inp["hg_norm_g"][0], 1)
    p[:, P_BM:P_BM + 16] = fm(inp["b_merge"][0], 16)
    return p


def _pack_weights(inp):
    w_in = np.asarray(inp["w_in"], np.float32)[0]
    extra = {"pa": np.asarray(inp["proj_a"], np.float32)[0], "pb": np.asarray(inp["proj_b"], np.float32)[0],
             "wo": np.asarray(inp["w_out"], np.float32)[0]}
    out = np.empty((2 * len(WORDER), 128, 8 * 512), np.float32)
    for i, name in enumerate(WORDER):
        W = extra[name] if name in extra else w_in[:, W_IN_COLS[name]:W_IN_COLS[name] + 1024]
        Wr = W.reshape(8, 128, 2, 512)
        out[2 * i:2 * i + 2] = Wr.transpose(2, 1, 0, 3).reshape(2, 128, 8 * 512)
    return out


def make_in_maps(inp, n_cores=8):
    x = np.asarray(inp["x"], np.float32)
    shared = {
        "wpk": _pack_weights(inp),
        "rg_wx": np.ascontiguousarray(np.asarray(inp["rg_wx"], np.float32)[0].transpose(1, 0, 2)),
        "rg_wa": np.ascontiguousarray(np.asarray(inp["rg_wa"], np.float32)[0].transpose(1, 0, 2)),
        "params": _pack_params(inp),
        "fng": np.ascontiguousarray(np.broadcast_to(np.asarray(inp["final_norm_g"], np.float32)[None, :], (128, D))),
        "consts": _consts(),
    }
    maps = []
    for c in range(n_cores):
        m = dict(shared)
        m["x"] = np.ascontiguousarray(x[NB * c:NB * (c + 1)].reshape(NB * SEQ, D))
        maps.append(m)
    return maps


def kernel(**inputs):
    nc = build_nc()
    in_maps = make_in_maps(inputs)
    res = run_bass_kernel_spmd(nc, in_maps, core_ids=list(range(8)))
    out = np.concatenate([np.asarray(r["y"]).reshape(NB, SEQ, D) for r in res.results], axis=0)
    return out.astype(np.float32)
```

```python
import numpy as np
import concourse.bass as bass
import concourse.mybir as mybir
from concourse.bass_utils import run_bass_kernel_spmd

F32 = mybir.dt.float32
BF16 = mybir.dt.bfloat16
AF = mybir.ActivationFunctionType
ALU = mybir.AluOpType
AX = mybir.AxisListType

D = 1024
SEQ = 2048
NB = 2
TT = 512
NT = SEQ // TT
NCH = 8
IN_COLS = 8192
EPS = 1e-6
LRU_C = 8.0
HG_SCALE = 128.0 ** -0.5

P_NG = 0
P_CW = 8
P_CB = 40
P_BX = 48
P_BA = 56
P_LAM = 64
P_L0 = 72
P_L1 = 80
P_HG = 88
P_BM = 89
NPAR = 105

WORDER = ["xa", "ga", "gma", "i", "gb", "q", "f", "pa", "gmb", "pb", "wo"]
W_IN_COLS = {"xa": 0, "ga": 1024, "q": 2048, "f": 3072, "i": 4096, "gb": 5120, "gma": 6144, "gmb": 7168}

ENGINES = ("pe", "act", "dve", "pool", "sp")
LAST_PROG = None


class Buf:
    __slots__ = ("name", "last_w", "readers")

    def __init__(self, name):
        self.name = name
        self.last_w = None
        self.readers = []


class Op:
    __slots__ = ("eng", "fn", "deps", "dma_sem", "sig", "need_sig", "name")

    def __init__(self, eng, fn, dma_sem, name):
        self.eng = eng
        self.fn = fn
        self.deps = []
        self.dma_sem = dma_sem
        self.sig = None
        self.need_sig = False
        self.name = name


class Prog:
    def __init__(self):
        self.ops = {e: [] for e in ENGINES}
        self.final_waits = []
        self.lab = ""

    def emit(self, eng, fn, reads=(), writes=(), dma_sem=None, name=""):
        op = Op(eng, fn, dma_sem, name or self.lab)
        deps = {}
        for b in reads:
            w = b.last_w
            if w is not None:
                deps[id(w)] = w
        for b in writes:
            w = b.last_w
            if w is not None:
                deps[id(w)] = w
            for r in b.readers:
                if r is not op:
                    deps[id(r)] = r
        for w in deps.values():
            if w.eng == "pe" and eng == "pe":
                continue
            op.deps.append(w)
            w.need_sig = True
        for b in reads:
            b.readers.append(op)
        for b in writes:
            b.last_w = op
            b.readers = []
        if dma_sem is not None:
            op.need_sig = True
        self.ops[eng].append(op)
        return op

    def finalize(self, nc, block, engine_sems, out_ops):
        counts = {}
        for e in ENGINES:
            for op in self.ops[e]:
                if not op.need_sig:
                    continue
                if op.dma_sem is not None:
                    sem, inc = op.dma_sem, 16
                else:
                    sem, inc = engine_sems[e], 1
                counts[id(sem)] = counts.get(id(sem), 0) + inc
                op.sig = (sem, counts[id(sem)], inc)
        self.sigtable = {e: [(op.sig[1], op.name) for op in self.ops[e] if op.sig is not None and op.dma_sem is None] for e in ENGINES}
        finals = {}
        for op in out_ops:
            sem, val, _ = op.sig
            if id(sem) not in finals or finals[id(sem)][1] < val:
                finals[id(sem)] = (sem, val)

        def run(e, engobj):
            waited = {}
            for op in self.ops[e]:
                need = {}
                for d in op.deps:
                    sem, val, _ = d.sig
                    if need.get(id(sem), (None, 0))[1] < val:
                        need[id(sem)] = (sem, val)
                for k, (sem, val) in need.items():
                    if waited.get(k, 0) >= val:
                        continue
                    engobj.wait_ge(sem, val)
                    waited[k] = val
                ins = op.fn(engobj)
                if op.sig is not None:
                    ins.then_inc(op.sig[0], op.sig[2])
            if e == "sp":
                for sem, val in finals.values():
                    engobj.wait_ge(sem, val)

        @block.tensor
        def _(eng):
            run("pe", eng)

        @block.scalar
        def _(eng):
            run("act", eng)

        @block.vector
        def _(eng):
            run("dve", eng)

        @block.gpsimd
        def _(eng):
            run("pool", eng)

        @block.sync
        def _(eng):
            run("sp", eng)


class Region:
    def __init__(self, nc, es, name, nbytes, gran=1024):
        assert nbytes % 4 == 0
        self.GRAN = gran
        self.t = es.enter_context(nc.sbuf_tensor(name, [128, nbytes // 4], F32))
        self.nbytes = nbytes
        self.bufs_ = [Buf("%s.%d" % (name, i)) for i in range((nbytes + self.GRAN - 1) // self.GRAN)]
        self.f32 = self.t[:, :]
        self.b16 = self.t[:, :].bitcast(BF16)

    def bufs(self, lo, hi):
        return self.bufs_[lo // self.GRAN:(hi + self.GRAN - 1) // self.GRAN]


class Stream:
    def __init__(self, reg, dt, base, stride, nch, width):
        self.reg, self.dt, self.base, self.stride, self.nch, self.width = reg, dt, base, stride, nch, width
        self.es = 4 if dt == F32 else 2
        self.flat = reg.f32 if dt == F32 else reg.b16
        assert (base + nch * stride) * self.es <= reg.nbytes + 0, (base, nch, stride, reg.nbytes)

    def ap(self, c, lo=0, hi=None):
        hi = self.width if hi is None else hi
        o = self.base + c * self.stride
        return self.flat[:, o + lo:o + hi]

    def b(self, c, lo=0, hi=None):
        hi = self.width if hi is None else hi
        o = self.base + c * self.stride
        return self.reg.bufs((o + lo) * self.es, (o + hi) * self.es)

    def ap3(self, lo=0, hi=None):
        hi = self.width if hi is None else hi
        v = self.flat[:, self.base:self.base + self.nch * self.stride].rearrange("p (c t) -> p c t", c=self.nch)
        return v[:, :, lo:hi]

    def ball(self):
        return self.reg.bufs(self.base * self.es, (self.base + self.nch * self.stride) * self.es)


def build_nc(stage=99, debug=None):
    nc = bass.Bass("TRN2", target_bir_lowering=False)
    dram = {}
    dram["x"] = nc.dram_tensor("x", [NB * SEQ, D], F32, kind="ExternalInput").ap()
    dram["wpk"] = nc.dram_tensor("wpk", [2 * len(WORDER), 128, 8 * 512], F32, kind="ExternalInput").ap()
    dram["rg_wx"] = nc.dram_tensor("rg_wx", [128, 8, 128], F32, kind="ExternalInput").ap()
    dram["rg_wa"] = nc.dram_tensor("rg_wa", [128, 8, 128], F32, kind="ExternalInput").ap()
    dram["params"] = nc.dram_tensor("params", [128, NPAR], F32, kind="ExternalInput").ap()
    dram["fng"] = nc.dram_tensor("fng", [128, D], F32, kind="ExternalInput").ap()
    dram["consts"] = nc.dram_tensor("consts", [128, 256], F32, kind="ExternalInput").ap()
    dram["y"] = nc.dram_tensor("y", [NB * SEQ, D], F32, kind="ExternalOutput").ap()
    dram["wbf"] = nc.dram_tensor("wbf", [22, 128, 8 * 512], BF16, kind="Internal").ap()
    dbg = None
    if debug is not None:
        dbg = nc.dram_tensor("dbg", list(debug), F32, kind="ExternalOutput").ap()
    _build(nc, dram, stage, dbg)
    return nc


def _build(nc, dram, stage, dbg):
    from contextlib import ExitStack
    P = Prog()
    with ExitStack() as es:
        def sb(name, shape, dt):
            return es.enter_context(nc.sbuf_tensor(name, shape, dt))

        def sem(name):
            return es.enter_context(nc.semaphore(name))

        engine_sems = {e: sem("s_" + e) for e in ENGINES}

        XT = [sb("XT%d" % i, [128, 4, D], F32) for i in range(2)]
        HT = sb("HT", [128, NCH, TT], BF16)
        SQJ = sb("SQJ", [128, D], BF16)
        PAR = sb("PAR", [128, NPAR], F32)
        DP = sb("DP", [128, 64], F32)
        FNG = sb("FNG", [128, D], F32)
        IDB = sb("IDB", [128, 128], BF16)
        ONES = sb("ONES", [128, 1], F32)
        SMALL = sb("SMALL", [128, 128], F32)
        SM2 = sb("SM2", [128, 64], F32)
        REM = sb("REM", [128, 8], F32)
        HST = sb("HST", [128, 8], F32)
        HALO = sb("HALO", [128, 8, 3], F32)
        TST = sb("TST", [128, 8, 128], F32)
        SBF = sb("SBF", [128, 8, 128], BF16)
        ATT4 = [sb("ATT4_%d" % i, [128, 4, 128], BF16) for i in range(2)]
        MSK = sb("MSK", [128, 128], F32)
        ONB = sb("ONB", [128, D], BF16)
        WX = sb("WXb", [128, 8, 128], BF16)
        WA = sb("WAb", [128, 8, 128], BF16)
        NWS = 4
        WS = [sb("WS%d" % i, [128, 8, 512], BF16) for i in range(NWS)]
        b_XT = [[Buf("XT%d_%d" % (i, j)) for j in range(4)] for i in range(2)]
        b_HT = [Buf("HT%d" % c) for c in range(NCH)]
        b_SQJ, b_PAR, b_DP, b_FNG = Buf("SQJ"), Buf("PAR"), Buf("DP"), Buf("FNG")
        b_IDB, b_MSK, b_ONES, b_SM2 = Buf("IDB"), Buf("MSK"), Buf("ONES"), Buf("SM2")
        b_SMs0, b_SMhg, b_SMfin = Buf("SMs0"), Buf("SMhg"), Buf("SMfin")
        b_SMhg2 = [Buf("SMhg0"), Buf("SMhg1")]
        b_REM, b_HST, b_HALO = Buf("REM"), Buf("HST"), Buf("HALO")
        b_TST = [Buf("TST%d" % h) for h in range(8)]
        b_SBF = [Buf("SBF%d" % h) for h in range(8)]
        b_ATT4 = [Buf("ATT4_%d" % i) for i in range(2)]
        b_ONB = Buf("ONB")
        b_WX, b_WA = Buf("WX"), Buf("WA")
        b_WS = [Buf("WS%d" % i) for i in range(4)]

        XW = 516
        R1 = Region(nc, es, "R1", 8 * XW * 4, gran=XW * 4)
        R2 = Region(nc, es, "R2", 8 * TT * 4)
        R3 = Region(nc, es, "R3", 8 * TT * 4)
        R4 = Region(nc, es, "R4", 8 * TT * 4)
        Q1 = Region(nc, es, "Q1", 8 * TT * 2)
        Q2 = Region(nc, es, "Q2", 8 * TT * 2)
        Q3 = Region(nc, es, "Q3", 8 * TT * 2)
        Q4 = Region(nc, es, "Q4", 8 * TT * 2)
        Q5 = Region(nc, es, "Q5", 8 * TT * 2)
        Q6 = Region(nc, es, "Q6", 8 * TT * 2)
        XA = Stream(R1, F32, 0, XW, 8, 3 + TT)
        XC = Stream(R2, F32, 0, TT, 8, TT)
        AA = Stream(R3, F32, 0, TT, 8, TT)
        MM = Stream(R4, F32, 0, TT, 8, TT)
        XCb = Stream(Q1, BF16, 0, TT, 8, TT)
        GI = Stream(Q2, BF16, 0, TT, 8, TT)
        SG = Stream(Q3, BF16, 0, TT, 8, TT)
        YA = Stream(Q4, BF16, 0, TT, 8, TT)
        GA = Stream(Q6, BF16, 0, TT, 8, TT)
        OA = Stream(Q5, BF16, 0, TT, 8, TT)
        SS = Stream(R2, F32, 0, TT, 8, TT)
        BB = Stream(R3, F32, 0, TT, 8, TT)
        SMb = Stream(Q2, BF16, 0, TT, 8, TT)
        EP = Stream(R4, BF16, 8 * TT, TT, 8, TT)
        EN = Stream(Q3, BF16, 0, TT, 8, TT)
        SQ = Stream(Q1, BF16, 0, TT, 8, TT)
        KTT = Stream(R4, BF16, 0, TT, 8, TT)
        VV = Stream(R1, BF16, 8 * TT, D, 4, D)
        SGB = Stream(R1, BF16, 0, TT, 8, TT)
        YB = Stream(R4, BF16, 8 * TT, TT, 8, TT)
        ON = Stream(Q6, BF16, 0, D, 4, D)
        XN = ON
        GB = Stream(Q6, BF16, 0, TT, 8, TT)
        MIX = Stream(Q3, BF16, 0, TT, 8, TT)

        PO = [es.enter_context(nc.psum_tensor("PO%d" % i, [128, 1024], F32)) for i in range(2)]
        b_PO = [[Buf("PO%d_%d" % (i, h)) for h in range(2)] for i in range(2)]
        NROT = 4
        PS = [es.enter_context(nc.psum_tensor("PS%d" % i, [128, 512], F32)) for i in range(NROT)]
        b_PS = [Buf("PS%d" % i) for i in range(NROT)]
        ps_rr = [0]
        ROT = [(PS[i][:, :], b_PS[i]) for i in range(NROT)] + \
              [(PO[i][:, h * 512:(h + 1) * 512], b_PO[i][h]) for i in range(2) for h in range(2)]
        rot_lim = [8]

        def next_ps():
            i = ps_rr[0] % rot_lim[0]
            ps_rr[0] = (i + 1) % rot_lim[0]
            return ROT[i]

        s_ld = [sem("ld_x0"), sem("ld_x1")]
        s_c = [sem("ld_c%d" % i) for i in range(5)]
        s_st = [sem("st_y0"), sem("st_y1")]
        s_w = [sem("ld_w%d" % i) for i in range(4)]
        out_ops = []

        P.emit("sp", lambda e: e.dma_start(out=XT[0][:, :, :], in_=dram["x"].rearrange("(n j p) d -> n p j d", j=4, p=128)[0]),
               writes=b_XT[0], dma_sem=s_ld[0])
        P.emit("sp", lambda e: e.dma_start(out=PAR[:, :], in_=dram["params"]), writes=[b_PAR], dma_sem=s_c[0])
        CST = Q6.f32[:, 0:256]
        b_CST = Q6.bufs(0, 1024)
        P.emit("sp", lambda e: e.dma_start(out=CST, in_=dram["consts"]), writes=b_CST, dma_sem=s_c[2])
        P.emit("pool", lambda e: e.dma_start(out=WX[:, :, :], in_=dram["rg_wx"]), writes=[b_WX], dma_sem=s_c[3])
        P.emit("pool", lambda e: e.dma_start(out=WA[:, :, :], in_=dram["rg_wa"]), writes=[b_WA], dma_sem=s_c[4])
        P.emit("dve", lambda e: e.tensor_copy(IDB[:, :], CST[:, 0:128]), reads=b_CST, writes=[b_IDB])
        P.emit("dve", lambda e: e.tensor_copy(MSK[:, :], CST[:, 128:256]), reads=b_CST, writes=[b_MSK])
        P.emit("dve", lambda e: e.memset(ONES[:, :], 1.0), writes=[b_ONES])
        P.emit("act", lambda e: e.activation(DP[:, 32:40], PAR[:, P_LAM:P_LAM + 8], AF.Exp, scale=-1.0), reads=[b_PAR], writes=[b_DP])
        P.emit("act", lambda e: e.activation(DP[:, 40:48], DP[:, 32:40], AF.Ln, bias=1.0), reads=[b_DP], writes=[b_DP])
        P.emit("dve", lambda e: e.tensor_scalar(DP[:, 0:8], DP[:, 40:48], -LRU_C, None, ALU.mult), reads=[b_DP], writes=[b_DP])
        P.emit("dve", lambda e: e.tensor_tensor(DP[:, 48:56], PAR[:, P_L0:P_L0 + 8], PAR[:, P_L1:P_L1 + 8], ALU.subtract), reads=[b_PAR], writes=[b_DP])
        P.emit("act", lambda e: e.activation(DP[:, 8:16], DP[:, 48:56], AF.Sigmoid), reads=[b_DP], writes=[b_DP])
        P.emit("dve", lambda e: e.tensor_scalar(DP[:, 16:24], DP[:, 8:16], -1.0, 1.0, ALU.mult, ALU.add), reads=[b_DP], writes=[b_DP])
        P.emit("act", lambda e: e.activation(DP[:, 24:32], DP[:, 16:24], AF.Ln, scale=HG_SCALE), reads=[b_DP], writes=[b_DP])

        xv = dram["x"].rearrange("(n j p) d -> n p j d", j=4, p=128)
        yv = dram["y"].rearrange("(n j p) d -> n p j d", j=4, p=128)

        NHC = 2 * len(WORDER)
        total_w = NB * NT * NHC
        wstate = {"n": 0, "issued": 0}
        b_WBF = [Buf("WBF%d" % i) for i in range(NHC)]
        s_wst = [sem("st_w%d" % i) for i in range(NWS)]
        PREF = NWS - 1
        NHC_A = 12

        def issue_w(idx):
            if idx >= total_w:
                return
            k = idx % NHC
            name, h = WORDER[k // 2], k % 2
            s = idx % NWS
            tile_i = idx // NHC
            cached = (tile_i >= 2) or (tile_i == 1 and k < NHC_A)
            if not cached:
                gate = b_XT[0] if idx <= PREF else []
                P.emit("pool", lambda e, s=s, k=k: e.dma_start(out=WS[s][:, :, :].rearrange("p k n -> p (k n)"), in_=dram["wpk"][k]),
                       reads=gate, writes=[b_WS[s]], dma_sem=s_w[s], name="ldw_" + name)
                if (tile_i == 0 and k < NHC_A) or (tile_i == 1 and k >= NHC_A):
                    P.emit("sp", lambda e, s=s, k=k: e.dma_start(out=dram["wbf"][k], in_=WS[s][:, :, :].rearrange("p k n -> p (k n)")),
                           reads=[b_WS[s]], writes=[b_WBF[k]], dma_sem=s_wst[s], name="stw_" + name)
            else:
                P.emit("pool", lambda e, s=s, k=k: e.dma_start(out=WS[s][:, :, :].rearrange("p k n -> p (k n)"), in_=dram["wbf"][k]),
                       reads=[b_WBF[k]], writes=[b_WS[s]], dma_sem=s_w[s], name="ldwb_" + name)

        def next_w(name, h):
            idx = wstate["n"]
            assert WORDER[(idx % NHC) // 2] == name and idx % 2 == h, (name, h, idx)
            while wstate["issued"] <= min(idx + PREF, total_w - 1):
                issue_w(wstate["issued"])
                wstate["issued"] += 1
            wstate["n"] = idx + 1
            return WS[idx % NWS], b_WS[idx % NWS]

        def proj_fm(wname, src, src_bufs, consume, before=None):
            for m in range(8):
                if m % 4 == 0:
                    W, bW = next_w(wname, m // 4)
                if before is not None:
                    before(m)
                P.lab = wname
                ps, bps = next_ps()
                mm = m % 4
                for k in range(8):
                    P.emit("pe", lambda e, W=W, ps=ps, k=k, mm=mm: e.matmul(ps[:, :], W[:, k, mm * 128:(mm + 1) * 128], src(k), start=(k == 0), stop=(k == 7)),
                           reads=[bW] + src_bufs(k), writes=[bps])
                consume(m, ps, bps)

        def proj_tm(wname, src, src_bufs, consume):
            for half in range(2):
                W, bW = next_w(wname, half)
                P.lab = wname
                for j in range(4):
                    ps, bps = next_ps()
                    for k in range(8):
                        P.emit("pe", lambda e, W=W, ps=ps, k=k, j=j: e.matmul(ps[:, :], src(k, j), W[:, k, :], start=(k == 0), stop=(k == 7)),
                               reads=[bW] + src_bufs(k), writes=[bps])
                    consume(j, half, ps, bps)

        par = lambda col: PAR[:, col:col + 1]
        dp = lambda col: DP[:, col:col + 1]

        hsrc = lambda k: HT[:, k, :]
        hbuf = lambda k: [b_HT[k]]
        hsrc_tm = lambda k, j: HT[:, k, j * 128:(j + 1) * 128]

        def s0_load(tile):
            XTt, b_XTt = XT[tile % 2], b_XT[tile % 2]
            P.emit("sp", lambda e, tile=tile: e.dma_start(out=XTt[:, :, :], in_=xv[tile]),
                   writes=b_XTt, dma_sem=s_ld[tile % 2])

        def s0_stats(tile):
            P.lab = "s0_stats"
            XTt, b_XTt = XT[tile % 2], b_XT[tile % 2]
            for j in range(4):
                P.emit("act", lambda e, j=j: e.activation(SQJ[:, :], XTt[:, j, :], AF.Square, accum_out=SMALL[:, j:j + 1]),
                       reads=[b_XTt[j]], writes=[b_SQJ, b_SMs0])
            P.emit("act", lambda e: e.activation(SMALL[:, 4:8], SMALL[:, 0:4], AF.Ln, bias=EPS, scale=1.0 / D),
                   reads=[b_SMs0], writes=[b_SMs0])
            P.emit("act", lambda e: e.activation(SMALL[:, 8:12], SMALL[:, 4:8], AF.Exp, scale=-0.5),
                   reads=[b_SMs0], writes=[b_SMs0])
            for j in range(4):
                P.emit("dve", lambda e, j=j: e.tensor_scalar(XN.ap(j), XTt[:, j, :], SMALL[:, 8 + j:9 + j], None, ALU.mult),
                       reads=[b_XTt[j], b_SMs0], writes=XN.b(j))

        def s0_tr(tile):
            P.lab = "s0_tr"
            for c in range(NCH):
                pt, bpt = next_ps()
                ptb = pt[:, :].bitcast(BF16)
                for j in range(4):
                    P.emit("pe", lambda e, c=c, j=j, ptb=ptb: e.transpose(ptb[:, j * 128:(j + 1) * 128], XN.ap(j, c * 128, (c + 1) * 128), IDB[:, :]),
                           reads=XN.b(j) + [b_IDB], writes=[bpt])
                P.emit("act", lambda e, c=c, ptb=ptb: e.activation(HT[:, c, :], ptb[:, 0:512], AF.Copy, scale=par(P_NG + c)),
                       reads=[bpt, b_PAR], writes=[b_HT[c]])

        def front(tile, it, after_xa=None):
            if it == 0:
                P.emit("dve", lambda e: e.memset(XA.ap3(0, 3), 0.0), writes=XA.ball())
                P.emit("dve", lambda e: e.memset(HST[:, :], 0.0), writes=[b_HST])
                P.emit("dve", lambda e: e.memset(REM[:, :], 0.0), writes=[b_REM])
                P.emit("dve", lambda e: e.memset(TST[:, :, :], 0.0), writes=b_TST)
            else:
                P.emit("dve", lambda e: e.tensor_copy(XA.ap3(0, 3), HALO[:, :, :]), reads=[b_HALO], writes=XA.ball())

            def cast_xc(m):
                P.emit("dve", lambda e: e.tensor_copy(XCb.ap(m), XC.ap(m)), reads=XC.b(m), writes=XCb.b(m))

            def cons_xa(m, ps, bps):
                P.emit("act", lambda e: e.activation(XA.ap(m, 3, 3 + TT), ps[:, :], AF.Copy), reads=[bps], writes=XA.b(m))
                P.emit("dve", lambda e: e.tensor_scalar(XC.ap(m), XA.ap(m, 0, TT), par(P_CW + m), par(P_CB + m), ALU.mult, ALU.add),
                       reads=XA.b(m) + [b_PAR], writes=XC.b(m))
                for k in range(1, 4):
                    P.emit("dve", lambda e, k=k: e.scalar_tensor_tensor(XC.ap(m), XA.ap(m, k, k + TT), par(P_CW + 8 * k + m), XC.ap(m), ALU.mult, ALU.add),
                           reads=XA.b(m) + XC.b(m) + [b_PAR], writes=XC.b(m))
            proj_fm("xa", hsrc, hbuf, cons_xa)
            P.emit("dve", lambda e: e.tensor_copy(HALO[:, :, :], XA.ap3(TT, TT + 3)), reads=XA.ball(), writes=[b_HALO])
            if after_xa is not None:
                after_xa()

            def cons_ga(m, ps, bps):
                P.emit("act", lambda e: e.activation(SG.ap(m), ps[:, :], AF.Silu), reads=[bps], writes=SG.b(m))
            proj_fm("ga", hsrc, hbuf, cons_ga)
            P.lab = "cast"
            for m in range(8):
                cast_xc(m)

            def cons_gma(m, ps, bps):
                P.emit("act", lambda e: e.activation(GA.ap(m), ps[:, :], AF.Sigmoid, bias=par(P_BM + m)),
                       reads=[bps, b_PAR], writes=GA.b(m))
            proj_fm("gma", hsrc, hbuf, cons_gma)

            P.lab = "bd"
            for m in range(8):
                ps, bps = next_ps()
                P.emit("pe", lambda e, ps=ps, m=m: e.matmul(ps[:, :], WX[:, m, :], XCb.ap(m), start=True, stop=True),
                       reads=[b_WX] + XCb.b(m), writes=[bps])
                P.emit("act", lambda e, ps=ps, m=m: e.activation(GI.ap(m), ps[:, :], AF.Sigmoid, bias=par(P_BX + m)),
                       reads=[bps, b_PAR], writes=GI.b(m))
                ps, bps = next_ps()
                P.emit("pe", lambda e, ps=ps, m=m: e.matmul(ps[:, :], WA[:, m, :], XCb.ap(m), start=True, stop=True),
                       reads=[b_WA] + XCb.b(m), writes=[bps])
                P.emit("act", lambda e, ps=ps, m=m: e.activation(AA.ap(m), ps[:, :], AF.Sigmoid, bias=par(P_BA + m)),
                       reads=[bps, b_PAR], writes=AA.b(m))

            P.lab = "A1"
            for m in range(8):
                P.emit("act", lambda e, m=m: e.activation(AA.ap(m), AA.ap(m), AF.Exp, scale=dp(m)),
                       reads=AA.b(m) + [b_DP], writes=AA.b(m))
            for m in range(8):
                P.emit("dve", lambda e, m=m: e.scalar_tensor_tensor(MM.ap(m), AA.ap(m), -1.0, AA.ap(m), ALU.mult, ALU.mult),
                       reads=AA.b(m), writes=MM.b(m))

            def cons_v(j, half, ps, bps):
                P.emit("dve", lambda e: e.tensor_copy(VV.ap(j, half * 512, half * 512 + 512), ps[:, :]),
                       reads=[bps], writes=VV.b(j, half * 512, half * 512 + 512))
            proj_tm("i", hsrc_tm, hbuf, cons_v)


            P.lab = "A2"
            for m in range(8):
                P.emit("act", lambda e, m=m: e.activation(MM.ap(m), MM.ap(m), AF.Ln, bias=1.0),
                       reads=MM.b(m), writes=MM.b(m))


            P.lab = "A3"
            for m in range(8):
                P.emit("act", lambda e, m=m: e.activation(MM.ap(m), MM.ap(m), AF.Exp, scale=0.5),
                       reads=MM.b(m), writes=MM.b(m))
            for m in range(8):
                P.emit("dve", lambda e, m=m: e.tensor_tensor(MM.ap(m), MM.ap(m), GI.ap(m), ALU.mult),
                       reads=MM.b(m) + GI.b(m), writes=MM.b(m))
                P.emit("dve", lambda e, m=m: e.tensor_tensor(XC.ap(m), MM.ap(m), XC.ap(m), ALU.mult),
                       reads=MM.b(m) + XC.b(m), writes=XC.b(m))
                P.emit("dve", lambda e, m=m: e.tensor_tensor_scan(MM.ap(m), AA.ap(m), XC.ap(m), HST[:, m:m + 1], ALU.mult, ALU.add),
                       reads=AA.b(m) + XC.b(m) + [b_HST], writes=MM.b(m))
                P.emit("dve", lambda e, m=m: e.tensor_tensor(YA.ap(m), MM.ap(m), SG.ap(m), ALU.mult),
                       reads=MM.b(m) + SG.b(m), writes=YA.b(m))
            P.emit("dve", lambda e: e.tensor_copy(HST[:, :], MM.ap3()[:, :, TT - 1]), reads=MM.ball(), writes=[b_HST])

            def cons_gb(m, ps, bps):
                P.emit("act", lambda e: e.activation(SGB.ap(m), ps[:, :], AF.Silu), reads=[bps], writes=SGB.b(m))
            proj_fm("gb", hsrc, hbuf, cons_gb)

            def cons_q(m, ps, bps):
                P.emit("act", lambda e: e.activation(SQ.ap(m), ps[:, :], AF.Silu), reads=[bps], writes=SQ.b(m))
            proj_fm("q", hsrc, hbuf, cons_q)

            def cons_f(m, ps, bps):
                P.emit("act", lambda e: e.activation(SS.ap(m), ps[:, :], AF.Sigmoid), reads=[bps], writes=SS.b(m))
                P.emit("act", lambda e: e.activation(SMb.ap(m), ps[:, :], AF.Sigmoid, scale=-1.0), reads=[bps], writes=SMb.b(m))
            proj_fm("f", hsrc, hbuf, cons_f)

            P.lab = "B1"
            for hd in range(8):
                P.emit("act", lambda e, hd=hd: e.activation(SS.ap(hd), SS.ap(hd), AF.Ln, bias=dp(8 + hd), scale=dp(16 + hd)),
                       reads=SS.b(hd) + [b_DP], writes=SS.b(hd))
            def bscan(hd):
                P.emit("dve", lambda e: e.tensor_tensor_scan(BB.ap(hd), ONES[:, 0:1].to_broadcast([128, TT]), SS.ap(hd), 0.0, ALU.mult, ALU.add),
                       reads=SS.b(hd) + [b_ONES], writes=BB.b(hd), name="B1")
            for hd in range(4):
                bscan(hd)
            DG3 = SM2[:, 0:32].rearrange("p (h n) -> p h n", h=8)
            bref = BB.ap3()[:, :, 63::128]

            def decay_book():
                P.emit("dve", lambda e: e.tensor_tensor(DG3[:, :, 1:4], bref[:, :, 1:4], bref[:, :, 0:3], ALU.subtract),
                       reads=BB.ball(), writes=[b_SM2], name="B2")
                P.emit("dve", lambda e: e.tensor_tensor(DG3[:, :, 0], bref[:, :, 0], REM[:, :], ALU.add),
                       reads=BB.ball() + [b_REM], writes=[b_SM2], name="B2")
                P.emit("dve", lambda e: e.tensor_tensor(REM[:, :], BB.ap3()[:, :, TT - 1], bref[:, :, 3], ALU.subtract),
                       reads=BB.ball(), writes=[b_REM], name="B2")
                P.emit("act", lambda e: e.activation(SM2[:, 32:64], SM2[:, 0:32], AF.Exp), reads=[b_SM2], writes=[b_SM2], name="B2")

            def bc(hd):
                bc_out = SS.ap(hd).rearrange("p (n t) -> p n t", n=4)
                b_in = BB.ap(hd).rearrange("p (n t) -> p n t", n=4)
                b_ref = BB.ap(hd)[:, 63::128].unsqueeze(2).to_broadcast([128, 4, 128])
                P.emit("dve", lambda e, o=bc_out, i0=b_in, i1=b_ref: e.tensor_tensor(o, i0, i1, ALU.subtract),
                       reads=BB.b(hd), writes=SS.b(hd), name="B2")

            def cons_pa(m, ps, bps):
                P.emit("dve", lambda e: e.tensor_tensor(OA.ap(m), ps[:, :], GA.ap(m), ALU.mult),
                       reads=[bps] + GA.b(m), writes=OA.b(m))
                if m < 4:
                    bscan(4 + m)
                    if m == 3:
                        decay_book()
                else:
                    bc(2 * (m - 4))
                    bc(2 * (m - 4) + 1)
            proj_fm("pa", lambda k: YA.ap(k), lambda k: YA.b(k), cons_pa)

            def epen(m):
                if m % 4 != 0:
                    return
                P.lab = "B2"
                for hd in range(m, m + 4):
                    P.emit("act", lambda e, hd=hd: e.activation(EP.ap(hd), SS.ap(hd), AF.Exp, bias=dp(24 + hd)),
                           reads=SS.b(hd) + [b_DP], writes=EP.b(hd))
                    P.emit("act", lambda e, hd=hd: e.activation(EN.ap(hd), SS.ap(hd), AF.Exp, scale=-1.0),
                           reads=SS.b(hd), writes=EN.b(hd))
            def cons_gmb(m, ps, bps):
                P.emit("act", lambda e: e.activation(GB.ap(m), ps[:, :], AF.Sigmoid, bias=par(P_BM + 8 + m)),
                       reads=[bps, b_PAR], writes=GB.b(m))
            proj_fm("gmb", hsrc, hbuf, cons_gmb, before=epen)

            P.lab = "B3"
            for hd in range(8):
                P.emit("dve", lambda e, hd=hd: e.tensor_tensor(SQ.ap(hd), SQ.ap(hd), EP.ap(hd), ALU.mult),
                       reads=SQ.b(hd) + EP.b(hd), writes=SQ.b(hd))
                P.emit("dve", lambda e, hd=hd: e.tensor_tensor(SMb.ap(hd), SMb.ap(hd), EN.ap(hd), ALU.mult),
                       reads=SMb.b(hd) + EN.b(hd), writes=SMb.b(hd))
            QT, KT = SQ, SMb

            P.lab = "KTtr"
            rot_lim[0] = 4
            for hd in range(8):
                pt, bpt = next_ps()
                ptb = pt[:, :].bitcast(BF16)
                for n in range(4):
                    P.emit("pe", lambda e, hd=hd, n=n, ptb=ptb: e.transpose(ptb[:, n * 128:(n + 1) * 128], KT.ap(hd, n * 128, (n + 1) * 128), IDB[:, :]),
                           reads=KT.b(hd) + [b_IDB], writes=[bpt])
                P.emit("act", lambda e, hd=hd, ptb=ptb: e.activation(KTT.ap(hd), ptb[:, 0:512], AF.Copy),
                       reads=[bpt], writes=KTT.b(hd))
            G3 = SM2[:, 32:64].rearrange("p (h n) -> p h n", h=8)

            def emit_sbf(n, half):
                hsl = slice(half * 4, half * 4 + 4)
                P.emit("dve", lambda e: e.tensor_tensor(SBF[:, hsl, :], TST[:, hsl, :], G3[:, hsl, n].unsqueeze(2).to_broadcast([128, 4, 128]), ALU.mult),
                       reads=b_TST[hsl] + [b_SM2], writes=b_SBF[hsl])
            att_rr = [0]
            pend_att = {}

            def att_group(n, half):
                P.lab = "heads%d" % n
                ps_a, bps_a = next_ps()
                for hh in range(4):
                    hd = half * 4 + hh
                    P.emit("pe", lambda e, hh=hh, hd=hd: e.matmul(ps_a[:, hh * 128:(hh + 1) * 128], KT.ap(hd, n * 128, (n + 1) * 128), QT.ap(hd, n * 128, (n + 1) * 128), start=True, stop=True),
                           reads=KT.b(hd) + QT.b(hd), writes=[bps_a])
                gi = att_rr[0] % 2
                att_rr[0] += 1
                P.emit("dve", lambda e: e.tensor_tensor(ATT4[gi][:, :, :], ps_a[:, :].rearrange("p (h t) -> p h t", h=4),
                                                        MSK[:, :].unsqueeze(1).to_broadcast([128, 4, 128]), ALU.mult),
                       reads=[bps_a, b_MSK], writes=[b_ATT4[gi]])
                pend_att[(n, half)] = gi

            def heads(n, half):
                P.lab = "heads%d" % n
                POn, b_POn = PO[n % 2], b_PO[n % 2]
                gcols = [SM2[:, 32 + hd * 4 + n:33 + hd * 4 + n] for hd in range(8)]
                if n == 0 and half == 0:
                    emit_sbf(0, 0)
                    emit_sbf(0, 1)
                    att_group(0, 0)
                nxt = (n, 1) if half == 0 else (n + 1, 0)
                if nxt[0] < 4:
                    att_group(*nxt)
                P.lab = "heads%d" % n
                gi = pend_att.pop((n, half))
                for hd in range(half * 4, half * 4 + 4):
                    hs = slice(hd * 128, (hd + 1) * 128)
                    P.emit("pe", lambda e, hd=hd, hs=hs: e.matmul(POn[:, hs], ATT4[gi][:, hd % 4, :], VV.ap(n, hs.start, hs.stop), start=True, stop=False),
                           reads=[b_ATT4[gi]] + VV.b(n), writes=[b_POn[hd // 4]])
                    P.emit("pe", lambda e, hd=hd, hs=hs: e.matmul(POn[:, hs], QT.ap(hd, n * 128, (n + 1) * 128), SBF[:, hd, :], start=False, stop=True),
                           reads=QT.b(hd) + [b_SBF[hd]], writes=[b_POn[hd // 4]])
                    ps_k, bps_k = next_ps()
                    P.emit("pe", lambda e, ps_k=ps_k, hd=hd, hs=hs: e.matmul(ps_k[:, 0:128], KTT.ap(hd, n * 128, (n + 1) * 128), VV.ap(n, hs.start, hs.stop), start=True, stop=True),
                           reads=KTT.b(hd) + VV.b(n), writes=[bps_k])
                    P.emit("dve", lambda e, ps_k=ps_k, hd=hd, gcol=gcols[hd]: e.scalar_tensor_tensor(TST[:, hd, :], TST[:, hd, :], gcol, ps_k[:, 0:128], ALU.mult, ALU.add),
                           reads=[b_TST[hd], b_SM2, bps_k], writes=[b_TST[hd]])
                smc = 64 + 24 * (n % 2)
                for h2 in range(half * 4, half * 4 + 4):
                    P.emit("act", lambda e, h2=h2: e.activation(SQJ[:, h2 * 128:(h2 + 1) * 128], POn[:, h2 * 128:(h2 + 1) * 128], AF.Square,
                                                                accum_out=SMALL[:, smc + h2:smc + h2 + 1]),
                           reads=[b_POn[half]], writes=[b_SQJ, b_SMhg2[n % 2]])
                if n + 1 < 4:
                    emit_sbf(n + 1, half)

            def stats(n):
                P.lab = "stats%d" % n
                POn, b_POn = PO[n % 2], b_PO[n % 2]
                smc = 64 + 24 * (n % 2)
                b_sm = b_SMhg2[n % 2]
                P.emit("act", lambda e: e.activation(SMALL[:, smc + 8:smc + 16], SMALL[:, smc:smc + 8], AF.Ln, bias=EPS, scale=1.0 / 128),
                       reads=[b_sm], writes=[b_sm])
                P.emit("act", lambda e: e.activation(SMALL[:, smc + 16:smc + 24], SMALL[:, smc + 8:smc + 16], AF.Exp, scale=-0.5),
                       reads=[b_sm], writes=[b_sm])
                P.emit("dve", lambda e: e.tensor_tensor(ONB[:, :].rearrange("p (h v) -> p h v", h=8), POn[:, :].rearrange("p (h v) -> p h v", h=8),
                                                        SMALL[:, smc + 16:smc + 24].unsqueeze(2).to_broadcast([128, 8, 128]), ALU.mult),
                       reads=b_POn + [b_sm], writes=[b_ONB])

            def trans(n):
                P.lab = "trans%d" % n
                cs = slice(n * 128, (n + 1) * 128)
                pt, bpt = next_ps()
                ptb = pt[:, :].bitcast(BF16)
                for hd in range(8):
                    P.emit("pe", lambda e, hd=hd, ptb=ptb: e.transpose(ptb[:, hd * 128:(hd + 1) * 128], ONB[:, hd * 128:(hd + 1) * 128], IDB[:, :]),
                           reads=[b_ONB, b_IDB], writes=[bpt])
                if n == 3:
                    P.emit("dve", lambda e, cs=cs, ptb=ptb: e.scalar_tensor_tensor(YB.ap3(cs.start, cs.stop), ptb[:, :].rearrange("p (h t) -> p h t", h=8), par(P_HG),
                                                                                  SGB.ap3(cs.start, cs.stop), ALU.mult, ALU.mult),
                           reads=[bpt, b_PAR] + SGB.ball(), writes=YB.ball())
                    return
                P.emit("act", lambda e, cs=cs, ptb=ptb: e.activation(YB.ap3(cs.start, cs.stop), ptb[:, :].rearrange("p (h t) -> p h t", h=8), AF.Copy, scale=par(P_HG)),
                       reads=[bpt, b_PAR], writes=YB.ball())
                P.emit("pool", lambda e, cs=cs: e.tensor_tensor(YB.ap3(cs.start, cs.stop), YB.ap3(cs.start, cs.stop), SGB.ap3(cs.start, cs.stop), ALU.mult),
                       reads=YB.ball() + SGB.ball(), writes=YB.ball())

            heads(0, 0)
            heads(0, 1)
            for n in range(1, 4):
                heads(n, 0)
                stats(n - 1)
                heads(n, 1)
                trans(n - 1)
            stats(3)
            trans(3)
            rot_lim[0] = 8

        def merge_mm(tile):
            XTt, b_XTt = XT[tile % 2], b_XT[tile % 2]
            def cons_pb(m, ps, bps):
                P.emit("dve", lambda e: e.tensor_tensor(MIX.ap(m), ps[:, :], GB.ap(m), ALU.mult),
                       reads=[bps] + GB.b(m), writes=MIX.b(m))
                P.emit("dve", lambda e: e.tensor_tensor(MIX.ap(m), MIX.ap(m), OA.ap(m), ALU.add),
                       reads=MIX.b(m) + OA.b(m), writes=MIX.b(m))
            proj_fm("pb", lambda k: YB.ap(k), lambda k: YB.b(k), cons_pb)

        def merge_wo(tile):
            XTt, b_XTt = XT[tile % 2], b_XT[tile % 2]

            def cons_wo(j, half, ps, bps):
                sl = slice(half * 512, half * 512 + 512)
                P.emit("dve", lambda e: e.tensor_tensor(XTt[:, j, sl], ps[:, :], XTt[:, j, sl], ALU.add),
                       reads=[bps, b_XTt[j]], writes=[b_XTt[j]])
            proj_tm("wo", lambda k, j: MIX.ap(k, j * 128, (j + 1) * 128), lambda k: MIX.b(k), cons_wo)

        s_fin = [sem("st_f%d" % j) for j in range(4)]
        b_SMfj = [Buf("SMfin%d" % j) for j in range(4)]

        def final_split(tile):
            P.lab = "final"
            XTt, b_XTt = XT[tile % 2], b_XT[tile % 2]
            for j in range(4):
                P.emit("act", lambda e, j=j: e.activation(SQJ[:, :], XTt[:, j, :], AF.Square, accum_out=SMALL[:, 40 + j:41 + j]),
                       reads=[b_XTt[j]], writes=[b_SQJ, b_SMfj[j]])
                P.emit("act", lambda e, j=j: e.activation(SMALL[:, 44 + j:45 + j], SMALL[:, 40 + j:41 + j], AF.Ln, bias=EPS, scale=1.0 / D),
                       reads=[b_SMfj[j]], writes=[b_SMfj[j]])
                P.emit("act", lambda e, j=j: e.activation(SMALL[:, 48 + j:49 + j], SMALL[:, 44 + j:45 + j], AF.Exp, scale=-0.5),
                       reads=[b_SMfj[j]], writes=[b_SMfj[j]])
                P.emit("dve", lambda e, j=j: e.scalar_tensor_tensor(XTt[:, j, :], XTt[:, j, :], SMALL[:, 48 + j:49 + j], FNG[:, :], ALU.mult, ALU.mult),
                       reads=[b_XTt[j], b_SMfj[j], b_FNG], writes=[b_XTt[j]])
                op = P.emit("sp", lambda e, tile=tile, j=j: e.dma_start(out=yv[tile][:, j, :], in_=XTt[:, j, :]), reads=[b_XTt[j]], dma_sem=s_fin[j])
                out_ops.append(op)

        def final(tile):
            P.lab = "final"
            XTt, b_XTt = XT[tile % 2], b_XT[tile % 2]
            for j in range(4):
                P.emit("act", lambda e, j=j: e.activation(SQJ[:, :], XTt[:, j, :], AF.Square, accum_out=SMALL[:, 40 + j:41 + j]),
                       reads=[b_XTt[j]], writes=[b_SQJ, b_SMfin])
            P.emit("act", lambda e: e.activation(SMALL[:, 44:48], SMALL[:, 40:44], AF.Ln, bias=EPS, scale=1.0 / D),
                   reads=[b_SMfin], writes=[b_SMfin])
            P.emit("act", lambda e: e.activation(SMALL[:, 48:52], SMALL[:, 44:48], AF.Exp, scale=-0.5),
                   reads=[b_SMfin], writes=[b_SMfin])
            for j in range(4):
                P.emit("dve", lambda e, j=j: e.scalar_tensor_tensor(XTt[:, j, :], XTt[:, j, :], SMALL[:, 48 + j:49 + j], FNG[:, :], ALU.mult, ALU.mult),
                       reads=[b_XTt[j], b_SMfin, b_FNG], writes=[b_XTt[j]])
            op = P.emit("sp", lambda e, tile=tile: e.dma_start(out=yv[tile], in_=XTt[:, :, :]), reads=b_XTt, dma_sem=s_st[tile % 2])
            out_ops.append(op)


        ntiles = NB * NT
        s0_stats(0)
        s0_tr(0)
        for tile in range(ntiles):
            it = tile % NT
            if tile == 0:
                s0_load(1)
                P.emit("sp", lambda e: e.dma_start(out=FNG[:, :], in_=dram["fng"]), writes=[b_FNG], dma_sem=s_c[1])
                front(tile, it)
            else:
                def after_xa(tile=tile):
                    final(tile - 1)
                    if tile + 1 < ntiles:
                        s0_load(tile + 1)
                front(tile, it, after_xa)
            merge_mm(tile)
            if tile + 1 < ntiles:
                s0_stats(tile + 1)
            merge_wo(tile)
            if tile + 1 < ntiles:
                s0_tr(tile + 1)
        final_split(ntiles - 1)

        block = es.enter_context(nc.Block())
        P.finalize(nc, block, engine_sems, out_ops)
        global LAST_PROG
        LAST_PROG = P


def _consts():
    c = np.zeros((128, 256), np.float32)
    c[:, 0:128] = np.eye(128, dtype=np.float32)
    s = np.arange(128)[:, None]
    t = np.arange(128)[None, :]
    c[:, 128:256] = (s <= t).astype(np.float32)
    return c


def _pack_params(inp):
    def fm(v, n):
        return np.ascontiguousarray(np.asarray(v, np.float32).reshape(n, 128).T)
    p = np.zeros((128, NPAR), np.float32)
    p[:, P_NG:P_NG + 8] = fm(inp["norm_g"][0], 8)
    for k in range(4):
        p[:, P_CW + 8 * k:P_CW + 8 * k + 8] = fm(inp["conv_w"][0, k], 8)
    p[:, P_CB:P_CB + 8] = fm(inp["conv_b"][0], 8)
    p[:, P_BX:P_BX + 8] = fm(inp["rg_bx"][0].reshape(-1), 8)
    p[:, P_BA:P_BA + 8] = fm(inp["rg_ba"][0].reshape(-1), 8)
    p[:, P_LAM:P_LAM + 8] = fm(inp["rg_lambda"][0], 8)
    p[:, P_L0:P_L0 + 8] = fm(inp["hg_lb_logits"][0], 8)
    p[:, P_L1:P_L1 + 8] = fm(inp["hg_lb_logits"][1], 8)
    p[:, P_HG:P_HG + 1] = fm(inp["hg_norm_g"][0], 1)
    p[:, P_BM:P_BM + 16] = fm(inp["b_merge"][0], 16)
    return p


def _pack_weights(inp):
    w_in = np.asarray(inp["w_in"], np.float32)[0]
    extra = {"pa": np.asarray(inp["proj_a"], np.float32)[0], "pb": np.asarray(inp["proj_b"], np.float32)[0],
             "wo": np.asarray(inp["w_out"], np.float32)[0]}
    out = np.empty((2 * len(WORDER), 128, 8 * 512), np.float32)
    for i, name in enumerate(WORDER):
        W = extra[name] if name in extra else w_in[:, W_IN_COLS[name]:W_IN_COLS[name] + 1024]
        Wr = W.reshape(8, 128, 2, 512)
        out[2 * i:2 * i + 2] = Wr.transpose(2, 1, 0, 3).reshape(2, 128, 8 * 512)
    return out


def make_in_maps(inp, n_cores=8):
    x = np.asarray(inp["x"], np.float32)
    shared = {
        "wpk": _pack_weights(inp),
        "rg_wx": np.ascontiguousarray(np.asarray(inp["rg_wx"], np.float32)[0].transpose(1, 0, 2)),
        "rg_wa": np.ascontiguousarray(np.asarray(inp["rg_wa"], np.float32)[0].transpose(1, 0, 2)),
        "params": _pack_params(inp),
        "fng": np.ascontiguousarray(np.broadcast_to(np.asarray(inp["final_norm_g"], np.float32)[None, :], (128, D))),
        "consts": _consts(),
    }
    maps = []
    for c in range(n_cores):
        m = dict(shared)
        m["x"] = np.ascontiguousarray(x[NB * c:NB * (c + 1)].reshape(NB * SEQ, D))
        maps.append(m)
    return maps


def kernel(**inputs):
    nc = build_nc()
    in_maps = make_in_maps(inputs)
    res = run_bass_kernel_spmd(nc, in_maps, core_ids=list(range(8)))
    out = np.concatenate([np.asarray(r["y"]).reshape(NB, SEQ, D) for r in res.results], axis=0)
    return out.astype(np.float32)
```

```python
import numpy as np
import concourse.bass as bass
import concourse.mybir as mybir
from concourse.bass_utils import run_bass_kernel_spmd

F32 = mybir.dt.float32
BF16 = mybir.dt.bfloat16
AF = mybir.ActivationFunctionType
ALU = mybir.AluOpType
AX = mybir.AxisListType

D = 1024
SEQ = 2048
NB = 2
TT = 512
NT = SEQ // TT
NCH = 8
IN_COLS = 8192
EPS = 1e-6
LRU_C = 8.0
HG_SCALE = 128.0 ** -0.5

P_NG = 0
P_CW = 8
P_CB = 40
P_BX = 48
P_BA = 56
P_LAM = 64
P_L0 = 72
P_L1 = 80
P_HG = 88
P_BM = 89
NPAR = 105

WORDER = ["xa", "ga", "gma", "i", "gb", "q", "f", "pa", "gmb", "pb", "wo"]
W_IN_COLS = {"xa": 0, "ga": 1024, "q": 2048, "f": 3072, "i": 4096, "gb": 5120, "gma": 6144, "gmb": 7168}

ENGINES = ("pe", "act", "dve", "pool", "sp")
LAST_PROG = None


class Buf:
    __slots__ = ("name", "last_w", "readers")

    def __init__(self, name):
        self.name = name
        self.last_w = None
        self.readers = []


class Op:
    __slots__ = ("eng", "fn", "deps", "dma_sem", "sig", "need_sig", "name")

    def __init__(self, eng, fn, dma_sem, name):
        self.eng = eng
        self.fn = fn
        self.deps = []
        self.dma_sem = dma_sem
        self.sig = None
        self.need_sig = False
        self.name = name


class Prog:
    def __init__(self):
        self.ops = {e: [] for e in ENGINES}
        self.final_waits = []
        self.lab = ""

    def emit(self, eng, fn, reads=(), writes=(), dma_sem=None, name=""):
        op = Op(eng, fn, dma_sem, name or self.lab)
        deps = {}
        for b in reads:
            w = b.last_w
            if w is not None:
                deps[id(w)] = w
        for b in writes:
            w = b.last_w
            if w is not None:
                deps[id(w)] = w
            for r in b.readers:
                if r is not op:
                    deps[id(r)] = r
        for w in deps.values():
            if w.eng == "pe" and eng == "pe":
                continue
            op.deps.append(w)
            w.need_sig = True
        for b in reads:
            b.readers.append(op)
        for b in writes:
            b.last_w = op
            b.readers = []
        if dma_sem is not None:
            op.need_sig = True
        self.ops[eng].append(op)
        return op

    def finalize(self, nc, block, engine_sems, out_ops):
        counts = {}
        for e in ENGINES:
            for op in self.ops[e]:
                if not op.need_sig:
                    continue
                if op.dma_sem is not None:
                    sem, inc = op.dma_sem, 16
                else:
                    sem, inc = engine_sems[e], 1
                counts[id(sem)] = counts.get(id(sem), 0) + inc
                op.sig = (sem, counts[id(sem)], inc)
        self.sigtable = {e: [(op.sig[1], op.name) for op in self.ops[e] if op.sig is not None and op.dma_sem is None] for e in ENGINES}
        finals = {}
        for op in out_ops:
            sem, val, _ = op.sig
            if id(sem) not in finals or finals[id(sem)][1] < val:
                finals[id(sem)] = (sem, val)

        def run(e, engobj):
            waited = {}
            for op in self.ops[e]:
                need = {}
                for d in op.deps:
                    sem, val, _ = d.sig
                    if need.get(id(sem), (None, 0))[1] < val:
                        need[id(sem)] = (sem, val)
                for k, (sem, val) in need.items():
                    if waited.get(k, 0) >= val:
                        continue
                    engobj.wait_ge(sem, val)
                    waited[k] = val
                ins = op.fn(engobj)
                if op.sig is not None:
                    ins.then_inc(op.sig[0], op.sig[2])
            if e == "sp":
                for sem, val in finals.values():
                    engobj.wait_ge(sem, val)

        @block.tensor
        def _(eng):
            run("pe", eng)

        @block.scalar
        def _(eng):
            run("act", eng)

        @block.vector
        def _(eng):
            run("dve", eng)

        @block.gpsimd
        def _(eng):
            run("pool", eng)

        @block.sync
        def _(eng):
            run("sp", eng)


class Region:
    def __init__(self, nc, es, name, nbytes, gran=1024):
        assert nbytes % 4 == 0
        self.GRAN = gran
        self.t = es.enter_context(nc.sbuf_tensor(name, [128, nbytes // 4], F32))
        self.nbytes = nbytes
        self.bufs_ = [Buf("%s.%d" % (name, i)) for i in range((nbytes + self.GRAN - 1) // self.GRAN)]
        self.f32 = self.t[:, :]
        self.b16 = self.t[:, :].bitcast(BF16)

    def bufs(self, lo, hi):
        return self.bufs_[lo // self.GRAN:(hi + self.GRAN - 1) // self.GRAN]


class Stream:
    def __init__(self, reg, dt, base, stride, nch, width):
        self.reg, self.dt, self.base, self.stride, self.nch, self.width = reg, dt, base, stride, nch, width
        self.es = 4 if dt == F32 else 2
        self.flat = reg.f32 if dt == F32 else reg.b16
        assert (base + nch * stride) * self.es <= reg.nbytes + 0, (base, nch, stride, reg.nbytes)

    def ap(self, c, lo=0, hi=None):
        hi = self.width if hi is None else hi
        o = self.base + c * self.stride
        return self.flat[:, o + lo:o + hi]

    def b(self, c, lo=0, hi=None):
        hi = self.width if hi is None else hi
        o = self.base + c * self.stride
        return self.reg.bufs((o + lo) * self.es, (o + hi) * self.es)

    def ap3(self, lo=0, hi=None):
        hi = self.width if hi is None else hi
        v = self.flat[:, self.base:self.base + self.nch * self.stride].rearrange("p (c t) -> p c t", c=self.nch)
        return v[:, :, lo:hi]

    def ball(self):
        return self.reg.bufs(self.base * self.es, (self.base + self.nch * self.stride) * self.es)


def build_nc(stage=99, debug=None):
    nc = bass.Bass("TRN2", target_bir_lowering=False)
    dram = {}
    dram["x"] = nc.dram_tensor("x", [NB * SEQ, D], F32, kind="ExternalInput").ap()
    dram["wpk"] = nc.dram_tensor("wpk", [2 * len(WORDER), 128, 8 * 512], F32, kind="ExternalInput").ap()
    dram["rg_wx"] = nc.dram_tensor("rg_wx", [128, 8, 128], F32, kind="ExternalInput").ap()
    dram["rg_wa"] = nc.dram_tensor("rg_wa", [128, 8, 128], F32, kind="ExternalInput").ap()
    dram["params"] = nc.dram_tensor("params", [128, NPAR], F32, kind="ExternalInput").ap()
    dram["fng"] = nc.dram_tensor("fng", [128, D], F32, kind="ExternalInput").ap()
    dram["consts"] = nc.dram_tensor("consts", [128, 256], F32, kind="ExternalInput").ap()
    dram["y"] = nc.dram_tensor("y", [NB * SEQ, D], F32, kind="ExternalOutput").ap()
    dram["wbf"] = nc.dram_tensor("wbf", [22, 128, 8 * 512], BF16, kind="Internal").ap()
    dbg = None
    if debug is not None:
        dbg = nc.dram_tensor("dbg", list(debug), F32, kind="ExternalOutput").ap()
    _build(nc, dram, stage, dbg)
    return nc


def _build(nc, dram, stage, dbg):
    from contextlib import ExitStack
    P = Prog()
    with ExitStack() as es:
        def sb(name, shape, dt):
            return es.enter_context(nc.sbuf_tensor(name, shape, dt))

        def sem(name):
            return es.enter_context(nc.semaphore(name))

        engine_sems = {e: sem("s_" + e) for e in ENGINES}

        XT = [sb("XT%d" % i, [128, 4, D], F32) for i in range(2)]
        HT = sb("HT", [128, NCH, TT], BF16)
        SQJ = sb("SQJ", [128, D], BF16)
        PAR = sb("PAR", [128, NPAR], F32)
        DP = sb("DP", [128, 64], F32)
        FNG = sb("FNG", [128, D], F32)
        IDB = sb("IDB", [128, 128], BF16)
        ONES = sb("ONES", [128, 1], F32)
        SMALL = sb("SMALL", [128, 128], F32)
        SM2 = sb("SM2", [128, 64], F32)
        REM = sb("REM", [128, 8], F32)
        HST = sb("HST", [128, 8], F32)
        HALO = sb("HALO", [128, 8, 3], F32)
        TST = sb("TST", [128, 8, 128], F32)
        SBF = sb("SBF", [128, 8, 128], BF16)
        ATT4 = [sb("ATT4_%d" % i, [128, 4, 128], BF16) for i in range(2)]
        MSK = sb("MSK", [128, 128], F32)
        ONB = sb("ONB", [128, D], BF16)
        WX = sb("WXb", [128, 8, 128], BF16)
        WA = sb("WAb", [128, 8, 128], BF16)
        NWS = 4
        WS = [sb("WS%d" % i, [128, 8, 512], BF16) for i in range(NWS)]
        b_XT = [[Buf("XT%d_%d" % (i, j)) for j in range(4)] for i in range(2)]
        b_HT = [Buf("HT%d" % c) for c in range(NCH)]
        b_SQJ, b_PAR, b_DP, b_FNG = Buf("SQJ"), Buf("PAR"), Buf("DP"), Buf("FNG")
        b_IDB, b_MSK, b_ONES, b_SM2 = Buf("IDB"), Buf("MSK"), Buf("ONES"), Buf("SM2")
        b_SMs0, b_SMhg, b_SMfin = Buf("SMs0"), Buf("SMhg"), Buf("SMfin")
        b_SMhg2 = [Buf("SMhg0"), Buf("SMhg1")]
        b_REM, b_HST, b_HALO = Buf("REM"), Buf("HST"), Buf("HALO")
        b_TST = [Buf("TST%d" % h) for h in range(8)]
        b_SBF = [Buf("SBF%d" % h) for h in range(8)]
        b_ATT4 = [Buf("ATT4_%d" % i) for i in range(2)]
        b_ONB = Buf("ONB")
        b_WX, b_WA = Buf("WX"), Buf("WA")
        b_WS = [Buf("WS%d" % i) for i in range(4)]

        XW = 516
        R1 = Region(nc, es, "R1", 8 * XW * 4, gran=XW * 4)
        R2 = Region(nc, es, "R2", 8 * TT * 4)
        R3 = Region(nc, es, "R3", 8 * TT * 4)
        R4 = Region(nc, es, "R4", 8 * TT * 4)
        Q1 = Region(nc, es, "Q1", 8 * TT * 2)
        Q2 = Region(nc, es, "Q2", 8 * TT * 2)
        Q3 = Region(nc, es, "Q3", 8 * TT * 2)
        Q4 = Region(nc, es, "Q4", 8 * TT * 2)
        Q5 = Region(nc, es, "Q5", 8 * TT * 2)
        Q6 = Region(nc, es, "Q6", 8 * TT * 2)
        XA = Stream(R1, F32, 0, XW, 8, 3 + TT)
        XC = Stream(R2, F32, 0, TT, 8, TT)
        AA = Stream(R3, F32, 0, TT, 8, TT)
        MM = Stream(R4, F32, 0, TT, 8, TT)
        XCb = Stream(Q1, BF16, 0, TT, 8, TT)
        GI = Stream(Q2, BF16, 0, TT, 8, TT)
        SG = Stream(Q3, BF16, 0, TT, 8, TT)
        YA = Stream(Q4, BF16, 0, TT, 8, TT)
        GA = Stream(Q6, BF16, 0, TT, 8, TT)
        OA = Stream(Q5, BF16, 0, TT, 8, TT)
        SS = Stream(R2, F32, 0, TT, 8, TT)
        BB = Stream(R3, F32, 0, TT, 8, TT)
        SMb = Stream(Q2, BF16, 0, TT, 8, TT)
        EP = Stream(R4, BF16, 8 * TT, TT, 8, TT)
        EN = Stream(Q3, BF16, 0, TT, 8, TT)
        SQ = Stream(Q1, BF16, 0, TT, 8, TT)
        KTT = Stream(R4, BF16, 0, TT, 8, TT)
        VV = Stream(R1, BF16, 8 * TT, D, 4, D)
        SGB = Stream(R1, BF16, 0, TT, 8, TT)
        YB = Stream(R4, BF16, 8 * TT, TT, 8, TT)
        ON = Stream(Q6, BF16, 0, D, 4, D)
        XN = ON
        GB = Stream(Q6, BF16, 0, TT, 8, TT)
        MIX = Stream(Q3, BF16, 0, TT, 8, TT)

        PO = [es.enter_context(nc.psum_tensor("PO%d" % i, [128, 1024], F32)) for i in range(2)]
        b_PO = [[Buf("PO%d_%d" % (i, h)) for h in range(2)] for i in range(2)]
        NROT = 4
        PS = [es.enter_context(nc.psum_tensor("PS%d" % i, [128, 512], F32)) for i in range(NROT)]
        b_PS = [Buf("PS%d" % i) for i in range(NROT)]
        ps_rr = [0]
        ROT = [(PS[i][:, :], b_PS[i]) for i in range(NROT)] + \
              [(PO[i][:, h * 512:(h + 1) * 512], b_PO[i][h]) for i in range(2) for h in range(2)]
        rot_lim = [8]

        def next_ps():
            i = ps_rr[0] % rot_lim[0]
            ps_rr[0] = (i + 1) % rot_lim[0]
            return ROT[i]

        s_ld = [sem("ld_x0"), sem("ld_x1")]
        s_c = [sem("ld_c%d" % i) for i in range(5)]
        s_st = [sem("st_y0"), sem("st_y1")]
        s_w = [sem("ld_w%d" % i) for i in range(4)]
        out_ops = []

        P.emit("sp", lambda e: e.dma_start(out=XT[0][:, :, :], in_=dram["x"].rearrange("(n j p) d -> n p j d", j=4, p=128)[0]),
               writes=b_XT[0], dma_sem=s_ld[0])
        P.emit("sp", lambda e: e.dma_start(out=PAR[:, :], in_=dram["params"]), writes=[b_PAR], dma_sem=s_c[0])
        CST = Q6.f32[:, 0:256]
        b_CST = Q6.bufs(0, 1024)
        P.emit("sp", lambda e: e.dma_start(out=CST, in_=dram["consts"]), writes=b_CST, dma_sem=s_c[2])
        P.emit("pool", lambda e: e.dma_start(out=WX[:, :, :], in_=dram["rg_wx"]), writes=[b_WX], dma_sem=s_c[3])
        P.emit("pool", lambda e: e.dma_start(out=WA[:, :, :], in_=dram["rg_wa"]), writes=[b_WA], dma_sem=s_c[4])
        P.emit("dve", lambda e: e.tensor_copy(IDB[:, :], CST[:, 0:128]), reads=b_CST, writes=[b_IDB])
        P.emit("dve", lambda e: e.tensor_copy(MSK[:, :], CST[:, 128:256]), reads=b_CST, writes=[b_MSK])
        P.emit("dve", lambda e: e.memset(ONES[:, :], 1.0), writes=[b_ONES])
        P.emit("act", lambda e: e.activation(DP[:, 32:40], PAR[:, P_LAM:P_LAM + 8], AF.Exp, scale=-1.0), reads=[b_PAR], writes=[b_DP])
        P.emit("act", lambda e: e.activation(DP[:, 40:48], DP[:, 32:40], AF.Ln, bias=1.0), reads=[b_DP], writes=[b_DP])
        P.emit("dve", lambda e: e.tensor_scalar(DP[:, 0:8], DP[:, 40:48], -LRU_C, None, ALU.mult), reads=[b_DP], writes=[b_DP])
        P.emit("dve", lambda e: e.tensor_tensor(DP[:, 48:56], PAR[:, P_L0:P_L0 + 8], PAR[:, P_L1:P_L1 + 8], ALU.subtract), reads=[b_PAR], writes=[b_DP])
        P.emit("act", lambda e: e.activation(DP[:, 8:16], DP[:, 48:56], AF.Sigmoid), reads=[b_DP], writes=[b_DP])
        P.emit("dve", lambda e: e.tensor_scalar(DP[:, 16:24], DP[:, 8:16], -1.0, 1.0, ALU.mult, ALU.add), reads=[b_DP], writes=[b_DP])
        P.emit("act", lambda e: e.activation(DP[:, 24:32], DP[:, 16:24], AF.Ln, scale=HG_SCALE), reads=[b_DP], writes=[b_DP])

        xv = dram["x"].rearrange("(n j p) d -> n p j d", j=4, p=128)
        yv = dram["y"].rearrange("(n j p) d -> n p j d", j=4, p=128)

        NHC = 2 * len(WORDER)
        total_w = NB * NT * NHC
        wstate = {"n": 0, "issued": 0}
        b_WBF = [Buf("WBF%d" % i) for i in range(NHC)]
        s_wst = [sem("st_w%d" % i) for i in range(NWS)]
        PREF = NWS - 1
        NHC_A = 12

        def issue_w(idx):
            if idx >= total_w:
                return
            k = idx % NHC
            name, h = WORDER[k // 2], k % 2
            s = idx % NWS
            tile_i = idx // NHC
            cached = (tile_i >= 2) or (tile_i == 1 and k < NHC_A)
            if not cached:
                gate = b_XT[0] if idx <= PREF else []
                P.emit("pool", lambda e, s=s, k=k: e.dma_start(out=WS[s][:, :, :].rearrange("p k n -> p (k n)"), in_=dram["wpk"][k]),
                       reads=gate, writes=[b_WS[s]], dma_sem=s_w[s], name="ldw_" + name)
                if (tile_i == 0 and k < NHC_A) or (tile_i == 1 and k >= NHC_A):
                    P.emit("sp", lambda e, s=s, k=k: e.dma_start(out=dram["wbf"][k], in_=WS[s][:, :, :].rearrange("p k n -> p (k n)")),
                           reads=[b_WS[s]], writes=[b_WBF[k]], dma_sem=s_wst[s], name="stw_" + name)
            else:
                P.emit("pool", lambda e, s=s, k=k: e.dma_start(out=WS[s][:, :, :].rearrange("p k n -> p (k n)"), in_=dram["wbf"][k]),
                       reads=[b_WBF[k]], writes=[b_WS[s]], dma_sem=s_w[s], name="ldwb_" + name)

        def next_w(name, h):
            idx = wstate["n"]
            assert WORDER[(idx % NHC) // 2] == name and idx % 2 == h, (name, h, idx)
            while wstate["issued"] <= min(idx + PREF, total_w - 1):
                issue_w(wstate["issued"])
                wstate["issued"] += 1
            wstate["n"] = idx + 1
            return WS[idx % NWS], b_WS[idx % NWS]

        def proj_fm(wname, src, src_bufs, consume, before=None):
            for m in range(8):
                if m % 4 == 0:
                    W, bW = next_w(wname, m // 4)
                if before is not None:
                    before(m)
                P.lab = wname
                ps, bps = next_ps()
                mm = m % 4
                for k in range(8):
                    P.emit("pe", lambda e, W=W, ps=ps, k=k, mm=mm: e.matmul(ps[:, :], W[:, k, mm * 128:(mm + 1) * 128], src(k), start=(k == 0), stop=(k == 7)),
                           reads=[bW] + src_bufs(k), writes=[bps])
                consume(m, ps, bps)

        def proj_tm(wname, src, src_bufs, consume):
            for half in range(2):
                W, bW = next_w(wname, half)
                P.lab = wname
                for j in range(4):
                    ps, bps = next_ps()
                    for k in range(8):
                        P.emit("pe", lambda e, W=W, ps=ps, k=k, j=j: e.matmul(ps[:, :], src(k, j), W[:, k, :], start=(k == 0), stop=(k == 7)),
                               reads=[bW] + src_bufs(k), writes=[bps])
                    consume(j, half, ps, bps)

        par = lambda col: PAR[:, col:col + 1]
        dp = lambda col: DP[:, col:col + 1]

        hsrc = lambda k: HT[:, k, :]
        hbuf = lambda k: [b_HT[k]]
        hsrc_tm = lambda k, j: HT[:, k, j * 128:(j + 1) * 128]

        def s0_load(tile):
            XTt, b_XTt = XT[tile % 2], b_XT[tile % 2]
            P.emit("sp", lambda e, tile=tile: e.dma_start(out=XTt[:, :, :], in_=xv[tile]),
                   writes=b_XTt, dma_sem=s_ld[tile % 2])

        def s0_stats(tile):
            P.lab = "s0_stats"
            XTt, b_XTt = XT[tile % 2], b_XT[tile % 2]
            for j in range(4):
                P.emit("act", lambda e, j=j: e.activation(SQJ[:, :], XTt[:, j, :], AF.Square, accum_out=SMALL[:, j:j + 1]),
                       reads=[b_XTt[j]], writes=[b_SQJ, b_SMs0])
            P.emit("act", lambda e: e.activation(SMALL[:, 4:8], SMALL[:, 0:4], AF.Ln, bias=EPS, scale=1.0 / D),
                   reads=[b_SMs0], writes=[b_SMs0])
            P.emit("act", lambda e: e.activation(SMALL[:, 8:12], SMALL[:, 4:8], AF.Exp, scale=-0.5),
                   reads=[b_SMs0], writes=[b_SMs0])
            for j in range(4):
                P.emit("dve", lambda e, j=j: e.tensor_scalar(XN.ap(j), XTt[:, j, :], SMALL[:, 8 + j:9 + j], None, ALU.mult),
                       reads=[b_XTt[j], b_SMs0], writes=XN.b(j))

        def s0_tr(tile):
            P.lab = "s0_tr"
            for c in range(NCH):
                pt, bpt = next_ps()
                ptb = pt[:, :].bitcast(BF16)
                for j in range(4):
                    P.emit("pe", lambda e, c=c, j=j, ptb=ptb: e.transpose(ptb[:, j * 128:(j + 1) * 128], XN.ap(j, c * 128, (c + 1) * 128), IDB[:, :]),
                           reads=XN.b(j) + [b_IDB], writes=[bpt])
                P.emit("act", lambda e, c=c, ptb=ptb: e.activation(HT[:, c, :], ptb[:, 0:512], AF.Copy, scale=par(P_NG + c)),
                       reads=[bpt, b_PAR], writes=[b_HT[c]])

        def front(tile, it, after_xa=None):
            if it == 0:
                P.emit("dve", lambda e: e.memset(XA.ap3(0, 3), 0.0), writes=XA.ball())
                P.emit("dve", lambda e: e.memset(HST[:, :], 0.0), writes=[b_HST])
                P.emit("dve", lambda e: e.memset(REM[:, :], 0.0), writes=[b_REM])
                P.emit("dve", lambda e: e.memset(TST[:, :, :], 0.0), writes=b_TST)
            else:
                P.emit("dve", lambda e: e.tensor_copy(XA.ap3(0, 3), HALO[:, :, :]), reads=[b_HALO], writes=XA.ball())

            def cast_xc(m):
                P.emit("dve", lambda e: e.tensor_copy(XCb.ap(m), XC.ap(m)), reads=XC.b(m), writes=XCb.b(m))

            def cons_xa(m, ps, bps):
                P.emit("act", lambda e: e.activation(XA.ap(m, 3, 3 + TT), ps[:, :], AF.Copy), reads=[bps], writes=XA.b(m))
                P.emit("dve", lambda e: e.tensor_scalar(XC.ap(m), XA.ap(m, 0, TT), par(P_CW + m), par(P_CB + m), ALU.mult, ALU.add),
                       reads=XA.b(m) + [b_PAR], writes=XC.b(m))
                for k in range(1, 4):
                    P.emit("dve", lambda e, k=k: e.scalar_tensor_tensor(XC.ap(m), XA.ap(m, k, k + TT), par(P_CW + 8 * k + m), XC.ap(m), ALU.mult, ALU.add),
                           reads=XA.b(m) + XC.b(m) + [b_PAR], writes=XC.b(m))
            proj_fm("xa", hsrc, hbuf, cons_xa)
            P.emit("dve", lambda e: e.tensor_copy(HALO[:, :, :], XA.ap3(TT, TT + 3)), reads=XA.ball(), writes=[b_HALO])
            if after_xa is not None:
                after_xa()

            def cons_ga(m, ps, bps):
                P.emit("act", lambda e: e.activation(SG.ap(m), ps[:, :], AF.Silu), reads=[bps], writes=SG.b(m))
            proj_fm("ga", hsrc, hbuf, cons_ga)
            P.lab = "cast"
            for m in range(8):
                cast_xc(m)

            def cons_gma(m, ps, bps):
                P.emit("act", lambda e: e.activation(GA.ap(m), ps[:, :], AF.Sigmoid, bias=par(P_BM + m)),
                       reads=[bps, b_PAR], writes=GA.b(m))
            proj_fm("gma", hsrc, hbuf, cons_gma)

            P.lab = "bd"
            for m in range(8):
                ps, bps = next_ps()
                P.emit("pe", lambda e, ps=ps, m=m: e.matmul(ps[:, :], WX[:, m, :], XCb.ap(m), start=True, stop=True),
                       reads=[b_WX] + XCb.b(m), writes=[bps])
                P.emit("act", lambda e, ps=ps, m=m: e.activation(GI.ap(m), ps[:, :], AF.Sigmoid, bias=par(P_BX + m)),
                       reads=[bps, b_PAR], writes=GI.b(m))
                ps, bps = next_ps()
                P.emit("pe", lambda e, ps=ps, m=m: e.matmul(ps[:, :], WA[:, m, :], XCb.ap(m), start=True, stop=True),
                       reads=[b_WA] + XCb.b(m), writes=[bps])
                P.emit("act", lambda e, ps=ps, m=m: e.activation(AA.ap(m), ps[:, :], AF.Sigmoid, bias=par(P_BA + m)),
                       reads=[bps, b_PAR], writes=AA.b(m))

            P.lab = "A1"
            for m in range(8):
                P.emit("act", lambda e, m=m: e.activation(AA.ap(m), AA.ap(m), AF.Exp, scale=dp(m)),
                       reads=AA.b(m) + [b_DP], writes=AA.b(m))
            for m in range(8):
                P.emit("dve", lambda e, m=m: e.scalar_tensor_tensor(MM.ap(m), AA.ap(m), -1.0, AA.ap(m), ALU.mult, ALU.mult),
                       reads=AA.b(m), writes=MM.b(m))

            def cons_v(j, half, ps, bps):
                P.emit("dve", lambda e: e.tensor_copy(VV.ap(j, half * 512, half * 512 + 512), ps[:, :]),
                       reads=[bps], writes=VV.b(j, half * 512, half * 512 + 512))
            proj_tm("i", hsrc_tm, hbuf, cons_v)


            P.lab = "A2"
            for m in range(8):
                P.emit("act", lambda e, m=m: e.activation(MM.ap(m), MM.ap(m), AF.Ln, bias=1.0),
                       reads=MM.b(m), writes=MM.b(m))


            P.lab = "A3"
            for m in range(8):
                P.emit("act", lambda e, m=m: e.activation(MM.ap(m), MM.ap(m), AF.Exp, scale=0.5),
                       reads=MM.b(m), writes=MM.b(m))
            for m in range(8):
                P.emit("dve", lambda e, m=m: e.tensor_tensor(MM.ap(m), MM.ap(m), GI.ap(m), ALU.mult),
                       reads=MM.b(m) + GI.b(m), writes=MM.b(m))
                P.emit("dve", lambda e, m=m: e.tensor_tensor(XC.ap(m), MM.ap(m), XC.ap(m), ALU.mult),
                       reads=MM.b(m) + XC.b(m), writes=XC.b(m))
                P.emit("dve", lambda e, m=m: e.tensor_tensor_scan(MM.ap(m), AA.ap(m), XC.ap(m), HST[:, m:m + 1], ALU.mult, ALU.add),
                       reads=AA.b(m) + XC.b(m) + [b_HST], writes=MM.b(m))
                P.emit("dve", lambda e, m=m: e.tensor_tensor(YA.ap(m), MM.ap(m), SG.ap(m), ALU.mult),
                       reads=MM.b(m) + SG.b(m), writes=YA.b(m))
            P.emit("dve", lambda e: e.tensor_copy(HST[:, :], MM.ap3()[:, :, TT - 1]), reads=MM.ball(), writes=[b_HST])

            def cons_gb(m, ps, bps):
                P.emit("act", lambda e: e.activation(SGB.ap(m), ps[:, :], AF.Silu), reads=[bps], writes=SGB.b(m))
            proj_fm("gb", hsrc, hbuf, cons_gb)

            def cons_q(m, ps, bps):
                P.emit("act", lambda e: e.activation(SQ.ap(m), ps[:, :], AF.Silu), reads=[bps], writes=SQ.b(m))
            proj_fm("q", hsrc, hbuf, cons_q)

            def cons_f(m, ps, bps):
                P.emit("act", lambda e: e.activation(SS.ap(m), ps[:, :], AF.Sigmoid), reads=[bps], writes=SS.b(m))
                P.emit("act", lambda e: e.activation(SMb.ap(m), ps[:, :], AF.Sigmoid, scale=-1.0), reads=[bps], writes=SMb.b(m))
            proj_fm("f", hsrc, hbuf, cons_f)

            P.lab = "B1"
            for hd in range(8):
                P.emit("act", lambda e, hd=hd: e.activation(SS.ap(hd), SS.ap(hd), AF.Ln, bias=dp(8 + hd), scale=dp(16 + hd)),
                       reads=SS.b(hd) + [b_DP], writes=SS.b(hd))
            def bscan(hd):
                P.emit("dve", lambda e: e.tensor_tensor_scan(BB.ap(hd), ONES[:, 0:1].to_broadcast([128, TT]), SS.ap(hd), 0.0, ALU.mult, ALU.add),
                       reads=SS.b(hd) + [b_ONES], writes=BB.b(hd), name="B1")
            for hd in range(4):
                bscan(hd)
            DG3 = SM2[:, 0:32].rearrange("p (h n) -> p h n", h=8)
            bref = BB.ap3()[:, :, 63::128]

            def decay_book():
                P.emit("dve", lambda e: e.tensor_tensor(DG3[:, :, 1:4], bref[:, :, 1:4], bref[:, :, 0:3], ALU.subtract),
                       reads=BB.ball(), writes=[b_SM2], name="B2")
                P.emit("dve", lambda e: e.tensor_tensor(DG3[:, :, 0], bref[:, :, 0], REM[:, :], ALU.add),
                       reads=BB.ball() + [b_REM], writes=[b_SM2], name="B2")
                P.emit("dve", lambda e: e.tensor_tensor(REM[:, :], BB.ap3()[:, :, TT - 1], bref[:, :, 3], ALU.subtract),
                       reads=BB.ball(), writes=[b_REM], name="B2")
                P.emit("act", lambda e: e.activation(SM2[:, 32:64], SM2[:, 0:32], AF.Exp), reads=[b_SM2], writes=[b_SM2], name="B2")

            def bc(hd):
                bc_out = SS.ap(hd).rearrange("p (n t) -> p n t", n=4)
                b_in = BB.ap(hd).rearrange("p (n t) -> p n t", n=4)
                b_ref = BB.ap(hd)[:, 63::128].unsqueeze(2).to_broadcast([128, 4, 128])
                P.emit("dve", lambda e, o=bc_out, i0=b_in, i1=b_ref: e.tensor_tensor(o, i0, i1, ALU.subtract),
                       reads=BB.b(hd), writes=SS.b(hd), name="B2")

            def cons_pa(m, ps, bps):
                P.emit("dve", lambda e: e.tensor_tensor(OA.ap(m), ps[:, :], GA.ap(m), ALU.mult),
                       reads=[bps] + GA.b(m), writes=OA.b(m))
                if m < 2:
                    bscan(4 + 2 * m)
                    bscan(5 + 2 * m)
                    if m == 1:
                        decay_book()
                elif m < 6:
                    bc(2 * (m - 2))
                    bc(2 * (m - 2) + 1)
            proj_fm("pa", lambda k: YA.ap(k), lambda k: YA.b(k), cons_pa)

            def epen(m):
                if m % 4 != 0:
                    return
                P.lab = "B2"
                for hd in range(m, m + 4):
                    P.emit("act", lambda e, hd=hd: e.activation(EP.ap(hd), SS.ap(hd), AF.Exp, bias=dp(24 + hd)),
                           reads=SS.b(hd) + [b_DP], writes=EP.b(hd))
                    P.emit("act", lambda e, hd=hd: e.activation(EN.ap(hd), SS.ap(hd), AF.Exp, scale=-1.0),
                           reads=SS.b(hd), writes=EN.b(hd))
            def cons_gmb(m, ps, bps):
                P.emit("act", lambda e: e.activation(GB.ap(m), ps[:, :], AF.Sigmoid, bias=par(P_BM + 8 + m)),
                       reads=[bps, b_PAR], writes=GB.b(m))
            epen(0)
            epen(4)
            proj_fm("gmb", hsrc, hbuf, cons_gmb)

            P.lab = "B3"
            for hd in range(8):
                P.emit("dve", lambda e, hd=hd: e.tensor_tensor(SQ.ap(hd), SQ.ap(hd), EP.ap(hd), ALU.mult),
                       reads=SQ.b(hd) + EP.b(hd), writes=SQ.b(hd))
                P.emit("dve", lambda e, hd=hd: e.tensor_tensor(SMb.ap(hd), SMb.ap(hd), EN.ap(hd), ALU.mult),
                       reads=SMb.b(hd) + EN.b(hd), writes=SMb.b(hd))
            QT, KT = SQ, SMb

            P.lab = "KTtr"
            rot_lim[0] = 4
            for hd in range(8):
                pt, bpt = next_ps()
                ptb = pt[:, :].bitcast(BF16)
                for n in range(4):
                    P.emit("pe", lambda e, hd=hd, n=n, ptb=ptb: e.transpose(ptb[:, n * 128:(n + 1) * 128], KT.ap(hd, n * 128, (n + 1) * 128), IDB[:, :]),
                           reads=KT.b(hd) + [b_IDB], writes=[bpt])
                P.emit("act", lambda e, hd=hd, ptb=ptb: e.activation(KTT.ap(hd), ptb[:, 0:512], AF.Copy),
                       reads=[bpt], writes=KTT.b(hd))
            G3 = SM2[:, 32:64].rearrange("p (h n) -> p h n", h=8)

            def emit_sbf(n, half):
                hsl = slice(half * 4, half * 4 + 4)
                P.emit("dve", lambda e: e.tensor_tensor(SBF[:, hsl, :], TST[:, hsl, :], G3[:, hsl, n].unsqueeze(2).to_broadcast([128, 4, 128]), ALU.mult),
                       reads=b_TST[hsl] + [b_SM2], writes=b_SBF[hsl])
            att_rr = [0]
            pend_att = {}

            def att_group(n, half):
                P.lab = "heads%d" % n
                ps_a, bps_a = next_ps()
                for hh in range(4):
                    hd = half * 4 + hh
                    P.emit("pe", lambda e, hh=hh, hd=hd: e.matmul(ps_a[:, hh * 128:(hh + 1) * 128], KT.ap(hd, n * 128, (n + 1) * 128), QT.ap(hd, n * 128, (n + 1) * 128), start=True, stop=True),
                           reads=KT.b(hd) + QT.b(hd), writes=[bps_a])
                gi = att_rr[0] % 2
                att_rr[0] += 1
                P.emit("dve", lambda e: e.tensor_tensor(ATT4[gi][:, :, :], ps_a[:, :].rearrange("p (h t) -> p h t", h=4),
                                                        MSK[:, :].unsqueeze(1).to_broadcast([128, 4, 128]), ALU.mult),
                       reads=[bps_a, b_MSK], writes=[b_ATT4[gi]])
                pend_att[(n, half)] = gi

            def heads(n, half):
                P.lab = "heads%d" % n
                POn, b_POn = PO[n % 2], b_PO[n % 2]
                gcols = [SM2[:, 32 + hd * 4 + n:33 + hd * 4 + n] for hd in range(8)]
                if n == 0 and half == 0:
                    emit_sbf(0, 0)
                    emit_sbf(0, 1)
                    att_group(0, 0)
                nxt = (n, 1) if half == 0 else (n + 1, 0)
                if nxt[0] < 4:
                    att_group(*nxt)
                P.lab = "heads%d" % n
                gi = pend_att.pop((n, half))
                for hd in range(half * 4, half * 4 + 4):
                    hs = slice(hd * 128, (hd + 1) * 128)
                    P.emit("pe", lambda e, hd=hd, hs=hs: e.matmul(POn[:, hs], ATT4[gi][:, hd % 4, :], VV.ap(n, hs.start, hs.stop), start=True, stop=False),
                           reads=[b_ATT4[gi]] + VV.b(n), writes=[b_POn[hd // 4]])
                    P.emit("pe", lambda e, hd=hd, hs=hs: e.matmul(POn[:, hs], QT.ap(hd, n * 128, (n + 1) * 128), SBF[:, hd, :], start=False, stop=True),
                           reads=QT.b(hd) + [b_SBF[hd]], writes=[b_POn[hd // 4]])
                    ps_k, bps_k = next_ps()
                    P.emit("pe", lambda e, ps_k=ps_k, hd=hd, hs=hs: e.matmul(ps_k[:, 0:128], KTT.ap(hd, n * 128, (n + 1) * 128), VV.ap(n, hs.start, hs.stop), start=True, stop=True),
                           reads=KTT.b(hd) + VV.b(n), writes=[bps_k])
                    P.emit("dve", lambda e, ps_k=ps_k, hd=hd, gcol=gcols[hd]: e.scalar_tensor_tensor(TST[:, hd, :], TST[:, hd, :], gcol, ps_k[:, 0:128], ALU.mult, ALU.add),
                           reads=[b_TST[hd], b_SM2, bps_k], writes=[b_TST[hd]])
                smc = 64 + 24 * (n % 2)
                for h2 in range(half * 4, half * 4 + 4):
                    P.emit("act", lambda e, h2=h2: e.activation(SQJ[:, h2 * 128:(h2 + 1) * 128], POn[:, h2 * 128:(h2 + 1) * 128], AF.Square,
                                                                accum_out=SMALL[:, smc + h2:smc + h2 + 1]),
                           reads=[b_POn[half]], writes=[b_SQJ, b_SMhg2[n % 2]])
                if n + 1 < 4:
                    emit_sbf(n + 1, half)

            def stats(n):
                P.lab = "stats%d" % n
                POn, b_POn = PO[n % 2], b_PO[n % 2]
                smc = 64 + 24 * (n % 2)
                b_sm = b_SMhg2[n % 2]
                P.emit("act", lambda e: e.activation(SMALL[:, smc + 8:smc + 16], SMALL[:, smc:smc + 8], AF.Ln, bias=EPS, scale=1.0 / 128),
                       reads=[b_sm], writes=[b_sm])
                P.emit("act", lambda e: e.activation(SMALL[:, smc + 16:smc + 24], SMALL[:, smc + 8:smc + 16], AF.Exp, scale=-0.5),
                       reads=[b_sm], writes=[b_sm])
                P.emit("dve", lambda e: e.tensor_tensor(ONB[:, :].rearrange("p (h v) -> p h v", h=8), POn[:, :].rearrange("p (h v) -> p h v", h=8),
                                                        SMALL[:, smc + 16:smc + 24].unsqueeze(2).to_broadcast([128, 8, 128]), ALU.mult),
                       reads=b_POn + [b_sm], writes=[b_ONB])

            def trans(n):
                P.lab = "trans%d" % n
                cs = slice(n * 128, (n + 1) * 128)
                pt, bpt = next_ps()
                ptb = pt[:, :].bitcast(BF16)
                for hd in range(8):
                    P.emit("pe", lambda e, hd=hd, ptb=ptb: e.transpose(ptb[:, hd * 128:(hd + 1) * 128], ONB[:, hd * 128:(hd + 1) * 128], IDB[:, :]),
                           reads=[b_ONB, b_IDB], writes=[bpt])
                if n == 3:
                    P.emit("dve", lambda e, cs=cs, ptb=ptb: e.scalar_tensor_tensor(YB.ap3(cs.start, cs.stop), ptb[:, :].rearrange("p (h t) -> p h t", h=8), par(P_HG),
                                                                                  SGB.ap3(cs.start, cs.stop), ALU.mult, ALU.mult),
                           reads=[bpt, b_PAR] + SGB.ball(), writes=YB.ball())
                    return
                P.emit("act", lambda e, cs=cs, ptb=ptb: e.activation(YB.ap3(cs.start, cs.stop), ptb[:, :].rearrange("p (h t) -> p h t", h=8), AF.Copy, scale=par(P_HG)),
                       reads=[bpt, b_PAR], writes=YB.ball())
                P.emit("pool", lambda e, cs=cs: e.tensor_tensor(YB.ap3(cs.start, cs.stop), YB.ap3(cs.start, cs.stop), SGB.ap3(cs.start, cs.stop), ALU.mult),
                       reads=YB.ball() + SGB.ball(), writes=YB.ball())

            heads(0, 0)
            heads(0, 1)
            for n in range(1, 4):
                heads(n, 0)
                stats(n - 1)
                heads(n, 1)
                trans(n - 1)
            stats(3)
            trans(3)
            rot_lim[0] = 8

        def merge_mm(tile):
            XTt, b_XTt = XT[tile % 2], b_XT[tile % 2]
            def cons_pb(m, ps, bps):
                P.emit("dve", lambda e: e.tensor_tensor(MIX.ap(m), ps[:, :], GB.ap(m), ALU.mult),
                       reads=[bps] + GB.b(m), writes=MIX.b(m))
                P.emit("dve", lambda e: e.tensor_tensor(MIX.ap(m), MIX.ap(m), OA.ap(m), ALU.add),
                       reads=MIX.b(m) + OA.b(m), writes=MIX.b(m))
            proj_fm("pb", lambda k: YB.ap(k), lambda k: YB.b(k), cons_pb)

        def merge_wo(tile):
            XTt, b_XTt = XT[tile % 2], b_XT[tile % 2]

            def cons_wo(j, half, ps, bps):
                sl = slice(half * 512, half * 512 + 512)
                P.emit("dve", lambda e: e.tensor_tensor(XTt[:, j, sl], ps[:, :], XTt[:, j, sl], ALU.add),
                       reads=[bps, b_XTt[j]], writes=[b_XTt[j]])
            proj_tm("wo", lambda k, j: MIX.ap(k, j * 128, (j + 1) * 128), lambda k: MIX.b(k), cons_wo)

        s_fin = [sem("st_f%d" % j) for j in range(4)]
        b_SMfj = [Buf("SMfin%d" % j) for j in range(4)]

        def final_split(tile):
            P.lab = "final"
            XTt, b_XTt = XT[tile % 2], b_XT[tile % 2]
            for j in range(4):
                P.emit("act", lambda e, j=j: e.activation(SQJ[:, :], XTt[:, j, :], AF.Square, accum_out=SMALL[:, 40 + j:41 + j]),
                       reads=[b_XTt[j]], writes=[b_SQJ, b_SMfj[j]])
                P.emit("act", lambda e, j=j: e.activation(SMALL[:, 44 + j:45 + j], SMALL[:, 40 + j:41 + j], AF.Ln, bias=EPS, scale=1.0 / D),
                       reads=[b_SMfj[j]], writes=[b_SMfj[j]])
                P.emit("act", lambda e, j=j: e.activation(SMALL[:, 48 + j:49 + j], SMALL[:, 44 + j:45 + j], AF.Exp, scale=-0.5),
                       reads=[b_SMfj[j]], writes=[b_SMfj[j]])
                P.emit("dve", lambda e, j=j: e.scalar_tensor_tensor(XTt[:, j, :], XTt[:, j, :], SMALL[:, 48 + j:49 + j], FNG[:, :], ALU.mult, ALU.mult),
                       reads=[b_XTt[j], b_SMfj[j], b_FNG], writes=[b_XTt[j]])
                op = P.emit("sp", lambda e, tile=tile, j=j: e.dma_start(out=yv[tile][:, j, :], in_=XTt[:, j, :]), reads=[b_XTt[j]], dma_sem=s_fin[j])
                out_ops.append(op)

        def final(tile):
            P.lab = "final"
            XTt, b_XTt = XT[tile % 2], b_XT[tile % 2]
            for j in range(4):
                P.emit("act", lambda e, j=j: e.activation(SQJ[:, :], XTt[:, j, :], AF.Square, accum_out=SMALL[:, 40 + j:41 + j]),
                       reads=[b_XTt[j]], writes=[b_SQJ, b_SMfin])
            P.emit("act", lambda e: e.activation(SMALL[:, 44:48], SMALL[:, 40:44], AF.Ln, bias=EPS, scale=1.0 / D),
                   reads=[b_SMfin], writes=[b_SMfin])
            P.emit("act", lambda e: e.activation(SMALL[:, 48:52], SMALL[:, 44:48], AF.Exp, scale=-0.5),
                   reads=[b_SMfin], writes=[b_SMfin])
            for j in range(4):
                P.emit("dve", lambda e, j=j: e.scalar_tensor_tensor(XTt[:, j, :], XTt[:, j, :], SMALL[:, 48 + j:49 + j], FNG[:, :], ALU.mult, ALU.mult),
                       reads=[b_XTt[j], b_SMfin, b_FNG], writes=[b_XTt[j]])
            op = P.emit("sp", lambda e, tile=tile: e.dma_start(out=yv[tile], in_=XTt[:, :, :]), reads=b_XTt, dma_sem=s_st[tile % 2])
            out_ops.append(op)


        ntiles = NB * NT
        s0_stats(0)
        s0_tr(0)
        for tile in range(ntiles):
            it = tile % NT
            if tile == 0:
                s0_load(1)
                P.emit("sp", lambda e: e.dma_start(out=FNG[:, :], in_=dram["fng"]), writes=[b_FNG], dma_sem=s_c[1])
                front(tile, it)
            else:
                def after_xa(tile=tile):
                    final(tile - 1)
                    if tile + 1 < ntiles:
                        s0_load(tile + 1)
                front(tile, it, after_xa)
            merge_mm(tile)
            if tile + 1 < ntiles:
                s0_stats(tile + 1)
            merge_wo(tile)
            if tile + 1 < ntiles:
                s0_tr(tile + 1)
        final_split(ntiles - 1)

        block = es.enter_context(nc.Block())
        P.finalize(nc, block, engine_sems, out_ops)
        global LAST_PROG
        LAST_PROG = P


def _consts():
    c = np.zeros((128, 256), np.float32)
    c[:, 0:128] = np.eye(128, dtype=np.float32)
    s = np.arange(128)[:, None]
    t = np.arange(128)[None, :]
    c[:, 128:256] = (s <= t).astype(np.float32)
    return c


def _pack_params(inp):
    def fm(v, n):
        return np.ascontiguousarray(np.asarray(v, np.float32).reshape(n, 128).T)
    p = np.zeros((128, NPAR), np.float32)
    p[:, P_NG:P_NG + 8] = fm(inp["norm_g"][0], 8)
    for k in range(4):
        p[:, P_CW + 8 * k:P_CW + 8 * k + 8] = fm(inp["conv_w"][0, k], 8)
    p[:, P_CB:P_CB + 8] = fm(inp["conv_b"][0], 8)
    p[:, P_BX:P_BX + 8] = fm(inp["rg_bx"][0].reshape(-1), 8)
    p[:, P_BA:P_BA + 8] = fm(inp["rg_ba"][0].reshape(-1), 8)
    p[:, P_LAM:P_LAM + 8] = fm(inp["rg_lambda"][0], 8)
    p[:, P_L0:P_L0 + 8] = fm(inp["hg_lb_logits"][0], 8)
    p[:, P_L1:P_L1 + 8] = fm(inp["hg_lb_logits"][1], 8)
    p[:, P_HG:P_HG + 1] = fm(inp["hg_norm_g"][0], 1)
    p[:, P_BM:P_BM + 16] = fm(inp["b_merge"][0], 16)
    return p


def _pack_weights(inp):
    w_in = np.asarray(inp["w_in"], np.float32)[0]
    extra = {"pa": np.asarray(inp["proj_a"], np.float32)[0], "pb": np.asarray(inp["proj_b"], np.float32)[0],
             "wo": np.asarray(inp["w_out"], np.float32)[0]}
    out = np.empty((2 * len(WORDER), 128, 8 * 512), np.float32)
    for i, name in enumerate(WORDER):
        W = extra[name] if name in extra else w_in[:, W_IN_COLS[name]:W_IN_COLS[name] + 1024]
        Wr = W.reshape(8, 128, 2, 512)
        out[2 * i:2 * i + 2] = Wr.transpose(2, 1, 0, 3).reshape(2, 128, 8 * 512)
    return out


def make_in_maps(inp, n_cores=8):
    x = np.asarray(inp["x"], np.float32)
    shared = {
        "wpk": _pack_weights(inp),
        "rg_wx": np.ascontiguousarray(np.asarray(inp["rg_wx"], np.float32)[0].transpose(1, 0, 2)),
        "rg_wa": np.ascontiguousarray(np.asarray(inp["rg_wa"], np.float32)[0].transpose(1, 0, 2)),
        "params": _pack_params(inp),
        "fng": np.ascontiguousarray(np.broadcast_to(np.asarray(inp["final_norm_g"], np.float32)[None, :], (128, D))),
        "consts": _consts(),
    }
    maps = []
    for c in range(n_cores):
        m = dict(shared)
        m["x"] = np.ascontiguousarray(x[NB * c:NB * (c + 1)].reshape(NB * SEQ, D))
        maps.append(m)
    return maps


def kernel(**inputs):
    nc = build_nc()
    in_maps = make_in_maps(inputs)
    res = run_bass_kernel_spmd(nc, in_maps, core_ids=list(range(8)))
    out = np.concatenate([np.asarray(r["y"]).reshape(NB, SEQ, D) for r in res.results], axis=0)
    return out.astype(np.float32)
```
